# Optimizing a Trainium2 kernel written in Bass

```python
import math
import jax, jax.numpy as jnp
from jax import lax
import numpy as np

D_MODEL = 2048
BATCH = 8
SEQ = 4096
DEPTH = 4

N_MIXERS = 2
N_MLA = (DEPTH + 1) // 2
N_DIFF = DEPTH // 2
Q_BLOCK = 128
PLE_DIM = 256

MLA_HEAD_DIM_NOPE = 128
MLA_HEAD_DIM_ROPE = 64
MLA_HEAD_DIM_V = 128
MLA_HEADS = D_MODEL // 128
MLA_Q_RANK = D_MODEL // 4
MLA_KV_RANK = D_MODEL // 4
ROPE_THETA = 10000.0

DIFF_HEAD_DIM = 128
DIFF_HEADS = D_MODEL // (2 * DIFF_HEAD_DIM)
DIFF_QK = 2 * DIFF_HEADS * DIFF_HEAD_DIM
DIFF_V = DIFF_HEADS * 2 * DIFF_HEAD_DIM

REL_BUCKETS = 32
REL_MAX_DIST = 128

D_FF = -(-8 * D_MODEL // (3 * 256)) * 256

ALPHA = (2 * DEPTH) ** 0.25
BETA = (8 * DEPTH) ** -0.25

LN_EPS = 1e-5
RMS_EPS = 1e-6

kernel_name = "hybrid_mla_diffattn_deepnorm_encoder"


def layer_norm(x, g, b):
    xf = x.astype(jnp.float32)
    mu = jnp.mean(xf, -1, keepdims=True)
    var = jnp.mean(jnp.square(xf - mu), -1, keepdims=True)
    return ((xf - mu) * lax.rsqrt(var + LN_EPS) * g.astype(jnp.float32)
            + b.astype(jnp.float32)).astype(x.dtype)


def rms_norm(x, g):
    xf = x.astype(jnp.float32)
    return (xf * lax.rsqrt(jnp.mean(xf * xf, -1, keepdims=True) + RMS_EPS)
            * g.astype(jnp.float32)).astype(x.dtype)


def rope_tables(seq, dtype):
    pos = jnp.arange(seq, dtype=jnp.float32)
    inv = 1.0 / (ROPE_THETA ** (jnp.arange(0, MLA_HEAD_DIM_ROPE, 2, dtype=jnp.float32) / MLA_HEAD_DIM_ROPE))
    ang = pos[:, None] * inv[None, :]
    return jnp.cos(ang).astype(dtype), jnp.sin(ang).astype(dtype)


def apply_rope(x, cos, sin):
    x1, x2 = jnp.split(x, 2, axis=-1)
    return jnp.concatenate([x1 * cos - x2 * sin, x2 * cos + x1 * sin], axis=-1)


def to_blocks(t):
    b, s = t.shape[:2]
    return jnp.moveaxis(t.reshape((b, s // Q_BLOCK, Q_BLOCK) + t.shape[2:]), 1, 0)


def from_blocks(t):
    t = jnp.moveaxis(t, 0, 1)
    return t.reshape((t.shape[0], t.shape[1] * t.shape[2]) + t.shape[3:])


def t5_bucket(rel):
    nb = REL_BUCKETS // 2
    max_exact = nb // 2
    ret = (rel > 0).astype(jnp.int32) * nb
    n = jnp.abs(rel)
    nf = jnp.maximum(n, 1).astype(jnp.float32)
    large = max_exact + (jnp.log(nf / max_exact) / math.log(REL_MAX_DIST / max_exact)
                         * (nb - max_exact)).astype(jnp.int32)
    large = jnp.minimum(large, nb - 1)
    return ret + jnp.where(n < max_exact, n, large)


def mla_mixer(x, w_in, q_norm, kv_norm, w_uq, w_ukv, w_o, cos, sin):
    b, s, _ = x.shape
    h = x @ w_in
    c_q, c_kv, k_rope = jnp.split(h, [MLA_Q_RANK, MLA_Q_RANK + MLA_KV_RANK], axis=-1)
    q = (rms_norm(c_q, q_norm) @ w_uq).reshape(b, s, MLA_HEADS, MLA_HEAD_DIM_NOPE + MLA_HEAD_DIM_ROPE)
    q_nope, q_rope = q[..., :MLA_HEAD_DIM_NOPE], q[..., MLA_HEAD_DIM_NOPE:]
    q_rope = apply_rope(q_rope, cos[None, :, None], sin[None, :, None])
    kv = (rms_norm(c_kv, kv_norm) @ w_ukv).reshape(b, s, MLA_HEADS, MLA_HEAD_DIM_NOPE + MLA_HEAD_DIM_V)
    k_nope, v = kv[..., :MLA_HEAD_DIM_NOPE], kv[..., MLA_HEAD_DIM_NOPE:]
    k_rope = apply_rope(k_rope, cos[None], sin[None])
    scale = (MLA_HEAD_DIM_NOPE + MLA_HEAD_DIM_ROPE) ** -0.5

    def attend(blk):
        qn, qr = blk
        logits = (jnp.einsum('bqhd,bkhd->bhqk', qn, k_nope)
                  + jnp.einsum('bqhr,bkr->bhqk', qr, k_rope)).astype(jnp.float32) * scale
        probs = jax.nn.softmax(logits, axis=-1).astype(v.dtype)
        return jnp.einsum('bhqk,bkhd->bqhd', probs, v)

    o = from_blocks(lax.map(attend, (to_blocks(q_nope), to_blocks(q_rope))))
    return o.reshape(b, s, MLA_HEADS * MLA_HEAD_DIM_V) @ w_o


def diff_mixer(x, w_in, lam, sub_norm, w_o, rel_bias, layer_idx):
    b, s, _ = x.shape
    h = x @ w_in
    q = h[..., :DIFF_QK].reshape(b, s, 2 * DIFF_HEADS, DIFF_HEAD_DIM)
    k = h[..., DIFF_QK:2 * DIFF_QK].reshape(b, s, 2 * DIFF_HEADS, DIFF_HEAD_DIM)
    v = h[..., 2 * DIFF_QK:].reshape(b, s, DIFF_HEADS, 2 * DIFF_HEAD_DIM)
    lambda_init = 0.8 - 0.6 * math.exp(-0.3 * layer_idx)
    lf = lam.astype(jnp.float32)
    lam_full = jnp.exp(jnp.sum(lf[0] * lf[1])) - jnp.exp(jnp.sum(lf[2] * lf[3])) + lambda_init
    scale = DIFF_HEAD_DIM ** -0.5
    key_pos = jnp.arange(s, dtype=jnp.int32)
    starts = jnp.arange(s // Q_BLOCK, dtype=jnp.int32) * Q_BLOCK
    table = rel_bias.astype(jnp.float32)

    def attend(blk):
        qb, start = blk
        logits = jnp.einsum('bqgd,bkgd->bgqk', qb, k).astype(jnp.float32) * scale
        logits = logits.reshape(b, DIFF_HEADS, 2, Q_BLOCK, s)
        rel = key_pos[None, :] - (start + jnp.arange(Q_BLOCK, dtype=jnp.int32))[:, None]
        bias = jnp.moveaxis(table[t5_bucket(rel)], -1, 0)
        probs = jax.nn.softmax(logits + bias[None, :, None], axis=-1)
        diff = (probs[:, :, 0] - lam_full * probs[:, :, 1]).astype(v.dtype)
        return jnp.einsum('bhqk,bkhe->bqhe', diff, v)

    o = from_blocks(lax.map(attend, (to_blocks(q), starts)))
    o = rms_norm(o, sub_norm) * (1.0 - lambda_init)
    return o.reshape(b, s, DIFF_V) @ w_o


def swiglu(x, w_in, w_out):
    g, u = jnp.split(x @ w_in, 2, axis=-1)
    return (jax.nn.silu(g) * u) @ w_out


def setup_inputs(seed: int = 0) -> dict:
    key = jax.random.key(seed)
    ks = jax.random.split(key, 21)
    f32 = jnp.float32

    def nrm(k, shape, scale):
        return jax.random.normal(k, shape, f32) * scale

    mla_in_w = MLA_Q_RANK + MLA_KV_RANK + MLA_HEAD_DIM_ROPE
    return {
        "x": nrm(ks[0], (BATCH, SEQ, D_MODEL), 1.0),
        "p": nrm(ks[1], (DEPTH, BATCH, SEQ, PLE_DIM), 1.0),
        "mla_w_in": nrm(ks[2], (N_MLA, D_MODEL, mla_in_w), D_MODEL ** -0.5),
        "mla_q_norm": 1.0 + nrm(ks[3], (N_MLA, MLA_Q_RANK), 0.02),
        "mla_kv_norm": 1.0 + nrm(ks[4], (N_MLA, MLA_KV_RANK), 0.02),
        "mla_w_uq": nrm(ks[5], (N_MLA, MLA_Q_RANK, MLA_HEADS * (MLA_HEAD_DIM_NOPE + MLA_HEAD_DIM_ROPE)), MLA_Q_RANK ** -0.5),
        "mla_w_ukv": nrm(ks[6], (N_MLA, MLA_KV_RANK, MLA_HEADS * (MLA_HEAD_DIM_NOPE + MLA_HEAD_DIM_V)), MLA_KV_RANK ** -0.5),
        "mla_w_o": nrm(ks[7], (N_MLA, MLA_HEADS * MLA_HEAD_DIM_V, D_MODEL), BETA * (MLA_HEADS * MLA_HEAD_DIM_V) ** -0.5),
        "diff_w_in": nrm(ks[8], (N_DIFF, D_MODEL, 2 * DIFF_QK + DIFF_V), D_MODEL ** -0.5),
        "diff_lambda": nrm(ks[9], (N_DIFF, 4, DIFF_HEAD_DIM), 0.1),
        "diff_sub_norm": 1.0 + nrm(ks[10], (N_DIFF, 2 * DIFF_HEAD_DIM), 0.02),
        "diff_w_o": nrm(ks[11], (N_DIFF, DIFF_V, D_MODEL), BETA * DIFF_V ** -0.5),
        "rel_bias": nrm(ks[12], (REL_BUCKETS, DIFF_HEADS), 0.5),
        "ln_g": 1.0 + nrm(ks[13], (DEPTH, 2, D_MODEL), 0.02),
        "ln_b": nrm(ks[14], (DEPTH, 2, D_MODEL), 0.02),
        "ffn_w_in": nrm(ks[15], (DEPTH, D_MODEL, 2 * D_FF), D_MODEL ** -0.5),
        "ffn_w_out": nrm(ks[16], (DEPTH, D_FF, D_MODEL), BETA * D_FF ** -0.5),
        "ple_w_gate": nrm(ks[17], (DEPTH, D_MODEL, D_MODEL), D_MODEL ** -0.5),
        "ple_w_proj": nrm(ks[18], (DEPTH, PLE_DIM, D_MODEL), PLE_DIM ** -0.5),
    }


def reference(x, p, mla_w_in, mla_q_norm, mla_kv_norm, mla_w_uq, mla_w_ukv, mla_w_o,
              diff_w_in, diff_lambda, diff_sub_norm, diff_w_o, rel_bias,
              ln_g, ln_b, ffn_w_in, ffn_w_out, ple_w_gate, ple_w_proj):
    cos, sin = rope_tables(x.shape[1], x.dtype)
    for i in range(DEPTH):
        j = i // N_MIXERS
        if i % N_MIXERS == 0:
            mix = mla_mixer(x, mla_w_in[j], mla_q_norm[j], mla_kv_norm[j],
                            mla_w_uq[j], mla_w_ukv[j], mla_w_o[j], cos, sin)
        else:
            mix = diff_mixer(x, diff_w_in[j], diff_lambda[j], diff_sub_norm[j],
                             diff_w_o[j], rel_bias, i)
        x = layer_norm(ALPHA * x + mix, ln_g[i, 0], ln_b[i, 0])
        x = layer_norm(ALPHA * x + swiglu(x, ffn_w_in[i], ffn_w_out[i]), ln_g[i, 1], ln_b[i, 1])
        x = x + jax.nn.sigmoid(x @ ple_w_gate[i]) * (p[i] @ ple_w_proj[i])
    return x
```

```python
import math
from contextlib import ExitStack
import numpy as np
import concourse.bass as bass
import concourse.mybir as mybir
from concourse.bass_utils import run_bass_kernel_spmd

F32 = mybir.dt.float32
BF16 = mybir.dt.bfloat16
AF = mybir.ActivationFunctionType
ALU = mybir.AluOpType

D = 2048
S = 4096
NTB = 8
TB = 512
DEPTH = 4
DFF = 5632
ALPHA = (2 * DEPTH) ** 0.25
LN_EPS = 1e-5
RMS_EPS = 1e-6
NCORES = 8
RL = 1280


class Buf:
    __slots__ = ("name", "w", "r", "parent", "kids", "wd")

    def __init__(self, name, parent=None):
        self.name = name
        self.w = None
        self.wd = {}
        self.r = {}
        self.parent = parent
        self.kids = []
        if parent is not None:
            parent.kids.append(self)


class Op:
    __slots__ = ("eng", "fn", "waits", "observed", "idx", "val", "dma_sem")

    def __init__(self, eng, fn):
        self.eng = eng
        self.fn = fn
        self.waits = []
        self.observed = False
        self.idx = 0
        self.val = 0
        self.dma_sem = None


ENGS = ["pe", "act", "dve", "pool", "sp"]


class Sched:
    def __init__(self, nc, stack, rings=None):
        rings = rings or {"sp": 28, "pool": 20}
        self.nc = nc
        self.ops = {e: [] for e in ENGS}
        self.esem = {e: stack.enter_context(nc.semaphore("es_" + e)) for e in ENGS}
        self.rings = {q: [stack.enter_context(nc.semaphore("rg_%s%d" % (q, i))) for i in range(n)]
                      for q, n in rings.items()}
        self.ring_use = {q: [0] * n for q, n in rings.items()}
        self.ring_pos = {q: 0 for q in rings}
        self.waited = {e: {} for e in ENGS}

    @staticmethod
    def _key(ev):
        return ev[1]

    @staticmethod
    def _ord(ev):
        return ev[2].idx + 1 if ev[0] == "e" else ev[2]

    def _collect(self, reads, writes):
        evs = []

        def wr(b):
            if b.w:
                evs.append(b.w)
            evs.extend(b.wd.values())
        for b in reads:
            wr(b)
            if b.parent is not None:
                wr(b.parent)
            for k in b.kids:
                wr(k)
        for b in writes:
            wr(b)
            evs.extend(b.r.values())
            if b.parent is not None:
                p = b.parent
                wr(p)
                evs.extend(p.r.values())
            for k in b.kids:
                wr(k)
                evs.extend(k.r.values())
        return evs

    def _waits(self, eng, evs):
        wd = self.waited[eng]
        best = {}
        for ev in evs:
            k = self._key(ev)
            if ev[0] == "e" and k == eng and eng in ("pe", "sp"):
                continue
            o = self._ord(ev)
            if wd.get(k, 0) >= o:
                continue
            if k not in best or self._ord(best[k]) < o:
                best[k] = ev
        out = []
        for k, ev in best.items():
            wd[k] = self._ord(ev)
            if ev[0] == "e":
                ev[2].observed = True
            out.append(ev)
        return out

    def _record(self, ev, reads, writes):
        k = self._key(ev)
        for b in reads:
            b.r[k] = ev
        for b in writes:
            if ev[0] == "d":
                b.wd[k] = ev
            else:
                b.w = ev
                b.wd = {}
            b.r = {}
            for kd in b.kids:
                kd.w = None
                kd.wd = {}
                kd.r = {}

    def op(self, eng, fn, reads=(), writes=()):
        o = Op(eng, fn)
        o.waits = self._waits(eng, self._collect(reads, writes))
        o.idx = len(self.ops[eng])
        self.ops[eng].append(o)
        self._record(("e", eng, o), reads, writes)
        return o

    def dma(self, q, out_ap, in_ap, reads=(), writes=()):
        n = len(self.rings[q])
        slot = self.ring_pos[q]
        self.ring_pos[q] = (slot + 1) % n
        prev = 16 * self.ring_use[q][slot]
        self.ring_use[q][slot] += 1
        evs = self._collect(reads, writes)
        if prev:
            evs.append(("d", (q, slot), prev))
        o = Op(q, lambda e: e.dma_start(out=out_ap, in_=in_ap))
        o.waits = self._waits(q, evs)
        o.idx = len(self.ops[q])
        o.dma_sem = self.rings[q][slot]
        self.ops[q].append(o)
        self._record(("d", (q, slot), prev + 16), reads, writes)
        return o

    def finalize(self):
        for e in ENGS:
            c = 0
            for o in self.ops[e]:
                if o.observed and o.dma_sem is None:
                    c += 1
                    o.val = c

    def emit(self, engobj, eng):
        for o in self.ops[eng]:
            for ev in o.waits:
                if ev[0] == "e":
                    engobj.wait_ge(self.esem[ev[1]], ev[2].val)
                else:
                    q, slot = ev[1]
                    engobj.wait_ge(self.rings[q][slot], ev[2])
            ins = o.fn(engobj)
            if o.dma_sem is not None:
                ins.then_inc(o.dma_sem, 16)
            elif o.observed:
                ins.then_inc(self.esem[eng], 1)
        if eng == "sp":
            for q, sems in self.rings.items():
                for slot, sem in enumerate(sems):
                    u = self.ring_use[q][slot]
                    if u:
                        engobj.wait_ge(sem, 16 * u)


def _t5_bucket_np(rel):
    nb = 16
    max_exact = 8
    ret = (rel > 0).astype(np.int32) * nb
    n = np.abs(rel)
    nf = np.maximum(n, 1).astype(np.float32)
    large = max_exact + (np.log(nf / np.float32(max_exact)) / np.float32(math.log(128 / max_exact))
                         * np.float32(nb - max_exact)).astype(np.int32)
    large = np.minimum(large, nb - 1)
    return ret + np.where(n < max_exact, n, large)


def _const_tables():
    pos = np.arange(S, dtype=np.float32)
    inv = (1.0 / (np.float32(10000.0) ** (np.arange(0, 64, 2, dtype=np.float32) / np.float32(64)))).astype(np.float32)
    ang = pos[:, None] * inv[None, :]
    cos = np.cos(ang).astype(np.float32).T
    sin = np.sin(ang).astype(np.float32).T
    cos2 = np.concatenate([cos, cos], 0)
    sins = np.concatenate([-sin, sin], 0)
    rope = np.stack([cos2, sins], 1)
    rope = rope.reshape(64, 2, NTB, TB).transpose(2, 0, 1, 3)
    i = np.arange(RL)
    rel = (RL - 1 - i) - 640
    bk = _t5_bucket_np(rel)
    onehot = np.zeros((32, RL), np.float32)
    onehot[bk, i] = 1.0
    return np.ascontiguousarray(rope), onehot


class KB:
    def __init__(self, debug=None, nlayers=DEPTH):
        self.debug = debug or ()
        self.nlayers = nlayers
        self.nc = bass.Bass("TRN2", target_bir_lowering=False)
        self.stack = ExitStack()
        self.dbufs = {}
        self.dt = {}
        self.stop_after = None

    def din(self, name, shape):
        t = self.nc.dram_tensor(name, list(shape), F32, kind="ExternalInput")
        self.dt[name] = t.ap()
        return self.dt[name]

    def dscr(self, name, shape, dtype, out=False):
        kind = "ExternalOutput" if (out or name in self.debug) else "Internal"
        t = self.nc.dram_tensor(name, list(shape), dtype, kind=kind)
        self.dt[name] = t.ap()
        return self.dt[name]

    def db(self, name, i=0):
        k = (name, i)
        if k not in self.dbufs:
            self.dbufs[k] = Buf("%s_%d" % (name, i))
        return self.dbufs[k]

    def sb(self, name, shape, dtype):
        return self.stack.enter_context(self.nc.sbuf_tensor(name, list(shape), dtype))

    def build(self):
        nc = self.nc
        st = self.stack
        L = self.nlayers
        x_in = self.din("x", [NTB, 128, 16, TB])
        p_in = self.din("p", [DEPTH, NTB, 128, 2, TB])
        mla_win = self.din("mla_win", [2, 128, 16, 1152])
        mla_wuq = self.din("mla_wuq", [2, 128, 4, 4096])
        mla_wuk = self.din("mla_wuk", [2, 128, 4, 2048])
        mla_wuv = self.din("mla_wuv", [2, 128, 4, 2048])
        mla_wo = self.din("mla_wo", [2, 2, 128, 16, 1024])
        diff_wqk = self.din("diff_wqk", [2, 4, 128, 16, 1024])
        diff_wv = self.din("diff_wv", [2, 2, 128, 16, 1024])
        diff_wo = self.din("diff_wo", [2, 2, 128, 16, 1024])
        ffn_win = self.din("ffn_win", [DEPTH, 11, 128, 16, 1024])
        ffn_wout = self.din("ffn_wout", [DEPTH, 4, 128, 44, 512])
        ple_wg = self.din("ple_wg", [DEPTH, 2, 128, 16, 1024])
        ple_wp = self.din("ple_wp", [DEPTH, 2, 128, 2, 1024])
        vecs_in = self.din("vecs", [128, 284])
        relb_in = self.din("relb", [32, 8])
        rope_in = self.din("rope", [NTB, 64, 2, TB])
        onehot_in = self.din("onehot", [32, RL])
        ident_in = self.din("ident", [128, 128])

        out_d = self.dscr("out", [NTB, 128, 16, TB], F32, out=True)
        X0F = self.dscr("X0F", [NTB, 128, 16, TB], F32)
        X1F = self.dscr("X1F", [NTB, 128, 16, TB], F32)
        X2F = self.dscr("X2F", [NTB, 128, 16, TB], F32)
        X0B = self.dscr("X0B", [NTB, 128, 16, TB], BF16)
        X1B = self.dscr("X1B", [NTB, 128, 16, TB], BF16)
        X2B = self.dscr("X2B", [NTB, 128, 16, TB], BF16)
        Z = self.dscr("Z", [NTB, 128, 16, TB], F32)
        OB = self.dscr("OB", [NTB, 128, 16, TB], BF16)
        H = self.dscr("H", [NTB, 128, 44, TB], BF16)
        CQN = self.dscr("CQN", [NTB, 128, 4, TB], BF16)
        CKVN = self.dscr("CKVN", [NTB, 128, 4, TB], BF16)
        KR = self.dscr("KR", [NTB, 64, TB], BF16)
        QN = self.dscr("QN", [NTB, 128, 16, TB], BF16)
        QR = self.dscr("QR", [NTB, 64, 16, TB], BF16)
        KN = self.dscr("KN", [NTB, 128, 16, TB], BF16)
        V = self.dscr("V", [32, 128, 2048], BF16)
        RD = self.dscr("RD", [8, RL], F32)
        ZT = self.dscr("ZT", [8, 128, RL], F32)

        Wt = [self.sb("W%d" % i, [128, 18432], BF16) for i in range(2)]
        At = [self.sb("A%d" % i, [128, 8192], BF16) for i in range(2)]
        Ft = [self.sb("F%d" % i, [128, 4096], F32) for i in range(2)]
        SBt = [self.sb("SB%d" % i, [128, 2048], BF16) for i in range(4)]
        SFt = [self.sb("SF%d" % i, [128, 2048], F32) for i in range(3)]
        Tt = [self.sb("T%d" % i, [128, 512], F32) for i in range(6)]
        PTt = [self.sb("PT%d" % i, [128, 512], BF16) for i in range(4)]
        STt = [self.sb("ST%d" % i, [128, 128], F32) for i in range(16)]
        RTt = [self.sb("RT%d" % i, [64, 1024], F32) for i in range(1)]
        ones_t = self.sb("ones", [128, 128], BF16)
        ones32_t = self.sb("ones32", [128, 128], F32)
        ident_t = self.sb("ident_sb", [128, 128], BF16)
        vecs_t = self.sb("vecs_sb", [128, 284], F32)
        small_t = self.sb("small", [128, 64], F32)
        relb_t = self.sb("relb_sb", [32, 8], F32)
        PSt = [st.enter_context(nc.psum_tensor("ps%d" % i, [128, 512], F32)) for i in range(8)]

        sch = Sched(nc, st)
        self.sch = sch
        Wb = [Buf("W0"), Buf("W1")]
        Ab = [Buf("A0"), Buf("A1")]
        Fb = [Buf("F0"), Buf("F1")]
        SBb = [Buf("SB%d" % i) for i in range(4)]
        SFb = [Buf("SF%d" % i) for i in range(3)]
        Tb_ = [Buf("T%d" % i) for i in range(6)]
        PTb = [Buf("PT%d" % i) for i in range(4)]
        STb = [Buf("ST%d" % i) for i in range(16)]
        RTb = [Buf("RT%d" % i) for i in range(1)]
        PSb = [Buf("PS%d" % i) for i in range(8)]
        cb = Buf("consts")
        smallb = Buf("small")
        W = [t[:, :] for t in Wt]
        A = [t[:, :] for t in At]
        Fa = [t[:, :] for t in Ft]
        SBa = [t[:, :] for t in SBt]
        SFa = [t[:, :] for t in SFt]
        Ta = [t[:, :] for t in Tt]
        PTa = [t[:, :] for t in PTt]
        STa = [t[:, :] for t in STt]
        RTa = [t[:, :] for t in RTt]
        PS = [t[:, :] for t in PSt]
        ones = ones_t[:, :]
        ones32 = ones32_t[:, :]
        vecs = vecs_t[:, :]
        small = small_t[:, :]

        rr = {"fqa": 0, "fqb": 0, "sb": 0, "sf": 0, "t": 0, "pt": 0, "st": 0, "w": 0, "a": 0, "f": 0, "rt": 0}

        def nxt(kind, n):
            i = rr[kind]
            rr[kind] = (i + 1) % n
            return i

        def v3(ap, off, c, n):
            return ap[:, off:off + c * n].rearrange("p (c n) -> p c n", n=n)

        def ln_g(l, s, c):
            return vecs[:, (l * 2 + s) * 16 + c:(l * 2 + s) * 16 + c + 1]

        def ln_b(l, s, c):
            return vecs[:, 128 + (l * 2 + s) * 16 + c:128 + (l * 2 + s) * 16 + c + 1]

        def qn_g(j, c):
            return vecs[:, 256 + j * 4 + c:256 + j * 4 + c + 1]

        def kvn_g(j, c):
            return vecs[:, 264 + j * 4 + c:264 + j * 4 + c + 1]

        def sub_g(j, c):
            return vecs[:, 272 + j * 2 + c:272 + j * 2 + c + 1]

        def lam_c(j, c):
            return vecs[:, 276 + j * 4 + c:276 + j * 4 + c + 1]

        sch.op("dve", lambda e: e.memset(ones, 1.0), writes=[cb])
        sch.op("dve", lambda e: e.memset(ones32, 1.0), writes=[cb])
        sch.dma("sp", vecs, vecs_in[:, :], writes=[cb])
        sch.dma("pool", ident_t[:, :], ident_in[:, :], writes=[cb])
        ident = ident_t[:, :]
        sch.dma("sp", relb_t[:, :], relb_in[:, :], writes=[cb])
        oh_ap = Ft[0][0:32, 0:RL]
        rrow_ap = Ft[1][0:8, 0:RL]
        sch.dma("sp", oh_ap, onehot_in[:, :], writes=[Fb[0]])
        for tb in range(NTB):
            sch.dma("pool", X0B[tb], x_in[tb], writes=[self.db("X0B", tb)])

        has_diff = L >= 2
        if has_diff:
            for c0 in range(0, RL, 512):
                w_ = min(512, RL - c0)
                sch.op("pe", lambda e, c0=c0, w_=w_: e.matmul(PS[0][0:8, 0:w_], relb_t[:, :], oh_ap[:, c0:c0 + w_],
                                                              start=True, stop=True),
                       reads=[cb, Fb[0]], writes=[PSb[0]])
                sch.op("dve", lambda e, c0=c0, w_=w_: e.tensor_copy(out=rrow_ap[:, c0:c0 + w_], in_=PS[0][0:8, 0:w_]),
                       reads=[PSb[0]], writes=[Fb[1]])
            sch.dma("sp", RD[:, :], rrow_ap, reads=[Fb[1]], writes=[self.db("RD")])
            for h in range(8):
                src = bass.AP(RD.tensor, h * RL, [[0, 128], [1, RL]])
                sch.dma("sp", ZT[h], src, reads=[self.db("RD")], writes=[self.db("ZT", h)])
            for h in range(8):
                for s_, row in ((0, 15), (1, 31)):
                    src = bass.AP(relb_in.tensor, row * 8 + h, [[0, 128], [1, 1]])
                    sch.dma("sp", small[:, 16 + 2 * h + s_:17 + 2 * h + s_], src, writes=[smallb])

        db = self.db
        PPd = self.dscr("PP", [NTB, 128, 16, TB], F32)
        QDd = self.dscr("QD", [NTB, 128, 16, TB], BF16)
        KDd = self.dscr("KD", [NTB, 128, 16, TB], BF16)
        VDd = self.dscr("VD", [32, 128, 2048], BF16)
        self.itc = 0
        self.rt_cache = {}

        def load_w(src_ap, kc, ncols):
            wi = nxt("w", 2)
            dst = v3(W[wi], 0, kc, ncols)
            step = max(1, 4096 // ncols)
            for c0 in range(0, kc, step):
                c1 = min(kc, c0 + step)
                sch.dma("pool", dst[:, c0:c1, :], src_ap[:, c0:c1, :], writes=[Wb[wi]])
            return wi

        def load_a(src_ap, kc, srcb, q="sp"):
            ai = nxt("a", 2)
            sch.dma(q, v3(A[ai], 0, kc, TB), src_ap, reads=[srcb], writes=[Ab[ai]])
            return ai

        class Pend:
            def __init__(self):
                self.p = None

            def push(self, fn):
                if self.p is not None:
                    self.p()
                self.p = fn

            def flush(self):
                self.push(None)

        def gemm(wslabs, kc_total, ncols, groups, a_src, a_name, kgroups, epi, aq="sp", pre=None, ksplit=False):
            pend = Pend()
            for sl, wsrc in enumerate(wslabs):
                if sl == 0 and pre is not None:
                    spare = 1 - rr["w"]
                if ksplit:
                    half = kc_total // 2
                    wvs = []
                    for hi_ in range(2):
                        dstv = v3(W[hi_], 0, half, ncols)
                        step = max(1, 4096 // ncols)
                        for c0 in range(0, half, step):
                            c1 = min(half, c0 + step)
                            sch.dma("pool", dstv[:, c0:c1, :], wsrc[:, hi_ * half + c0:hi_ * half + c1, :], writes=[Wb[hi_]])
                        wvs.append(dstv)
                else:
                    wi = load_w(wsrc, kc_total, ncols)
                    wv = v3(W[wi], 0, kc_total, ncols)
                for tb in range(NTB):
                    if sl == 0 and pre is not None and tb == 0:
                        pend.flush()
                        pre(0, spare)
                    single = len(kgroups) == 1
                    if single:
                        ai0 = load_a(a_src(tb), kc_total, db(a_name, tb), q=aq)
                    for gi, tiles in enumerate(groups):
                        base = 4 * (self.itc % 2)
                        self.itc += 1
                        banks = [base + i for i in range(len(tiles))]
                        for kgi, kcs in enumerate(kgroups):
                            if single:
                                ai = ai0
                                av = v3(A[ai], 0, kc_total, TB)
                            else:
                                ai = load_a(a_src(tb)[:, kcs[0]:kcs[-1] + 1, :], len(kcs), db(a_name, tb), q=aq)
                                av = v3(A[ai], 0, len(kcs), TB)
                            for t_i, (co, wd) in enumerate(tiles):
                                for i, kc in enumerate(kcs):
                                    ia = kc if single else i
                                    if ksplit:
                                        wi = kc // half
                                        wv = wvs[wi]
                                        kcw = kc % half
                                    else:
                                        kcw = kc
                                    sch.op("pe", lambda e, b=banks[t_i], co=co, wd=wd, kc=kcw, ia=ia, wv=wv, av=av,
                                           st_=(kgi == 0 and i == 0),
                                           sp_=(kgi == len(kgroups) - 1 and i == len(kcs) - 1):
                                           e.matmul(PS[b][0:wd, :], wv[:, kc, co:co + wd], av[:, ia, :], start=st_, stop=sp_),
                                           reads=[Wb[wi], Ab[ai]], writes=[PSb[banks[t_i]]])
                        pend.push(lambda sl=sl, tb=tb, gi=gi, banks=banks: epi(sl, tb, gi, banks))
                    if sl == 0 and pre is not None and tb + 1 < NTB:
                        pend.flush()
                        pre(tb + 1, spare)
            pend.flush()

        def copy_bank(eng, out_ap, outb, bank, rows=128):
            if eng == "act":
                sch.op("act", lambda e: e.activation(out=out_ap, in_=PS[bank][0:rows, :], func=AF.Copy),
                       reads=[PSb[bank]], writes=[outb])
            else:
                sch.op("dve", lambda e: e.tensor_copy(out=out_ap, in_=PS[bank][0:rows, :]),
                       reads=[PSb[bank]], writes=[outb])

        def epi_store_bf16(dst_fn):
            def epi(sl, tb, gi, banks):
                si = nxt("sb", 4)
                sv = v3(SBa[si], 0, len(banks), TB)
                for t_i, b in enumerate(banks):
                    copy_bank("act" if t_i % 2 == 0 else "dve", sv[:, t_i, :], SBb[si], b)
                dap, dname = dst_fn(sl, tb, gi)
                sch.dma("sp", dap, sv, reads=[SBb[si]], writes=[db(dname, tb)])
            return epi

        def rms_epi(cf_ap, cfb, nt, n, gfn, extra, stat_bank, out_ap, outb):
            for j in range(nt):
                pi = nxt("pt", 4)
                sqv = PTa[pi][:, 0:n]
                sch.op("pool", lambda e, j=j, sqv=sqv: e.tensor_tensor(out=sqv, in0=cf_ap[:, j, :], in1=cf_ap[:, j, :],
                                                                      op=ALU.mult), reads=[cfb], writes=[PTb[pi]])
                sch.op("pe", lambda e, j=j, sqv=sqv: e.matmul(PS[stat_bank][:, 0:n], ones, sqv, start=(j == 0),
                                                             stop=(j == nt - 1)),
                       reads=[PTb[pi], cb], writes=[PSb[stat_bank]])
            ti = nxt("t", 6)
            rs = Ta[ti][:, 0:n]
            sch.op("dve", lambda e: e.tensor_scalar(out=rs, in0=PS[stat_bank][:, 0:n], scalar1=1.0 / (nt * 128),
                                                    scalar2=RMS_EPS, op0=ALU.mult, op1=ALU.add),
                   reads=[PSb[stat_bank]], writes=[Tb_[ti]])
            sch.op("act", lambda e: e.activation(out=rs, in_=rs, func=AF.Sqrt), reads=[Tb_[ti]], writes=[Tb_[ti]])
            sch.op("dve", lambda e: e.reciprocal(out=rs, in_=rs), reads=[Tb_[ti]], writes=[Tb_[ti]])
            if extra != 1.0:
                sch.op("dve", lambda e: e.tensor_scalar(out=rs, in0=rs, scalar1=float(extra), scalar2=None, op0=ALU.mult),
                       reads=[Tb_[ti]], writes=[Tb_[ti]])
            for j in range(nt):
                sch.op("dve", lambda e, j=j: e.scalar_tensor_tensor(out=out_ap[:, j, :], in0=cf_ap[:, j, :], scalar=gfn(j),
                                                                     in1=rs, op0=ALU.mult, op1=ALU.mult),
                       reads=[cfb, Tb_[ti], cb], writes=[outb])

        def rope_epi(bank_x, bank_sw, tb, out_ap, outb):
            if self.rt_cache.get("tb") != tb:
                ri = nxt("rt", 1)
                sch.dma("sp", RTa[ri].rearrange("p (c n) -> p c n", n=TB), rope_in[tb], writes=[RTb[ri]])
                self.rt_cache = {"tb": tb, "ri": ri}
            ri = self.rt_cache["ri"]
            rt = RTa[ri].rearrange("p (c n) -> p c n", n=TB)
            t1 = nxt("t", 6)
            t2 = nxt("t", 6)
            sch.op("dve", lambda e: e.tensor_tensor(out=Ta[t1][0:64, :], in0=PS[bank_x][0:64, :], in1=rt[:, 0, :], op=ALU.mult),
                   reads=[PSb[bank_x], RTb[ri]], writes=[Tb_[t1]])
            sch.op("dve", lambda e: e.tensor_tensor(out=Ta[t2][0:64, :], in0=PS[bank_sw][0:64, :], in1=rt[:, 1, :], op=ALU.mult),
                   reads=[PSb[bank_sw], RTb[ri]], writes=[Tb_[t2]])
            sch.op("pool", lambda e: e.tensor_tensor(out=out_ap, in0=Ta[t1][0:64, :], in1=Ta[t2][0:64, :], op=ALU.add),
                   reads=[Tb_[t1], Tb_[t2]], writes=[outb])

        lnb = {}

        fq = [Buf("fq%d" % i, Fb[i // 2]) for i in range(4)]
        HBQ = 128
        NQ = TB // HBQ
        lnstat = {}

        def fq_ap(qi):
            return Fa[qi // 2][:, (qi % 2) * 2048:(qi % 2) * 2048 + 2048]

        NG = NTB * NQ

        def ln_pre(l, s, XFd, XFn, XBd, XBn):
            def zview(g):
                return fq_ap(g % 4).rearrange("p (c n) -> p c n", n=HBQ)

            def L(g):
                tb, hf = g // NQ, g % NQ
                sch.dma("sp", zview(g), Z[tb][:, :, hf * HBQ:(hf + 1) * HBQ], reads=[db("Z", tb)], writes=[fq[g % 4]])

            def A1(g, spare):
                wq = g % 4
                key = (spare, wq)
                if key not in lnb:
                    lnb[key] = (Buf("lnzb%d%d" % key, Wb[spare]), Buf("lnzq%d%d" % key, Wb[spare]))
                zbb, zqb = lnb[key]
                off = wq * 4096
                fa = fq_ap(wq)
                zb = W[spare][:, off:off + 2048]
                zq = W[spare][:, off + 2048:off + 4096]
                zbv = v3(W[spare], off, 16, HBQ)
                zqv = v3(W[spare], off + 2048, 16, HBQ)
                sch.op("dve", lambda e: e.tensor_copy(out=zb, in_=fa), reads=[fq[wq]], writes=[zbb])
                sch.op("act", lambda e: e.activation(out=zq, in_=fa, func=AF.Square), reads=[fq[wq]], writes=[zqb])
                b0 = 2 * wq
                for c in range(16):
                    sch.op("pe", lambda e, c=c: e.matmul(PS[b0][:, 0:HBQ], ones, zbv[:, c, :], start=(c == 0), stop=(c == 15)),
                           reads=[zbb, cb], writes=[PSb[b0]])
                for c in range(16):
                    sch.op("pe", lambda e, c=c: e.matmul(PS[b0 + 1][:, 0:HBQ], ones, zqv[:, c, :], start=(c == 0), stop=(c == 15)),
                           reads=[zqb, cb], writes=[PSb[b0 + 1]])

            def A2(g):
                b0 = 2 * (g % 4)
                si = nxt("st", 16)
                sj = nxt("st", 16)
                lnstat[g] = (si, sj)
                mean = STa[si][:, 0:HBQ]
                rstd = STa[sj][:, 0:HBQ]
                sch.op("dve", lambda e: e.tensor_scalar(out=mean, in0=PS[b0][:, 0:HBQ], scalar1=1.0 / D, scalar2=None,
                                                        op0=ALU.mult), reads=[PSb[b0]], writes=[STb[si]])
                sch.op("dve", lambda e: e.tensor_tensor(out=rstd, in0=mean, in1=mean, op=ALU.mult),
                       reads=[STb[si]], writes=[STb[sj]])
                sch.op("dve", lambda e: e.scalar_tensor_tensor(out=rstd, in0=PS[b0 + 1][:, 0:HBQ], scalar=1.0 / D, in1=rstd,
                                                               op0=ALU.mult, op1=ALU.subtract),
                       reads=[PSb[b0 + 1], STb[sj]], writes=[STb[sj]])
                sch.op("dve", lambda e: e.tensor_scalar(out=rstd, in0=rstd, scalar1=LN_EPS, scalar2=None, op0=ALU.add),
                       reads=[STb[sj]], writes=[STb[sj]])

            def A2b(g):
                si, sj = lnstat[g]
                rstd = STa[sj][:, 0:HBQ]
                sch.op("act", lambda e: e.activation(out=rstd, in_=rstd, func=AF.Sqrt), reads=[STb[sj]], writes=[STb[sj]])
                sch.op("dve", lambda e: e.reciprocal(out=rstd, in_=rstd), reads=[STb[sj]], writes=[STb[sj]])

            def B1(g):
                qi = g % 4
                fbuf = fq[qi]
                z = zview(g)
                si, sj = lnstat[g]
                mean_b = STa[si][:, 0:HBQ].unsqueeze(1).to_broadcast([128, 16, HBQ])
                rstd_b = STa[sj][:, 0:HBQ].unsqueeze(1).to_broadcast([128, 16, HBQ])
                sch.op("dve", lambda e: e.tensor_tensor(out=z, in0=z, in1=mean_b, op=ALU.subtract),
                       reads=[fbuf, STb[si]], writes=[fbuf])
                sch.op("dve", lambda e: e.tensor_tensor(out=z, in0=z, in1=rstd_b, op=ALU.mult),
                       reads=[fbuf, STb[sj]], writes=[fbuf])

            def B(g):
                tb, hf = g // NQ, g % NQ
                qi = g % 4
                fbuf = fq[qi]
                fa = fq_ap(qi)
                z = zview(g)
                for c in range(16):
                    sch.op("act", lambda e, c=c: e.activation(out=z[:, c, :], in_=z[:, c, :], func=AF.Identity,
                                                              bias=ln_b(l, s, c), scale=ln_g(l, s, c)),
                           reads=[fbuf, cb], writes=[fbuf])
                oi = nxt("sb", 4)
                sch.op("pool", lambda e: e.tensor_copy(out=SBa[oi], in_=fa), reads=[fbuf], writes=[SBb[oi]])
                sch.dma("sp", XFd[tb][:, :, hf * HBQ:(hf + 1) * HBQ], z, reads=[fbuf], writes=[db(XFn, tb)])
                sch.dma("sp", XBd[tb][:, :, hf * HBQ:(hf + 1) * HBQ], v3(SBa[oi], 0, 16, HBQ), reads=[SBb[oi]],
                        writes=[db(XBn, tb)])

            a1_done = set()

            def steps(g_lo, g_hi, spare):
                for g in range(g_lo, g_hi):
                    if 0 <= g + 3 < NG:
                        L(g + 3)
                    if 0 <= g < NG:
                        B1(g)
                    if 0 <= g + 1 < NG and (g + 1) not in a1_done:
                        A1(g + 1, spare)
                        a1_done.add(g + 1)
                    if 0 <= g + 2 < NG and g != g_hi - 1:
                        A1(g + 2, spare)
                        a1_done.add(g + 2)
                    if 0 <= g + 1 < NG:
                        A2(g + 1)
                    if 0 <= g < NG:
                        B(g)
                    if 0 <= g + 1 < NG:
                        A2b(g + 1)

            def hook(tb, spare):
                if tb == 0:
                    steps(-3, 2 * NQ, spare)
                elif tb + 1 < NTB:
                    steps((tb + 1) * NQ, (tb + 2) * NQ, spare)
            return hook

        def epi_resid(XFd, XFn, tiles_per_slab, group_tiles):
            def epi(sl, tb, gi, banks):
                c0 = sl * tiles_per_slab + gi * group_tiles
                nt = len(banks)
                xi = nxt("sf", 3)
                xf = v3(SFa[xi], 0, nt, TB)
                sch.dma("sp", xf, XFd[tb][:, c0:c0 + nt, :], reads=[db(XFn, tb)], writes=[SFb[xi]])
                for t_i in range(nt):
                    sch.op("dve", lambda e, t_i=t_i, xf=xf, b=banks[t_i]: e.scalar_tensor_tensor(
                        out=xf[:, t_i, :], in0=xf[:, t_i, :], scalar=float(ALPHA), in1=PS[b][:, :],
                        op0=ALU.mult, op1=ALU.add), reads=[SFb[xi], PSb[banks[t_i]]], writes=[SFb[xi]])
                sch.dma("sp", Z[tb][:, c0:c0 + nt, :], xf, reads=[SFb[xi]], writes=[db("Z", tb)])
            return epi

        def g4(n):
            return [[(c * 128, 128) for c in range(g * 4, g * 4 + 4)] for g in range(n // 512)]

        K16 = [list(range(16))]
        K4 = [list(range(4))]

        def gemm_v(wslabs, kc_total, ncols, a_src, a_name, Vd, vname, vcol0_fn):
            pend = Pend()
            for sl, wsrc in enumerate(wslabs):
                wi = load_w(wsrc, kc_total, ncols)
                wv = v3(W[wi], 0, kc_total, ncols)
                for tb in range(NTB):
                    ai = load_a(a_src(tb), kc_total, db(a_name, tb))
                    av = v3(A[ai], 0, kc_total, TB)
                    ncg = ncols // 512
                    for tt in range(4):
                        bl = []
                        for cg in range(ncg):
                            b = self.itc % 8
                            self.itc += 1
                            bl.append(b)
                            for kc in range(kc_total):
                                sch.op("pe", lambda e, b=b, kc=kc, tt=tt, cg=cg, wv=wv, av=av: e.matmul(
                                    PS[b][:, :], av[:, kc, tt * 128:(tt + 1) * 128], wv[:, kc, cg * 512:(cg + 1) * 512],
                                    start=(kc == 0), stop=(kc == kc_total - 1)),
                                    reads=[Wb[wi], Ab[ai]], writes=[PSb[b]])

                        def epi(sl=sl, tb=tb, tt=tt, bl=bl, ncg=ncg):
                            si = nxt("sb", 4)
                            for cg, b in enumerate(bl):
                                copy_bank("act" if cg % 2 == 0 else "dve", SBa[si][:, cg * 512:(cg + 1) * 512], SBb[si], b)
                            col = vcol0_fn(sl)
                            sch.dma("sp", Vd[tb * 4 + tt][:, col:col + ncg * 512], SBa[si][:, 0:ncg * 512], reads=[SBb[si]],
                                    writes=[db(vname, tb)])
                        pend.push(epi)
            pend.flush()

        def attn_mla():
            scale = float(192 ** -0.5)
            KNv = KN.rearrange("t p c n -> p t c n")
            QNv = QN.rearrange("t p c n -> p t c n")
            QRv = QR.rearrange("t p c n -> p t c n")
            Vv = V.rearrange("t p f -> p t f")
            krb = Buf("krb", Ab[0])
            krv = A[0][0:64, 0:4096]
            sch.dma("sp", krv.rearrange("p (t n) -> p t n", n=TB), KR.rearrange("t p n -> p t n"),
                    reads=[db("KR", t) for t in range(NTB)], writes=[krb])
            knb = [Buf("knb%d" % i, Wb[0]) for i in range(2)]
            vb = [Buf("vb%d" % i, Wb[0]) for i in range(2)]
            qnb = [Buf("qnb%d" % i, Wb[1]) for i in range(2)]
            qrb = [Buf("qrb%d" % i, Wb[1]) for i in range(2)]
            allk = [db("KN", t) for t in range(NTB)]
            allv = [db("V", t) for t in range(NTB)]
            allq = [db("QN", t) for t in range(NTB)]
            allqr = [db("QR", t) for t in range(NTB)]
            cnt = 0
            sidx = 0
            accb = [[Buf("accD%d" % i, Fb[i]), Buf("accP%d" % i, Fb[i])] for i in range(2)]
            acca = [[Fa[i][:, 0:512], Fa[i][:, 512:1024]] for i in range(2)]

            def views(h):
                i = h % 2
                return (W[0][:, i * 4096:(i + 1) * 4096], v3(W[0], 8192 + i * 4096, 32, 128),
                        W[1][:, i * 4096:(i + 1) * 4096], W[1][0:64, 8192 + i * 4096:8192 + (i + 1) * 4096])

            def loads(h):
                i = h % 2
                knv, vv, qnv, qrv = views(h)
                sch.dma("sp", knv.rearrange("p (t n) -> p t n", n=TB), KNv[:, :, h, :], reads=allk, writes=[knb[i]])
                sch.dma("sp", qnv.rearrange("p (t n) -> p t n", n=TB), QNv[:, :, h, :], reads=allq, writes=[qnb[i]])
                sch.dma("sp", qrv.rearrange("p (t n) -> p t n", n=TB), QRv[:, :, h, :], reads=allqr, writes=[qrb[i]])
                sch.dma("sp", vv, Vv[:, :, h * 128:(h + 1) * 128], reads=allv, writes=[vb[i]])
            loads(0)
            for h in range(16):
                i = h % 2
                knv, vv, qnv, qrv = views(h)
                if h + 1 < 16:
                    loads(h + 1)
                for qb in range(NTB):
                    ob = 3 + (cnt % 2)
                    smb = 5 + (cnt % 2)
                    cnt += 1
                    qs = slice(qb * TB, (qb + 1) * TB)
                    sbank = {}

                    def emit_s(kt):
                        nonlocal sidx
                        b = sidx % 3
                        sidx += 1
                        sbank[kt] = b
                        ks = slice(kt * 128, (kt + 1) * 128)
                        sch.op("pe", lambda e, b=b, ks=ks, knv=knv, qnv=qnv, qs=qs: e.matmul(PS[b][:, :], knv[:, ks], qnv[:, qs], start=True, stop=False),
                               reads=[knb[i], qnb[i]], writes=[PSb[b]])
                        sch.op("pe", lambda e, b=b, ks=ks, qrv=qrv, qs=qs: e.matmul(PS[b][:, :], krv[:, ks], qrv[:, qs], start=False, stop=True),
                               reads=[krb, qrb[i]], writes=[PSb[b]])
                    emit_s(0)
                    emit_s(1)
                    for kt in range(32):
                        if kt + 2 < 32:
                            emit_s(kt + 2)
                        b = sbank[kt]
                        pi = nxt("pt", 4)
                        sch.op("act", lambda e, b=b, pi=pi: e.activation(out=PTa[pi], in_=PS[b][:, :], func=AF.Exp, scale=scale),
                               reads=[PSb[b]], writes=[PTb[pi]])
                        sch.op("pe", lambda e, pi=pi, kt=kt, ob=ob, vv=vv: e.matmul(PS[ob][:, :], vv[:, kt, :], PTa[pi],
                                                                            start=(kt == 0), stop=(kt == 31)),
                               reads=[vb[i], PTb[pi]], writes=[PSb[ob]])
                        par = cnt % 2
                        which = 1 if kt % 3 == 2 else 0
                        eng = "pool" if which else "dve"
                        acc_ap = acca[par][which]
                        if kt == (2 if which else 0):
                            sch.op(eng, lambda e, pi=pi, acc_ap=acc_ap: e.tensor_copy(out=acc_ap, in_=PTa[pi]),
                                   reads=[PTb[pi]], writes=[accb[par][which]])
                        else:
                            sch.op(eng, lambda e, pi=pi, acc_ap=acc_ap: e.tensor_tensor(out=acc_ap, in0=acc_ap, in1=PTa[pi],
                                                                                       op=ALU.add),
                                   reads=[PTb[pi], accb[par][which]], writes=[accb[par][which]])
                    for which in range(2):
                        sch.op("pe", lambda e, which=which, smb=smb, a_=acca[cnt % 2][which]: e.matmul(
                            PS[smb][:, :], ones32, a_, start=(which == 0), stop=(which == 1)),
                            reads=[cb, accb[cnt % 2][which]], writes=[PSb[smb]])
                    ti = nxt("t", 6)
                    si = nxt("sb", 4)
                    sch.op("dve", lambda e, ti=ti, smb=smb: e.reciprocal(out=Ta[ti], in_=PS[smb][:, :]),
                           reads=[PSb[smb]], writes=[Tb_[ti]])
                    sch.op("dve", lambda e, ti=ti, si=si, ob=ob: e.tensor_tensor(out=SBa[si][:, 0:512], in0=PS[ob][:, :],
                                                                                in1=Ta[ti], op=ALU.mult),
                           reads=[PSb[ob], Tb_[ti]], writes=[SBb[si]])
                    sch.dma("sp", OB[qb][:, h, :], SBa[si][:, 0:512], reads=[SBb[si]], writes=[db("OB", qb)])

        def attn_diff(j, layer_idx):
            scale = float(128 ** -0.5)
            lambda_init = 0.8 - 0.6 * math.exp(-0.3 * layer_idx)
            sch.op("dve", lambda e: e.tensor_tensor(out=small[:, 0:1], in0=lam_c(j, 0), in1=lam_c(j, 1), op=ALU.mult),
                   reads=[cb], writes=[smallb])
            sch.op("dve", lambda e: e.tensor_tensor(out=small[:, 1:2], in0=lam_c(j, 2), in1=lam_c(j, 3), op=ALU.mult),
                   reads=[cb, smallb], writes=[smallb])
            sch.op("pe", lambda e: e.matmul(PS[7][:, 0:2], ones32, small[:, 0:2], start=True, stop=True),
                   reads=[cb, smallb], writes=[PSb[7]])
            sch.op("act", lambda e: e.activation(out=small[:, 2:4], in_=PS[7][:, 0:2], func=AF.Exp),
                   reads=[PSb[7], smallb], writes=[smallb])
            sch.op("dve", lambda e: e.tensor_tensor(out=small[:, 4:5], in0=small[:, 3:4], in1=small[:, 2:3], op=ALU.subtract),
                   reads=[smallb], writes=[smallb])
            sch.op("dve", lambda e: e.tensor_scalar(out=small[:, 4:5], in0=small[:, 4:5], scalar1=-float(lambda_init),
                                                    scalar2=None, op0=ALU.add), reads=[smallb], writes=[smallb])
            QDv = QDd.rearrange("t p c n -> p t c n")
            KDv = KDd.rearrange("t p c n -> p t c n")
            VDv = VDd.rearrange("t p f -> p t f")
            allq = [db("QD", t) for t in range(NTB)]
            allk = [db("KD", t) for t in range(NTB)]
            allv = [db("VD", t) for t in range(NTB)]
            kqb = [Buf("kq%d" % i, Wb[0]) for i in range(4)]
            vbb = [Buf("vd%d" % i, Wb[1]) for i in range(2)]
            sqb = [Buf("sq%d" % i, Ab[i]) for i in range(2)]
            deferred = []
            sidx = 0
            def vloads(h):
                hi = h % 2
                vv = v3(W[1], hi * 8192, 32, 256)
                sch.dma("sp", vv, VDv[:, :, h * 256:(h + 1) * 256], reads=allv, writes=[vbb[hi]])
                bt = v3(Fa[hi], 0, 6, TB)
                for idx in range(6):
                    delta = idx * 128 - 128
                    c0 = 639 - delta
                    src = bass.AP(ZT.tensor, h * 128 * RL + c0, [[RL - 1, 128], [1, TB]])
                    sch.dma("sp", bt[:, idx, :], src, reads=[db("ZT", h)], writes=[Fb[hi]])
                btf = Fa[hi][:, 0:3072]
                bhi = A[hi][:, 0:3072]
                blo = A[hi][:, 3072:6144]
                sch.op("dve", lambda e, btf=btf: e.tensor_scalar(out=btf, in0=btf, scalar1=float(1.0 / scale), scalar2=None,
                                                                 op0=ALU.mult), reads=[Fb[hi]], writes=[Fb[hi]])
                sch.op("dve", lambda e, btf=btf, bhi=bhi: e.tensor_copy(out=bhi, in_=btf), reads=[Fb[hi]], writes=[Ab[hi]])
                sch.op("dve", lambda e, btf=btf, bhi=bhi: e.tensor_tensor(out=btf, in0=btf, in1=bhi, op=ALU.subtract),
                       reads=[Fb[hi], Ab[hi]], writes=[Fb[hi]])
                sch.op("dve", lambda e, btf=btf, blo=blo: e.tensor_copy(out=blo, in_=btf), reads=[Fb[hi]], writes=[Ab[hi]])
            vloads(0)
            for h in range(8):
                hi = h % 2
                fi = hi
                vv = v3(W[1], hi * 8192, 32, 256)
                bt = v3(Fa[hi], 0, 6, TB)
                kv_ = []
                qv_ = []
                for gi in range(2):
                    g = 2 * h + gi
                    kvw = W[0][:, gi * 4096:(gi + 1) * 4096]
                    qvw = W[0][:, 8192 + gi * 4096:8192 + (gi + 1) * 4096]
                    sch.dma("sp", kvw.rearrange("p (t n) -> p t n", n=TB), KDv[:, :, g, :], reads=allk, writes=[kqb[gi]])
                    sch.dma("sp", qvw.rearrange("p (t n) -> p t n", n=TB), QDv[:, :, g, :], reads=allq, writes=[kqb[2 + gi]])
                    kv_.append(kvw)
                    qv_.append(qvw)
                if h + 1 < 8:
                    vloads(h + 1)
                for qb in range(NTB):
                    qs = slice(qb * TB, (qb + 1) * TB)
                    for gi in range(2):
                        obs = (2 + 2 * gi, 3 + 2 * gi)
                        smb = 6 + gi
                        sbank = {}

                        def emit_s(kt, gi=gi):
                            nonlocal sidx
                            b = sidx % 2
                            sidx += 1
                            sbank[kt] = b
                            ks = slice(kt * 128, (kt + 1) * 128)
                            delta_ = kt * 128 - qb * TB
                            near_ = -128 <= delta_ <= 512
                            sch.op("pe", lambda e, b=b, ks=ks, kk=kv_[gi], qq=qv_[gi], qs=qs, near_=near_: e.matmul(
                                PS[b][:, :], kk[:, ks], qq[:, qs], start=True, stop=(not near_)),
                                reads=[kqb[gi], kqb[2 + gi]], writes=[PSb[b]])
                            if near_:
                                ix = (delta_ + 128) // 128
                                for part in range(2):
                                    bsrc = A[hi][:, part * 3072 + ix * TB:part * 3072 + (ix + 1) * TB]
                                    sch.op("pe", lambda e, b=b, bsrc=bsrc, part=part: e.matmul(
                                        PS[b][:, :], ident, bsrc, start=False, stop=(part == 1)),
                                        reads=[cb, Ab[hi]], writes=[PSb[b]])
                        emit_s(0)
                        emit_s(1)
                        for kt in range(32):
                            b = sbank[kt]
                            pi = nxt("pt", 4)
                            delta = kt * 128 - qb * TB
                            if -128 <= delta <= 512:
                                sch.op("act", lambda e, b=b, pi=pi: e.activation(out=PTa[pi], in_=PS[b][:, :], func=AF.Exp,
                                                                                 scale=scale),
                                       reads=[PSb[b]], writes=[PTb[pi]])
                            else:
                                col = 16 + 2 * h + (0 if delta < 0 else 1)
                                sch.op("act", lambda e, b=b, pi=pi, col=col: e.activation(
                                    out=PTa[pi], in_=PS[b][:, :], func=AF.Exp, scale=scale, bias=small[:, col:col + 1]),
                                    reads=[PSb[b], smallb], writes=[PTb[pi]])
                            for dv in range(2):
                                sch.op("pe", lambda e, pi=pi, kt=kt, dv=dv, ob=obs[dv], vv=vv: e.matmul(
                                    PS[ob][:, :], vv[:, kt, dv * 128:(dv + 1) * 128], PTa[pi], start=(kt == 0), stop=(kt == 31)),
                                    reads=[vbb[hi], PTb[pi]], writes=[PSb[obs[dv]]])
                                if dv == 0 and kt + 2 < 32:
                                    emit_s(kt + 2)
                            sch.op("pe", lambda e, pi=pi, kt=kt, smb=smb: e.matmul(PS[smb][:, :], ones, PTa[pi],
                                                                                  start=(kt == 0), stop=(kt == 31)),
                                   reads=[cb, PTb[pi]], writes=[PSb[smb]])
                            if gi == 0 and kt == 6 and deferred:
                                deferred.pop(0)()
                    r0 = nxt("t", 6)
                    r1 = nxt("t", 6)
                    sch.op("dve", lambda e, r0=r0: e.reciprocal(out=Ta[r0], in_=PS[6][:, :]), reads=[PSb[6]], writes=[Tb_[r0]])
                    sch.op("dve", lambda e, r1=r1: e.reciprocal(out=Ta[r1], in_=PS[7][:, :]), reads=[PSb[7]], writes=[Tb_[r1]])
                    sch.op("dve", lambda e, r1=r1: e.tensor_scalar(out=Ta[r1], in0=Ta[r1], scalar1=small[:, 4:5], scalar2=None,
                                                                   op0=ALU.mult), reads=[Tb_[r1], smallb], writes=[Tb_[r1]])
                    ci = nxt("sf", 3)
                    cf = v3(SFa[ci], 0, 2, TB)
                    for dv in range(2):
                        t0 = nxt("t", 6)
                        sch.op("dve", lambda e, t0=t0, dv=dv, r0=r0: e.tensor_tensor(out=Ta[t0], in0=PS[2 + dv][:, :], in1=Ta[r0],
                                                                                     op=ALU.mult),
                               reads=[PSb[2 + dv], Tb_[r0]], writes=[Tb_[t0]])
                        sch.op("dve", lambda e, dv=dv, r1=r1, cf=cf: e.tensor_tensor(out=cf[:, dv, :], in0=PS[4 + dv][:, :],
                                                                                     in1=Ta[r1], op=ALU.mult),
                               reads=[PSb[4 + dv], Tb_[r1]], writes=[SFb[ci]])
                        sch.op("pool", lambda e, t0=t0, dv=dv, cf=cf: e.tensor_tensor(out=cf[:, dv, :], in0=cf[:, dv, :],
                                                                                      in1=Ta[t0], op=ALU.add),
                               reads=[SFb[ci], Tb_[t0]], writes=[SFb[ci]])
                    sqv = v3(A[hi], 6144, 2, TB)
                    for dv in range(2):
                        sch.op("pool", lambda e, dv=dv, cf=cf, sqv=sqv: e.tensor_tensor(out=sqv[:, dv, :], in0=cf[:, dv, :],
                                                                                        in1=cf[:, dv, :], op=ALU.mult),
                               reads=[SFb[ci]], writes=[sqb[hi]])

                    def part2(cf=cf, ci=ci, sqv=sqv, hi=hi, qb=qb, h=h):
                        for dv in range(2):
                            sch.op("pe", lambda e, dv=dv, sqv=sqv: e.matmul(PS[7][:, :], ones, sqv[:, dv, :], start=(dv == 0),
                                                                           stop=(dv == 1)),
                                   reads=[sqb[hi], cb], writes=[PSb[7]])
                        ti = nxt("t", 6)
                        rs = Ta[ti]
                        sch.op("dve", lambda e, rs=rs: e.tensor_scalar(out=rs, in0=PS[7][:, :], scalar1=1.0 / 256, scalar2=RMS_EPS,
                                                                      op0=ALU.mult, op1=ALU.add),
                               reads=[PSb[7]], writes=[Tb_[ti]])
                        sch.op("act", lambda e, rs=rs: e.activation(out=rs, in_=rs, func=AF.Sqrt), reads=[Tb_[ti]], writes=[Tb_[ti]])
                        sch.op("dve", lambda e, rs=rs: e.reciprocal(out=rs, in_=rs), reads=[Tb_[ti]], writes=[Tb_[ti]])
                        sch.op("dve", lambda e, rs=rs: e.tensor_scalar(out=rs, in0=rs, scalar1=float(1.0 - lambda_init), scalar2=None,
                                                                      op0=ALU.mult), reads=[Tb_[ti]], writes=[Tb_[ti]])
                        si = nxt("sb", 4)
                        ov = v3(SBa[si], 0, 2, TB)
                        for dv in range(2):
                            sch.op("dve", lambda e, dv=dv, cf=cf, ov=ov, rs=rs: e.scalar_tensor_tensor(
                                out=ov[:, dv, :], in0=cf[:, dv, :], scalar=sub_g(j, dv), in1=rs, op0=ALU.mult, op1=ALU.mult),
                                reads=[SFb[ci], Tb_[ti], cb], writes=[SBb[si]])
                        sch.dma("sp", OB[qb][:, 2 * h:2 * h + 2, :], ov, reads=[SBb[si]], writes=[db("OB", qb)])
                    deferred.append(part2)
            while deferred:
                deferred.pop(0)()

        XF_cur, XF_name = x_in, "x"
        for l in range(L):
            j = l // 2
            if l % 2 == 0:
                def epi_m1(sl, tb, gi, banks, j=j):
                    if gi < 2:
                        fi = nxt("sf", 3)
                        cf = v3(SFa[fi], 0, 4, TB)
                        for t_i, b in enumerate(banks):
                            copy_bank("act", cf[:, t_i, :], SFb[fi], b)
                        si = nxt("sb", 4)
                        ov = v3(SBa[si], 0, 4, TB)
                        gfn = (lambda c: qn_g(j, c)) if gi == 0 else (lambda c: kvn_g(j, c))
                        rms_epi(cf, SFb[fi], 4, TB, gfn, 1.0, banks[0], ov, SBb[si])
                        dst, nm = (CQN, "CQN") if gi == 0 else (CKVN, "CKVN")
                        sch.dma("sp", dst[tb], ov, reads=[SBb[si]], writes=[db(nm, tb)])
                    else:
                        si = nxt("sb", 4)
                        ov = SBa[si][0:64, 0:512]
                        rope_epi(banks[0], banks[1], tb, ov, SBb[si])
                        sch.dma("sp", KR[tb], ov, reads=[SBb[si]], writes=[db("KR", tb)])
                groups = [[(c * 128, 128) for c in range(0, 4)], [(c * 128, 128) for c in range(4, 8)],
                          [(1024, 64), (1088, 64)]]
                gemm([mla_win[j]], 16, 1152, groups, lambda tb: X0B[tb], "X0B", K16, epi_m1)

                m2s = {}

                def epi_m2(sl, tb, gi, banks):
                    h = gi
                    if h % 4 == 0:
                        m2s["n"] = nxt("sb", 4)
                        m2s["r"] = nxt("sb", 4)
                    si, s2 = m2s["n"], m2s["r"]
                    k = h % 4
                    copy_bank("act", SBa[si][:, k * 512:(k + 1) * 512], SBb[si], banks[0])
                    rope_epi(banks[1], banks[2], tb, SBa[s2][0:64, k * 512:(k + 1) * 512], SBb[s2])
                    if k == 3:
                        sch.dma("sp", QN[tb][:, h - 3:h + 1, :], v3(SBa[si], 0, 4, TB), reads=[SBb[si]], writes=[db("QN", tb)])
                        sch.dma("sp", QR[tb][:, h - 3:h + 1, :], v3(SBa[s2], 0, 4, TB)[0:64], reads=[SBb[s2]],
                                writes=[db("QR", tb)])
                groups = [[(h * 256, 128), (h * 256 + 128, 64), (h * 256 + 192, 64)] for h in range(16)]
                gemm([mla_wuq[j]], 4, 4096, groups, lambda tb: CQN[tb], "CQN", K4, epi_m2)
                gemm([mla_wuk[j]], 4, 2048, g4(2048), lambda tb: CKVN[tb], "CKVN", K4,
                     epi_store_bf16(lambda sl, tb, gi: (KN[tb][:, gi * 4:gi * 4 + 4, :], "KN")))
                gemm_v([mla_wuv[j]], 4, 2048, lambda tb: CKVN[tb], "CKVN", V, "V", lambda sl: 0)
                attn_mla()
                wo = mla_wo[j]
            else:
                def dst_qk(sl, tb, gi):
                    t0 = sl * 8 + gi * 4
                    if t0 < 16:
                        return QDd[tb][:, t0:t0 + 4, :], "QD"
                    return KDd[tb][:, t0 - 16:t0 - 12, :], "KD"
                gemm([diff_wqk[j][s_] for s_ in range(4)], 16, 1024, g4(1024), lambda tb: X0B[tb], "X0B", K16,
                     epi_store_bf16(dst_qk))
                gemm_v([diff_wv[j][0], diff_wv[j][1]], 16, 1024, lambda tb: X0B[tb], "X0B", VDd, "VD", lambda sl: sl * 1024)
                attn_diff(j, l)
                wo = diff_wo[j]
            gemm([wo[0], wo[1]], 16, 1024, g4(1024), lambda tb: OB[tb], "OB", K16, epi_resid(XF_cur, XF_name, 8, 4))

            def epi_f1(sl, tb, gi, banks):
                si = nxt("sb", 4)
                hv = v3(SBa[si], 0, 2, TB)
                for t in range(2):
                    ti = nxt("t", 6)
                    sch.op("act", lambda e, ti=ti, b=banks[t]: e.activation(out=Ta[ti], in_=PS[b][:, :], func=AF.Silu),
                           reads=[PSb[banks[t]]], writes=[Tb_[ti]])
                    sch.op("dve", lambda e, ti=ti, t=t, hv=hv, b=banks[2 + t]: e.tensor_tensor(
                        out=hv[:, t, :], in0=Ta[ti], in1=PS[b][:, :], op=ALU.mult),
                        reads=[Tb_[ti], PSb[banks[2 + t]]], writes=[SBb[si]])
                c0 = sl * 4 + gi * 2
                sch.dma("sp", H[tb][:, c0:c0 + 2, :], hv, reads=[SBb[si]], writes=[db("H", tb)])
            groups = [[(gi * 512 + k * 128, 128) for k in range(4)] for gi in range(2)]
            gemm([ffn_win[l][s_] for s_ in range(11)], 16, 1024, groups, lambda tb: X1B[tb], "X1B", K16, epi_f1,
                 pre=ln_pre(l, 0, X1F, "X1F", X1B, "X1B"))
            kg44 = [list(range(g * 11, g * 11 + 11)) for g in range(4)]
            gemm([ffn_wout[l][s_] for s_ in range(4)], 44, 512, g4(512), lambda tb: H[tb], "H", kg44,
                 epi_resid(X1F, "X1F", 4, 4), ksplit=True)
            last = (l == L - 1)
            XNd, XNn = (out_d, "out") if last else (X0F, "X0F")
            pre = ln_pre(l, 1, X2F, "X2F", X2B, "X2B")
            pend = Pend()
            for sl in range(2):
                if sl == 0:
                    spare = 1 - rr["w"]
                wi = nxt("w", 2)
                gv = v3(W[wi], 0, 16, 1024)
                pv = v3(W[wi], 16384, 2, 1024)
                for c0_ in range(0, 16, 4):
                    sch.dma("pool", gv[:, c0_:c0_ + 4, :], ple_wg[l][sl][:, c0_:c0_ + 4, :], writes=[Wb[wi]])
                sch.dma("pool", pv, ple_wp[l][sl], writes=[Wb[wi]])
                for tb in range(NTB):
                    if sl == 0 and tb == 0:
                        pend.flush()
                        pre(0, spare)
                    ai = load_a(X2B[tb], 16, db("X2B", tb))
                    av = v3(A[ai], 0, 16, TB)
                    pi_ = nxt("sb", 4)
                    ptv = v3(SBa[pi_], 0, 2, TB)
                    sch.dma("pool", ptv, p_in[l][tb], writes=[SBb[pi_]])
                    for gi in range(4):
                        base = 4 * (self.itc % 2)
                        self.itc += 1
                        banks = [base, base + 1, base + 2, base + 3]
                        for t in range(2):
                            co = gi * 256 + t * 128
                            for kc in range(16):
                                sch.op("pe", lambda e, b=banks[t], co=co, kc=kc, gv=gv, av=av: e.matmul(
                                    PS[b][:, :], gv[:, kc, co:co + 128], av[:, kc, :], start=(kc == 0), stop=(kc == 15)),
                                    reads=[Wb[wi], Ab[ai]], writes=[PSb[banks[t]]])
                            for kc in range(2):
                                sch.op("pe", lambda e, b=banks[2 + t], co=co, kc=kc, pv=pv, ptv=ptv: e.matmul(
                                    PS[b][:, :], pv[:, kc, co:co + 128], ptv[:, kc, :], start=(kc == 0), stop=(kc == 1)),
                                    reads=[Wb[wi], SBb[pi_]], writes=[PSb[banks[2 + t]]])

                        def epi(sl=sl, tb=tb, gi=gi, banks=banks):
                            c0 = sl * 8 + gi * 2
                            xi = nxt("sf", 3)
                            xv = v3(SFa[xi], 0, 2, TB)
                            sch.dma("sp", xv, X2F[tb][:, c0:c0 + 2, :], reads=[db("X2F", tb)], writes=[SFb[xi]])
                            si = nxt("sb", 4)
                            bv = v3(SBa[si], 0, 2, TB)
                            tis = []
                            for t in range(2):
                                ti = nxt("t", 6)
                                tis.append(ti)
                                sch.op("act", lambda e, ti=ti, b=banks[t]: e.activation(out=Ta[ti], in_=PS[b][:, :],
                                                                                        func=AF.Sigmoid),
                                       reads=[PSb[banks[t]]], writes=[Tb_[ti]])
                                sch.op("dve", lambda e, ti=ti, b=banks[2 + t]: e.tensor_tensor(out=Ta[ti], in0=Ta[ti],
                                                                                               in1=PS[b][:, :], op=ALU.mult),
                                       reads=[Tb_[ti], PSb[banks[2 + t]]], writes=[Tb_[ti]])
                            for t in range(2):
                                sch.op("dve", lambda e, t=t, ti=tis[t], xv=xv: e.tensor_tensor(out=xv[:, t, :], in0=xv[:, t, :],
                                                                                               in1=Ta[ti], op=ALU.add),
                                       reads=[Tb_[tis[t]], SFb[xi]], writes=[SFb[xi]])
                            sch.dma("sp", XNd[tb][:, c0:c0 + 2, :], xv, reads=[SFb[xi]], writes=[db(XNn, tb)])
                            if not last:
                                sch.op("act", lambda e, xv=xv, bv=bv: e.activation(out=bv, in_=xv, func=AF.Copy),
                                       reads=[SFb[xi]], writes=[SBb[si]])
                                sch.dma("sp", X0B[tb][:, c0:c0 + 2, :], bv, reads=[SBb[si]], writes=[db("X0B", tb)])
                        pend.push(epi)
                    if sl == 0 and tb + 1 < NTB:
                        pend.flush()
                        pre(tb + 1, spare)
            pend.flush()
            XF_cur, XF_name = X0F, "X0F"

        sch.finalize()
        with nc.Block() as block:
            @block.tensor
            def _(e):
                sch.emit(e, "pe")

            @block.scalar
            def _(e):
                sch.emit(e, "act")

            @block.vector
            def _(e):
                sch.emit(e, "dve")

            @block.gpsimd
            def _(e):
                sch.emit(e, "pool")

            @block.sync
            def _(e):
                sch.emit(e, "sp")
        self.stack.close()
        return self


def _pack(w, ncols):
    k, n = w.shape
    return np.ascontiguousarray(w.reshape(k // 128, 128, n // ncols, ncols).transpose(2, 1, 0, 3))


def _fm(a, c):
    return np.ascontiguousarray(a.reshape(NTB, TB, c, 128).transpose(0, 3, 2, 1))


def _prep_shared(inp):
    f = lambda a: np.asarray(a, dtype=np.float32)
    sw = np.concatenate([np.arange(32, 64), np.arange(0, 32)])
    o = {}
    w = f(inp["mla_w_in"])
    o["mla_win"] = np.stack([_pack(np.concatenate([w[j], w[j][:, 1024:1088][:, sw]], 1), 1152)[0] for j in range(2)])
    w = f(inp["mla_w_uq"]).reshape(2, 512, 16, 192)
    wq = np.concatenate([w[..., :128], w[..., 128:192], w[..., 128:192][..., sw]], -1).reshape(2, 512, 4096)
    o["mla_wuq"] = np.stack([_pack(wq[j], 4096)[0] for j in range(2)])
    w = f(inp["mla_w_ukv"]).reshape(2, 512, 16, 256)
    o["mla_wuk"] = np.stack([_pack(np.ascontiguousarray(w[j][..., :128]).reshape(512, 2048), 2048)[0] for j in range(2)])
    o["mla_wuv"] = np.stack([_pack(np.ascontiguousarray(w[j][..., 128:]).reshape(512, 2048), 2048)[0] for j in range(2)])
    o["mla_wo"] = np.stack([_pack(f(inp["mla_w_o"])[j], 1024) for j in range(2)])
    w = f(inp["diff_w_in"])
    o["diff_wqk"] = np.stack([_pack(np.ascontiguousarray(w[j][:, :4096]), 1024) for j in range(2)])
    o["diff_wv"] = np.stack([_pack(np.ascontiguousarray(w[j][:, 4096:]), 1024) for j in range(2)])
    o["diff_wo"] = np.stack([_pack(f(inp["diff_w_o"])[j], 1024) for j in range(2)])
    w = f(inp["ffn_w_in"])
    lst = []
    for l in range(DEPTH):
        g = w[l][:, :DFF].reshape(D, 11, 2, 1, 256)
        u = w[l][:, DFF:].reshape(D, 11, 2, 1, 256)
        lst.append(_pack(np.concatenate([g, u], 3).reshape(D, 11 * 1024), 1024))
    o["ffn_win"] = np.stack(lst)
    o["ffn_wout"] = np.stack([_pack(f(inp["ffn_w_out"])[l], 512) for l in range(DEPTH)])
    o["ple_wg"] = np.stack([_pack(f(inp["ple_w_gate"])[l], 1024) for l in range(DEPTH)])
    o["ple_wp"] = np.stack([_pack(f(inp["ple_w_proj"])[l], 1024) for l in range(DEPTH)])
    vecs = np.zeros((128, 284), np.float32)
    vecs[:, 0:128] = f(inp["ln_g"]).reshape(4, 2, 16, 128).transpose(3, 0, 1, 2).reshape(128, 128)
    vecs[:, 128:256] = f(inp["ln_b"]).reshape(4, 2, 16, 128).transpose(3, 0, 1, 2).reshape(128, 128)
    vecs[:, 256:264] = f(inp["mla_q_norm"]).reshape(2, 4, 128).transpose(2, 0, 1).reshape(128, 8)
    vecs[:, 264:272] = f(inp["mla_kv_norm"]).reshape(2, 4, 128).transpose(2, 0, 1).reshape(128, 8)
    vecs[:, 272:276] = f(inp["diff_sub_norm"]).reshape(2, 2, 128).transpose(2, 0, 1).reshape(128, 4)
    vecs[:, 276:284] = f(inp["diff_lambda"]).transpose(2, 0, 1).reshape(128, 8)
    o["vecs"] = vecs
    o["relb"] = np.ascontiguousarray(f(inp["rel_bias"]))
    rope, onehot = _const_tables()
    o["rope"] = rope
    o["onehot"] = onehot
    o["ident"] = np.eye(128, dtype=np.float32)
    return o


_NC_CACHE = {}


def _get_nc(debug=None, nlayers=DEPTH, stop_after=None):
    key = (tuple(debug or ()), nlayers, stop_after)
    if key not in _NC_CACHE:
        kb = KB(debug=debug, nlayers=nlayers)
        kb.stop_after = stop_after
        kb.build()
        _NC_CACHE[key] = kb.nc
    return _NC_CACHE[key]


def kernel(**inputs):
    shared = _prep_shared(inputs)
    x = np.asarray(inputs["x"], dtype=np.float32)
    p = np.asarray(inputs["p"], dtype=np.float32)
    in_maps = []
    for b in range(NCORES):
        m = dict(shared)
        m["x"] = _fm(x[b], 16)
        m["p"] = np.stack([_fm(p[l, b], 2) for l in range(DEPTH)])
        in_maps.append(m)
    nc = _get_nc()
    res = run_bass_kernel_spmd(nc, in_maps, core_ids=list(range(NCORES)))
    out = np.empty((NCORES, S, D), np.float32)
    for b in range(NCORES):
        o = np.asarray(res.results[b]["out"]).reshape(NTB, 128, 16, TB)
        out[b] = o.transpose(0, 3, 2, 1).reshape(S, D)
    return out
```

```python
import math
from contextlib import ExitStack
import numpy as np
import concourse.bass as bass
import concourse.mybir as mybir
from concourse.bass_utils import run_bass_kernel_spmd

F32 = mybir.dt.float32
BF16 = mybir.dt.bfloat16
AF = mybir.ActivationFunctionType
ALU = mybir.AluOpType

D = 2048
S = 4096
NTB = 8
TB = 512
DEPTH = 4
DFF = 5632
ALPHA = (2 * DEPTH) ** 0.25
LN_EPS = 1e-5
RMS_EPS = 1e-6
NCORES = 8
RL = 1280


class Buf:
    __slots__ = ("name", "w", "r", "parent", "kids", "wd")

    def __init__(self, name, parent=None):
        self.name = name
        self.w = None
        self.wd = {}
        self.r = {}
        self.parent = parent
        self.kids = []
        if parent is not None:
            parent.kids.append(self)


class Op:
    __slots__ = ("eng", "fn", "waits", "observed", "idx", "val", "dma_sem")

    def __init__(self, eng, fn):
        self.eng = eng
        self.fn = fn
        self.waits = []
        self.observed = False
        self.idx = 0
        self.val = 0
        self.dma_sem = None


ENGS = ["pe", "act", "dve", "pool", "sp"]


class Sched:
    def __init__(self, nc, stack, rings=None):
        rings = rings or {"sp": 28, "pool": 20}
        self.nc = nc
        self.ops = {e: [] for e in ENGS}
        self.esem = {e: stack.enter_context(nc.semaphore("es_" + e)) for e in ENGS}
        self.rings = {q: [stack.enter_context(nc.semaphore("rg_%s%d" % (q, i))) for i in range(n)]
                      for q, n in rings.items()}
        self.ring_use = {q: [0] * n for q, n in rings.items()}
        self.ring_pos = {q: 0 for q in rings}
        self.waited = {e: {} for e in ENGS}

    @staticmethod
    def _key(ev):
        return ev[1]

    @staticmethod
    def _ord(ev):
        return ev[2].idx + 1 if ev[0] == "e" else ev[2]

    def _collect(self, reads, writes):
        evs = []

        def wr(b):
            if b.w:
                evs.append(b.w)
            evs.extend(b.wd.values())
        for b in reads:
            wr(b)
            if b.parent is not None:
                wr(b.parent)
            for k in b.kids:
                wr(k)
        for b in writes:
            wr(b)
            evs.extend(b.r.values())
            if b.parent is not None:
                p = b.parent
                wr(p)
                evs.extend(p.r.values())
            for k in b.kids:
                wr(k)
                evs.extend(k.r.values())
        return evs

    def _waits(self, eng, evs):
        wd = self.waited[eng]
        best = {}
        for ev in evs:
            k = self._key(ev)
            if ev[0] == "e" and k == eng and eng in ("pe", "sp"):
                continue
            o = self._ord(ev)
            if wd.get(k, 0) >= o:
                continue
            if k not in best or self._ord(best[k]) < o:
                best[k] = ev
        out = []
        for k, ev in best.items():
            wd[k] = self._ord(ev)
            if ev[0] == "e":
                ev[2].observed = True
            out.append(ev)
        return out

    def _record(self, ev, reads, writes):
        k = self._key(ev)
        for b in reads:
            b.r[k] = ev
        for b in writes:
            if ev[0] == "d":
                b.wd[k] = ev
            else:
                b.w = ev
                b.wd = {}
            b.r = {}
            for kd in b.kids:
                kd.w = None
                kd.wd = {}
                kd.r = {}

    def op(self, eng, fn, reads=(), writes=()):
        o = Op(eng, fn)
        o.waits = self._waits(eng, self._collect(reads, writes))
        o.idx = len(self.ops[eng])
        self.ops[eng].append(o)
        self._record(("e", eng, o), reads, writes)
        return o

    def dma(self, q, out_ap, in_ap, reads=(), writes=()):
        n = len(self.rings[q])
        slot = self.ring_pos[q]
        self.ring_pos[q] = (slot + 1) % n
        prev = 16 * self.ring_use[q][slot]
        self.ring_use[q][slot] += 1
        evs = self._collect(reads, writes)
        if prev:
            evs.append(("d", (q, slot), prev))
        o = Op(q, lambda e: e.dma_start(out=out_ap, in_=in_ap))
        o.waits = self._waits(q, evs)
        o.idx = len(self.ops[q])
        o.dma_sem = self.rings[q][slot]
        self.ops[q].append(o)
        self._record(("d", (q, slot), prev + 16), reads, writes)
        return o

    def finalize(self):
        for e in ENGS:
            c = 0
            for o in self.ops[e]:
                if o.observed and o.dma_sem is None:
                    c += 1
                    o.val = c

    def emit(self, engobj, eng):
        for o in self.ops[eng]:
            for ev in o.waits:
                if ev[0] == "e":
                    engobj.wait_ge(self.esem[ev[1]], ev[2].val)
                else:
                    q, slot = ev[1]
                    engobj.wait_ge(self.rings[q][slot], ev[2])
            ins = o.fn(engobj)
            if o.dma_sem is not None:
                ins.then_inc(o.dma_sem, 16)
            elif o.observed:
                ins.then_inc(self.esem[eng], 1)
        if eng == "sp":
            for q, sems in self.rings.items():
                for slot, sem in enumerate(sems):
                    u = self.ring_use[q][slot]
                    if u:
                        engobj.wait_ge(sem, 16 * u)


def _t5_bucket_np(rel):
    nb = 16
    max_exact = 8
    ret = (rel > 0).astype(np.int32) * nb
    n = np.abs(rel)
    nf = np.maximum(n, 1).astype(np.float32)
    large = max_exact + (np.log(nf / np.float32(max_exact)) / np.float32(math.log(128 / max_exact))
                         * np.float32(nb - max_exact)).astype(np.int32)
    large = np.minimum(large, nb - 1)
    return ret + np.where(n < max_exact, n, large)


def _const_tables():
    pos = np.arange(S, dtype=np.float32)
    inv = (1.0 / (np.float32(10000.0) ** (np.arange(0, 64, 2, dtype=np.float32) / np.float32(64)))).astype(np.float32)
    ang = pos[:, None] * inv[None, :]
    cos = np.cos(ang).astype(np.float32).T
    sin = np.sin(ang).astype(np.float32).T
    cos2 = np.concatenate([cos, cos], 0)
    sins = np.concatenate([-sin, sin], 0)
    rope = np.stack([cos2, sins], 1)
    rope = rope.reshape(64, 2, NTB, TB).transpose(2, 0, 1, 3)
    i = np.arange(RL)
    rel = (RL - 1 - i) - 640
    bk = _t5_bucket_np(rel)
    onehot = np.zeros((32, RL), np.float32)
    onehot[bk, i] = 1.0
    return np.ascontiguousarray(rope), onehot


class KB:
    def __init__(self, debug=None, nlayers=DEPTH):
        self.debug = debug or ()
        self.nlayers = nlayers
        self.nc = bass.Bass("TRN2", target_bir_lowering=False)
        self.stack = ExitStack()
        self.dbufs = {}
        self.dt = {}
        self.stop_after = None

    def din(self, name, shape):
        t = self.nc.dram_tensor(name, list(shape), F32, kind="ExternalInput")
        self.dt[name] = t.ap()
        return self.dt[name]

    def dscr(self, name, shape, dtype, out=False):
        kind = "ExternalOutput" if (out or name in self.debug) else "Internal"
        t = self.nc.dram_tensor(name, list(shape), dtype, kind=kind)
        self.dt[name] = t.ap()
        return self.dt[name]

    def db(self, name, i=0):
        k = (name, i)
        if k not in self.dbufs:
            self.dbufs[k] = Buf("%s_%d" % (name, i))
        return self.dbufs[k]

    def sb(self, name, shape, dtype):
        return self.stack.enter_context(self.nc.sbuf_tensor(name, list(shape), dtype))

    def build(self):
        nc = self.nc
        st = self.stack
        L = self.nlayers
        x_in = self.din("x", [NTB, 128, 16, TB])
        p_in = self.din("p", [DEPTH, NTB, 128, 2, TB])
        mla_win = self.din("mla_win", [2, 128, 16, 1152])
        mla_wuq = self.din("mla_wuq", [2, 128, 4, 4096])
        mla_wuk = self.din("mla_wuk", [2, 128, 4, 2048])
        mla_wuv = self.din("mla_wuv", [2, 128, 4, 2048])
        mla_wo = self.din("mla_wo", [2, 2, 128, 16, 1024])
        diff_wqk = self.din("diff_wqk", [2, 4, 128, 16, 1024])
        diff_wv = self.din("diff_wv", [2, 2, 128, 16, 1024])
        diff_wo = self.din("diff_wo", [2, 2, 128, 16, 1024])
        ffn_win = self.din("ffn_win", [DEPTH, 11, 128, 16, 1024])
        ffn_wout = self.din("ffn_wout", [DEPTH, 4, 128, 44, 512])
        ple_wg = self.din("ple_wg", [DEPTH, 2, 128, 16, 1024])
        ple_wp = self.din("ple_wp", [DEPTH, 2, 128, 2, 1024])
        vecs_in = self.din("vecs", [128, 284])
        relb_in = self.din("relb", [32, 8])
        rope_in = self.din("rope", [NTB, 64, 2, TB])
        onehot_in = self.din("onehot", [32, RL])
        ident_in = self.din("ident", [128, 128])

        out_d = self.dscr("out", [NTB, 128, 16, TB], F32, out=True)
        X0F = self.dscr("X0F", [NTB, 128, 16, TB], F32)
        X1F = self.dscr("X1F", [NTB, 128, 16, TB], F32)
        X2F = self.dscr("X2F", [NTB, 128, 16, TB], F32)
        X0B = self.dscr("X0B", [NTB, 128, 16, TB], BF16)
        X1B = self.dscr("X1B", [NTB, 128, 16, TB], BF16)
        X2B = self.dscr("X2B", [NTB, 128, 16, TB], BF16)
        Z = self.dscr("Z", [NTB, 128, 16, TB], F32)
        OB = self.dscr("OB", [NTB, 128, 16, TB], BF16)
        H = self.dscr("H", [NTB, 128, 44, TB], BF16)
        CQN = self.dscr("CQN", [NTB, 128, 4, TB], BF16)
        CKVN = self.dscr("CKVN", [NTB, 128, 4, TB], BF16)
        KR = self.dscr("KR", [NTB, 64, TB], BF16)
        QN = self.dscr("QN", [NTB, 128, 16, TB], BF16)
        QR = self.dscr("QR", [NTB, 64, 16, TB], BF16)
        KN = self.dscr("KN", [NTB, 128, 16, TB], BF16)
        V = self.dscr("V", [32, 128, 2048], BF16)
        RD = self.dscr("RD", [8, RL], F32)
        ZT = self.dscr("ZT", [8, 128, RL], F32)

        Wt = [self.sb("W%d" % i, [128, 18432], BF16) for i in range(2)]
        At = [self.sb("A%d" % i, [128, 8192], BF16) for i in range(2)]
        Ft = [self.sb("F%d" % i, [128, 4096], F32) for i in range(2)]
        SBt = [self.sb("SB%d" % i, [128, 2048], BF16) for i in range(4)]
        SFt = [self.sb("SF%d" % i, [128, 2048], F32) for i in range(3)]
        Tt = [self.sb("T%d" % i, [128, 512], F32) for i in range(6)]
        PTt = [self.sb("PT%d" % i, [128, 512], BF16) for i in range(4)]
        STt = [self.sb("ST%d" % i, [128, 128], F32) for i in range(16)]
        RTt = [self.sb("RT%d" % i, [64, 1024], F32) for i in range(1)]
        ones_t = self.sb("ones", [128, 128], BF16)
        ones32_t = self.sb("ones32", [128, 128], F32)
        ident_t = self.sb("ident_sb", [128, 128], BF16)
        vecs_t = self.sb("vecs_sb", [128, 284], F32)
        small_t = self.sb("small", [128, 64], F32)
        relb_t = self.sb("relb_sb", [32, 8], F32)
        PSt = [st.enter_context(nc.psum_tensor("ps%d" % i, [128, 512], F32)) for i in range(8)]

        sch = Sched(nc, st)
        self.sch = sch
        Wb = [Buf("W0"), Buf("W1")]
        Ab = [Buf("A0"), Buf("A1")]
        Fb = [Buf("F0"), Buf("F1")]
        SBb = [Buf("SB%d" % i) for i in range(4)]
        SFb = [Buf("SF%d" % i) for i in range(3)]
        Tb_ = [Buf("T%d" % i) for i in range(6)]
        PTb = [Buf("PT%d" % i) for i in range(4)]
        STb = [Buf("ST%d" % i) for i in range(16)]
        RTb = [Buf("RT%d" % i) for i in range(1)]
        PSb = [Buf("PS%d" % i) for i in range(8)]
        cb = Buf("consts")
        smallb = Buf("small")
        W = [t[:, :] for t in Wt]
        A = [t[:, :] for t in At]
        Fa = [t[:, :] for t in Ft]
        SBa = [t[:, :] for t in SBt]
        SFa = [t[:, :] for t in SFt]
        Ta = [t[:, :] for t in Tt]
        PTa = [t[:, :] for t in PTt]
        STa = [t[:, :] for t in STt]
        RTa = [t[:, :] for t in RTt]
        PS = [t[:, :] for t in PSt]
        ones = ones_t[:, :]
        ones32 = ones32_t[:, :]
        vecs = vecs_t[:, :]
        small = small_t[:, :]

        rr = {"fqa": 0, "fqb": 0, "sb": 0, "sf": 0, "t": 0, "pt": 0, "st": 0, "w": 0, "a": 0, "f": 0, "rt": 0}

        def nxt(kind, n):
            i = rr[kind]
            rr[kind] = (i + 1) % n
            return i

        def v3(ap, off, c, n):
            return ap[:, off:off + c * n].rearrange("p (c n) -> p c n", n=n)

        def ln_g(l, s, c):
            return vecs[:, (l * 2 + s) * 16 + c:(l * 2 + s) * 16 + c + 1]

        def ln_b(l, s, c):
            return vecs[:, 128 + (l * 2 + s) * 16 + c:128 + (l * 2 + s) * 16 + c + 1]

        def qn_g(j, c):
            return vecs[:, 256 + j * 4 + c:256 + j * 4 + c + 1]

        def kvn_g(j, c):
            return vecs[:, 264 + j * 4 + c:264 + j * 4 + c + 1]

        def sub_g(j, c):
            return vecs[:, 272 + j * 2 + c:272 + j * 2 + c + 1]

        def lam_c(j, c):
            return vecs[:, 276 + j * 4 + c:276 + j * 4 + c + 1]

        sch.op("dve", lambda e: e.memset(ones, 1.0), writes=[cb])
        sch.op("dve", lambda e: e.memset(ones32, 1.0), writes=[cb])
        sch.dma("sp", vecs, vecs_in[:, :], writes=[cb])
        sch.dma("pool", ident_t[:, :], ident_in[:, :], writes=[cb])
        ident = ident_t[:, :]
        sch.dma("sp", relb_t[:, :], relb_in[:, :], writes=[cb])
        oh_ap = Ft[0][0:32, 0:RL]
        rrow_ap = Ft[1][0:8, 0:RL]
        sch.dma("sp", oh_ap, onehot_in[:, :], writes=[Fb[0]])
        for tb in range(NTB):
            sch.dma("pool", X0B[tb], x_in[tb], writes=[self.db("X0B", tb)])

        has_diff = L >= 2
        if has_diff:
            for c0 in range(0, RL, 512):
                w_ = min(512, RL - c0)
                sch.op("pe", lambda e, c0=c0, w_=w_: e.matmul(PS[0][0:8, 0:w_], relb_t[:, :], oh_ap[:, c0:c0 + w_],
                                                              start=True, stop=True),
                       reads=[cb, Fb[0]], writes=[PSb[0]])
                sch.op("dve", lambda e, c0=c0, w_=w_: e.tensor_copy(out=rrow_ap[:, c0:c0 + w_], in_=PS[0][0:8, 0:w_]),
                       reads=[PSb[0]], writes=[Fb[1]])
            sch.dma("sp", RD[:, :], rrow_ap, reads=[Fb[1]], writes=[self.db("RD")])
            for h in range(8):
                src = bass.AP(RD.tensor, h * RL, [[0, 128], [1, RL]])
                sch.dma("sp", ZT[h], src, reads=[self.db("RD")], writes=[self.db("ZT", h)])
            for h in range(8):
                for s_, row in ((0, 15), (1, 31)):
                    src = bass.AP(relb_in.tensor, row * 8 + h, [[0, 128], [1, 1]])
                    sch.dma("sp", small[:, 16 + 2 * h + s_:17 + 2 * h + s_], src, writes=[smallb])

        db = self.db
        PPd = self.dscr("PP", [NTB, 128, 16, TB], F32)
        QDd = self.dscr("QD", [NTB, 128, 16, TB], BF16)
        KDd = self.dscr("KD", [NTB, 128, 16, TB], BF16)
        VDd = self.dscr("VD", [32, 128, 2048], BF16)
        self.itc = 0
        self.rt_cache = {}

        def load_w(src_ap, kc, ncols):
            wi = nxt("w", 2)
            dst = v3(W[wi], 0, kc, ncols)
            step = max(1, 4096 // ncols)
            for c0 in range(0, kc, step):
                c1 = min(kc, c0 + step)
                sch.dma("pool", dst[:, c0:c1, :], src_ap[:, c0:c1, :], writes=[Wb[wi]])
            return wi

        def load_a(src_ap, kc, srcb, q="sp"):
            ai = nxt("a", 2)
            sch.dma(q, v3(A[ai], 0, kc, TB), src_ap, reads=[srcb], writes=[Ab[ai]])
            return ai

        class Pend:
            def __init__(self):
                self.p = None

            def push(self, fn):
                if self.p is not None:
                    self.p()
                self.p = fn

            def flush(self):
                self.push(None)

        def gemm(wslabs, kc_total, ncols, groups, a_src, a_name, kgroups, epi, aq="sp", pre=None, ksplit=False):
            pend = Pend()
            for sl, wsrc in enumerate(wslabs):
                if sl == 0 and pre is not None:
                    spare = 1 - rr["w"]
                if ksplit:
                    half = kc_total // 2
                    wvs = []
                    for hi_ in range(2):
                        dstv = v3(W[hi_], 0, half, ncols)
                        step = max(1, 4096 // ncols)
                        for c0 in range(0, half, step):
                            c1 = min(half, c0 + step)
                            sch.dma("pool", dstv[:, c0:c1, :], wsrc[:, hi_ * half + c0:hi_ * half + c1, :], writes=[Wb[hi_]])
                        wvs.append(dstv)
                else:
                    wi = load_w(wsrc, kc_total, ncols)
                    wv = v3(W[wi], 0, kc_total, ncols)
                for tb in range(NTB):
                    if sl == 0 and pre is not None and tb == 0:
                        pend.flush()
                        pre(0, spare)
                    single = len(kgroups) == 1
                    if single:
                        ai0 = load_a(a_src(tb), kc_total, db(a_name, tb), q=aq)
                    for gi, tiles in enumerate(groups):
                        base = 4 * (self.itc % 2)
                        self.itc += 1
                        banks = [base + i for i in range(len(tiles))]
                        for kgi, kcs in enumerate(kgroups):
                            if single:
                                ai = ai0
                                av = v3(A[ai], 0, kc_total, TB)
                            else:
                                ai = load_a(a_src(tb)[:, kcs[0]:kcs[-1] + 1, :], len(kcs), db(a_name, tb), q=aq)
                                av = v3(A[ai], 0, len(kcs), TB)
                            for t_i, (co, wd) in enumerate(tiles):
                                for i, kc in enumerate(kcs):
                                    ia = kc if single else i
                                    if ksplit:
                                        wi = kc // half
                                        wv = wvs[wi]
                                        kcw = kc % half
                                    else:
                                        kcw = kc
                                    sch.op("pe", lambda e, b=banks[t_i], co=co, wd=wd, kc=kcw, ia=ia, wv=wv, av=av,
                                           st_=(kgi == 0 and i == 0),
                                           sp_=(kgi == len(kgroups) - 1 and i == len(kcs) - 1):
                                           e.matmul(PS[b][0:wd, :], wv[:, kc, co:co + wd], av[:, ia, :], start=st_, stop=sp_),
                                           reads=[Wb[wi], Ab[ai]], writes=[PSb[banks[t_i]]])
                        pend.push(lambda sl=sl, tb=tb, gi=gi, banks=banks: epi(sl, tb, gi, banks))
                    if sl == 0 and pre is not None and tb + 1 < NTB:
                        pend.flush()
                        pre(tb + 1, spare)
            pend.flush()

        def copy_bank(eng, out_ap, outb, bank, rows=128):
            if eng == "act":
                sch.op("act", lambda e: e.activation(out=out_ap, in_=PS[bank][0:rows, :], func=AF.Copy),
                       reads=[PSb[bank]], writes=[outb])
            else:
                sch.op("dve", lambda e: e.tensor_copy(out=out_ap, in_=PS[bank][0:rows, :]),
                       reads=[PSb[bank]], writes=[outb])

        def epi_store_bf16(dst_fn):
            def epi(sl, tb, gi, banks):
                si = nxt("sb", 4)
                sv = v3(SBa[si], 0, len(banks), TB)
                for t_i, b in enumerate(banks):
                    copy_bank("act" if t_i % 2 == 0 else "dve", sv[:, t_i, :], SBb[si], b)
                dap, dname = dst_fn(sl, tb, gi)
                sch.dma("sp", dap, sv, reads=[SBb[si]], writes=[db(dname, tb)])
            return epi

        def rms_epi(cf_ap, cfb, nt, n, gfn, extra, stat_bank, out_ap, outb):
            for j in range(nt):
                pi = nxt("pt", 4)
                sqv = PTa[pi][:, 0:n]
                sch.op("pool", lambda e, j=j, sqv=sqv: e.tensor_tensor(out=sqv, in0=cf_ap[:, j, :], in1=cf_ap[:, j, :],
                                                                      op=ALU.mult), reads=[cfb], writes=[PTb[pi]])
                sch.op("pe", lambda e, j=j, sqv=sqv: e.matmul(PS[stat_bank][:, 0:n], ones, sqv, start=(j == 0),
                                                             stop=(j == nt - 1)),
                       reads=[PTb[pi], cb], writes=[PSb[stat_bank]])
            ti = nxt("t", 6)
            rs = Ta[ti][:, 0:n]
            sch.op("dve", lambda e: e.tensor_scalar(out=rs, in0=PS[stat_bank][:, 0:n], scalar1=1.0 / (nt * 128),
                                                    scalar2=RMS_EPS, op0=ALU.mult, op1=ALU.add),
                   reads=[PSb[stat_bank]], writes=[Tb_[ti]])
            sch.op("act", lambda e: e.activation(out=rs, in_=rs, func=AF.Sqrt), reads=[Tb_[ti]], writes=[Tb_[ti]])
            sch.op("dve", lambda e: e.reciprocal(out=rs, in_=rs), reads=[Tb_[ti]], writes=[Tb_[ti]])
            if extra != 1.0:
                sch.op("dve", lambda e: e.tensor_scalar(out=rs, in0=rs, scalar1=float(extra), scalar2=None, op0=ALU.mult),
                       reads=[Tb_[ti]], writes=[Tb_[ti]])
            for j in range(nt):
                sch.op("dve", lambda e, j=j: e.scalar_tensor_tensor(out=out_ap[:, j, :], in0=cf_ap[:, j, :], scalar=gfn(j),
                                                                     in1=rs, op0=ALU.mult, op1=ALU.mult),
                       reads=[cfb, Tb_[ti], cb], writes=[outb])

        def rope_epi(bank_x, bank_sw, tb, out_ap, outb):
            if self.rt_cache.get("tb") != tb:
                ri = nxt("rt", 1)
                sch.dma("sp", RTa[ri].rearrange("p (c n) -> p c n", n=TB), rope_in[tb], writes=[RTb[ri]])
                self.rt_cache = {"tb": tb, "ri": ri}
            ri = self.rt_cache["ri"]
            rt = RTa[ri].rearrange("p (c n) -> p c n", n=TB)
            t1 = nxt("t", 6)
            t2 = nxt("t", 6)
            sch.op("dve", lambda e: e.tensor_tensor(out=Ta[t1][0:64, :], in0=PS[bank_x][0:64, :], in1=rt[:, 0, :], op=ALU.mult),
                   reads=[PSb[bank_x], RTb[ri]], writes=[Tb_[t1]])
            sch.op("dve", lambda e: e.tensor_tensor(out=Ta[t2][0:64, :], in0=PS[bank_sw][0:64, :], in1=rt[:, 1, :], op=ALU.mult),
                   reads=[PSb[bank_sw], RTb[ri]], writes=[Tb_[t2]])
            sch.op("pool", lambda e: e.tensor_tensor(out=out_ap, in0=Ta[t1][0:64, :], in1=Ta[t2][0:64, :], op=ALU.add),
                   reads=[Tb_[t1], Tb_[t2]], writes=[outb])

        lnb = {}

        fq = [Buf("fq%d" % i, Fb[i // 2]) for i in range(4)]
        HBQ = 128
        NQ = TB // HBQ
        lnstat = {}

        def fq_ap(qi):
            return Fa[qi // 2][:, (qi % 2) * 2048:(qi % 2) * 2048 + 2048]

        NG = NTB * NQ

        def ln_pre(l, s, XFd, XFn, XBd, XBn):
            def zview(g):
                return fq_ap(g % 4).rearrange("p (c n) -> p c n", n=HBQ)

            def L(g):
                tb, hf = g // NQ, g % NQ
                sch.dma("sp", zview(g), Z[tb][:, :, hf * HBQ:(hf + 1) * HBQ], reads=[db("Z", tb)], writes=[fq[g % 4]])

            def A1(g, spare):
                wq = g % 4
                key = (spare, wq)
                if key not in lnb:
                    lnb[key] = (Buf("lnzb%d%d" % key, Wb[spare]), Buf("lnzq%d%d" % key, Wb[spare]))
                zbb, zqb = lnb[key]
                off = wq * 4096
                fa = fq_ap(wq)
                zb = W[spare][:, off:off + 2048]
                zq = W[spare][:, off + 2048:off + 4096]
                zbv = v3(W[spare], off, 16, HBQ)
                zqv = v3(W[spare], off + 2048, 16, HBQ)
                sch.op("dve", lambda e: e.tensor_copy(out=zb, in_=fa), reads=[fq[wq]], writes=[zbb])
                sch.op("act", lambda e: e.activation(out=zq, in_=fa, func=AF.Square), reads=[fq[wq]], writes=[zqb])
                b0 = 2 * wq
                for c in range(16):
                    sch.op("pe", lambda e, c=c: e.matmul(PS[b0][:, 0:HBQ], ones, zbv[:, c, :], start=(c == 0), stop=(c == 15)),
                           reads=[zbb, cb], writes=[PSb[b0]])
                for c in range(16):
                    sch.op("pe", lambda e, c=c: e.matmul(PS[b0 + 1][:, 0:HBQ], ones, zqv[:, c, :], start=(c == 0), stop=(c == 15)),
                           reads=[zqb, cb], writes=[PSb[b0 + 1]])

            def A2(g):
                b0 = 2 * (g % 4)
                si = nxt("st", 16)
                sj = nxt("st", 16)
                lnstat[g] = (si, sj)
                mean = STa[si][:, 0:HBQ]
                rstd = STa[sj][:, 0:HBQ]
                sch.op("dve", lambda e: e.tensor_scalar(out=mean, in0=PS[b0][:, 0:HBQ], scalar1=1.0 / D, scalar2=None,
                                                        op0=ALU.mult), reads=[PSb[b0]], writes=[STb[si]])
                sch.op("dve", lambda e: e.tensor_tensor(out=rstd, in0=mean, in1=mean, op=ALU.mult),
                       reads=[STb[si]], writes=[STb[sj]])
                sch.op("dve", lambda e: e.scalar_tensor_tensor(out=rstd, in0=PS[b0 + 1][:, 0:HBQ], scalar=1.0 / D, in1=rstd,
                                                               op0=ALU.mult, op1=ALU.subtract),
                       reads=[PSb[b0 + 1], STb[sj]], writes=[STb[sj]])
                sch.op("dve", lambda e: e.tensor_scalar(out=rstd, in0=rstd, scalar1=LN_EPS, scalar2=None, op0=ALU.add),
                       reads=[STb[sj]], writes=[STb[sj]])

            def A2b(g):
                si, sj = lnstat[g]
                rstd = STa[sj][:, 0:HBQ]
                sch.op("act", lambda e: e.activation(out=rstd, in_=rstd, func=AF.Sqrt), reads=[STb[sj]], writes=[STb[sj]])
                sch.op("dve", lambda e: e.reciprocal(out=rstd, in_=rstd), reads=[STb[sj]], writes=[STb[sj]])

            def B1(g):
                qi = g % 4
                fbuf = fq[qi]
                z = zview(g)
                si, sj = lnstat[g]
                mean_b = STa[si][:, 0:HBQ].unsqueeze(1).to_broadcast([128, 16, HBQ])
                rstd_b = STa[sj][:, 0:HBQ].unsqueeze(1).to_broadcast([128, 16, HBQ])
                sch.op("dve", lambda e: e.tensor_tensor(out=z, in0=z, in1=mean_b, op=ALU.subtract),
                       reads=[fbuf, STb[si]], writes=[fbuf])
                sch.op("dve", lambda e: e.tensor_tensor(out=z, in0=z, in1=rstd_b, op=ALU.mult),
                       reads=[fbuf, STb[sj]], writes=[fbuf])

            def B(g):
                tb, hf = g // NQ, g % NQ
                qi = g % 4
                fbuf = fq[qi]
                fa = fq_ap(qi)
                z = zview(g)
                for c in range(16):
                    sch.op("act", lambda e, c=c: e.activation(out=z[:, c, :], in_=z[:, c, :], func=AF.Identity,
                                                              bias=ln_b(l, s, c), scale=ln_g(l, s, c)),
                           reads=[fbuf, cb], writes=[fbuf])
                oi = nxt("sb", 4)
                sch.op("act", lambda e: e.activation(out=SBa[oi], in_=fa, func=AF.Copy), reads=[fbuf], writes=[SBb[oi]])
                sch.dma("sp", XFd[tb][:, :, hf * HBQ:(hf + 1) * HBQ], z, reads=[fbuf], writes=[db(XFn, tb)])
                sch.dma("sp", XBd[tb][:, :, hf * HBQ:(hf + 1) * HBQ], v3(SBa[oi], 0, 16, HBQ), reads=[SBb[oi]],
                        writes=[db(XBn, tb)])

            a1_done = set()

            def steps(g_lo, g_hi, spare):
                for g in range(g_lo, g_hi):
                    if 0 <= g + 3 < NG:
                        L(g + 3)
                    if 0 <= g < NG:
                        B1(g)
                    if 0 <= g + 1 < NG and (g + 1) not in a1_done:
                        A1(g + 1, spare)
                        a1_done.add(g + 1)
                    if 0 <= g + 2 < NG and g != g_hi - 1:
                        A1(g + 2, spare)
                        a1_done.add(g + 2)
                    if 0 <= g + 1 < NG:
                        A2(g + 1)
                    if 0 <= g < NG:
                        B(g)
                    if 0 <= g + 1 < NG:
                        A2b(g + 1)

            def hook(tb, spare):
                if tb == 0:
                    steps(-3, 2 * NQ, spare)
                elif tb + 1 < NTB:
                    steps((tb + 1) * NQ, (tb + 2) * NQ, spare)
            return hook

        def epi_resid(XFd, XFn, tiles_per_slab, group_tiles):
            def epi(sl, tb, gi, banks):
                c0 = sl * tiles_per_slab + gi * group_tiles
                nt = len(banks)
                xi = nxt("sf", 3)
                xf = v3(SFa[xi], 0, nt, TB)
                sch.dma("sp", xf, XFd[tb][:, c0:c0 + nt, :], reads=[db(XFn, tb)], writes=[SFb[xi]])
                for t_i in range(nt):
                    sch.op("dve", lambda e, t_i=t_i, xf=xf, b=banks[t_i]: e.scalar_tensor_tensor(
                        out=xf[:, t_i, :], in0=xf[:, t_i, :], scalar=float(ALPHA), in1=PS[b][:, :],
                        op0=ALU.mult, op1=ALU.add), reads=[SFb[xi], PSb[banks[t_i]]], writes=[SFb[xi]])
                sch.dma("sp", Z[tb][:, c0:c0 + nt, :], xf, reads=[SFb[xi]], writes=[db("Z", tb)])
            return epi

        def g4(n):
            return [[(c * 128, 128) for c in range(g * 4, g * 4 + 4)] for g in range(n // 512)]

        K16 = [list(range(16))]
        K4 = [list(range(4))]

        def gemm_v(wslabs, kc_total, ncols, a_src, a_name, Vd, vname, vcol0_fn):
            pend = Pend()
            for sl, wsrc in enumerate(wslabs):
                wi = load_w(wsrc, kc_total, ncols)
                wv = v3(W[wi], 0, kc_total, ncols)
                for tb in range(NTB):
                    ai = load_a(a_src(tb), kc_total, db(a_name, tb))
                    av = v3(A[ai], 0, kc_total, TB)
                    ncg = ncols // 512
                    for tt in range(4):
                        bl = []
                        for cg in range(ncg):
                            b = self.itc % 8
                            self.itc += 1
                            bl.append(b)
                            for kc in range(kc_total):
                                sch.op("pe", lambda e, b=b, kc=kc, tt=tt, cg=cg, wv=wv, av=av: e.matmul(
                                    PS[b][:, :], av[:, kc, tt * 128:(tt + 1) * 128], wv[:, kc, cg * 512:(cg + 1) * 512],
                                    start=(kc == 0), stop=(kc == kc_total - 1)),
                                    reads=[Wb[wi], Ab[ai]], writes=[PSb[b]])

                        def epi(sl=sl, tb=tb, tt=tt, bl=bl, ncg=ncg):
                            si = nxt("sb", 4)
                            for cg, b in enumerate(bl):
                                copy_bank("act" if cg % 2 == 0 else "dve", SBa[si][:, cg * 512:(cg + 1) * 512], SBb[si], b)
                            col = vcol0_fn(sl)
                            sch.dma("sp", Vd[tb * 4 + tt][:, col:col + ncg * 512], SBa[si][:, 0:ncg * 512], reads=[SBb[si]],
                                    writes=[db(vname, tb)])
                        pend.push(epi)
            pend.flush()

        def attn_mla():
            scale = float(192 ** -0.5)
            KNv = KN.rearrange("t p c n -> p t c n")
            QNv = QN.rearrange("t p c n -> p t c n")
            QRv = QR.rearrange("t p c n -> p t c n")
            Vv = V.rearrange("t p f -> p t f")
            krb = Buf("krb", Ab[0])
            krv = A[0][0:64, 0:4096]
            sch.dma("sp", krv.rearrange("p (t n) -> p t n", n=TB), KR.rearrange("t p n -> p t n"),
                    reads=[db("KR", t) for t in range(NTB)], writes=[krb])
            knb = [Buf("knb%d" % i, Wb[0]) for i in range(2)]
            vb = [Buf("vb%d" % i, Wb[0]) for i in range(2)]
            qnb = [Buf("qnb%d" % i, Wb[1]) for i in range(2)]
            qrb = [Buf("qrb%d" % i, Wb[1]) for i in range(2)]
            allk = [db("KN", t) for t in range(NTB)]
            allv = [db("V", t) for t in range(NTB)]
            allq = [db("QN", t) for t in range(NTB)]
            allqr = [db("QR", t) for t in range(NTB)]
            cnt = 0
            sidx = 0
            accb = [[Buf("accD%d" % i, Fb[i]), Buf("accP%d" % i, Fb[i])] for i in range(2)]
            acca = [[Fa[i][:, 0:512], Fa[i][:, 512:1024]] for i in range(2)]

            def views(h):
                i = h % 2
                return (W[0][:, i * 4096:(i + 1) * 4096], v3(W[0], 8192 + i * 4096, 32, 128),
                        W[1][:, i * 4096:(i + 1) * 4096], W[1][0:64, 8192 + i * 4096:8192 + (i + 1) * 4096])

            def loads(h):
                i = h % 2
                knv, vv, qnv, qrv = views(h)
                sch.dma("sp", knv.rearrange("p (t n) -> p t n", n=TB), KNv[:, :, h, :], reads=allk, writes=[knb[i]])
                sch.dma("sp", qnv.rearrange("p (t n) -> p t n", n=TB), QNv[:, :, h, :], reads=allq, writes=[qnb[i]])
                sch.dma("sp", qrv.rearrange("p (t n) -> p t n", n=TB), QRv[:, :, h, :], reads=allqr, writes=[qrb[i]])
                sch.dma("sp", vv, Vv[:, :, h * 128:(h + 1) * 128], reads=allv, writes=[vb[i]])
            loads(0)
            for h in range(16):
                i = h % 2
                knv, vv, qnv, qrv = views(h)
                if h + 1 < 16:
                    loads(h + 1)
                for qb in range(NTB):
                    ob = 3 + (cnt % 2)
                    smb = 5 + (cnt % 2)
                    cnt += 1
                    qs = slice(qb * TB, (qb + 1) * TB)
                    sbank = {}

                    def emit_s(kt):
                        nonlocal sidx
                        b = sidx % 3
                        sidx += 1
                        sbank[kt] = b
                        ks = slice(kt * 128, (kt + 1) * 128)
                        sch.op("pe", lambda e, b=b, ks=ks, knv=knv, qnv=qnv, qs=qs: e.matmul(PS[b][:, :], knv[:, ks], qnv[:, qs], start=True, stop=False),
                               reads=[knb[i], qnb[i]], writes=[PSb[b]])
                        sch.op("pe", lambda e, b=b, ks=ks, qrv=qrv, qs=qs: e.matmul(PS[b][:, :], krv[:, ks], qrv[:, qs], start=False, stop=True),
                               reads=[krb, qrb[i]], writes=[PSb[b]])
                    emit_s(0)
                    emit_s(1)
                    for kt in range(32):
                        if kt + 2 < 32:
                            emit_s(kt + 2)
                        b = sbank[kt]
                        pi = nxt("pt", 4)
                        sch.op("act", lambda e, b=b, pi=pi: e.activation(out=PTa[pi], in_=PS[b][:, :], func=AF.Exp, scale=scale),
                               reads=[PSb[b]], writes=[PTb[pi]])
                        sch.op("pe", lambda e, pi=pi, kt=kt, ob=ob, vv=vv: e.matmul(PS[ob][:, :], vv[:, kt, :], PTa[pi],
                                                                            start=(kt == 0), stop=(kt == 31)),
                               reads=[vb[i], PTb[pi]], writes=[PSb[ob]])
                        par = cnt % 2
                        which = 0
                        eng = "dve"
                        acc_ap = acca[par][which]
                        if kt == 0:
                            sch.op(eng, lambda e, pi=pi, acc_ap=acc_ap: e.tensor_copy(out=acc_ap, in_=PTa[pi]),
                                   reads=[PTb[pi]], writes=[accb[par][which]])
                        else:
                            sch.op(eng, lambda e, pi=pi, acc_ap=acc_ap: e.tensor_tensor(out=acc_ap, in0=acc_ap, in1=PTa[pi],
                                                                                       op=ALU.add),
                                   reads=[PTb[pi], accb[par][which]], writes=[accb[par][which]])
                    sch.op("pe", lambda e, smb=smb, a_=acca[cnt % 2][0]: e.matmul(PS[smb][:, :], ones32, a_, start=True, stop=True),
                           reads=[cb, accb[cnt % 2][0]], writes=[PSb[smb]])
                    ti = nxt("t", 6)
                    si = nxt("sb", 4)
                    sch.op("dve", lambda e, ti=ti, smb=smb: e.reciprocal(out=Ta[ti], in_=PS[smb][:, :]),
                           reads=[PSb[smb]], writes=[Tb_[ti]])
                    sch.op("dve", lambda e, ti=ti, si=si, ob=ob: e.tensor_tensor(out=SBa[si][:, 0:512], in0=PS[ob][:, :],
                                                                                in1=Ta[ti], op=ALU.mult),
                           reads=[PSb[ob], Tb_[ti]], writes=[SBb[si]])
                    sch.dma("sp", OB[qb][:, h, :], SBa[si][:, 0:512], reads=[SBb[si]], writes=[db("OB", qb)])

        def attn_diff(j, layer_idx):
            scale = float(128 ** -0.5)
            lambda_init = 0.8 - 0.6 * math.exp(-0.3 * layer_idx)
            sch.op("dve", lambda e: e.tensor_tensor(out=small[:, 0:1], in0=lam_c(j, 0), in1=lam_c(j, 1), op=ALU.mult),
                   reads=[cb], writes=[smallb])
            sch.op("dve", lambda e: e.tensor_tensor(out=small[:, 1:2], in0=lam_c(j, 2), in1=lam_c(j, 3), op=ALU.mult),
                   reads=[cb, smallb], writes=[smallb])
            sch.op("pe", lambda e: e.matmul(PS[7][:, 0:2], ones32, small[:, 0:2], start=True, stop=True),
                   reads=[cb, smallb], writes=[PSb[7]])
            sch.op("act", lambda e: e.activation(out=small[:, 2:4], in_=PS[7][:, 0:2], func=AF.Exp),
                   reads=[PSb[7], smallb], writes=[smallb])
            sch.op("dve", lambda e: e.tensor_tensor(out=small[:, 4:5], in0=small[:, 3:4], in1=small[:, 2:3], op=ALU.subtract),
                   reads=[smallb], writes=[smallb])
            sch.op("dve", lambda e: e.tensor_scalar(out=small[:, 4:5], in0=small[:, 4:5], scalar1=-float(lambda_init),
                                                    scalar2=None, op0=ALU.add), reads=[smallb], writes=[smallb])
            QDv = QDd.rearrange("t p c n -> p t c n")
            KDv = KDd.rearrange("t p c n -> p t c n")
            VDv = VDd.rearrange("t p f -> p t f")
            allq = [db("QD", t) for t in range(NTB)]
            allk = [db("KD", t) for t in range(NTB)]
            allv = [db("VD", t) for t in range(NTB)]
            kqb = [Buf("kq%d" % i, Wb[0]) for i in range(4)]
            vbb = [Buf("vd%d" % i, Wb[1]) for i in range(2)]
            sqb = [Buf("sq%d" % i, Ab[i]) for i in range(2)]
            deferred = []
            sidx = 0
            def vloads(h):
                hi = h % 2
                vv = v3(W[1], hi * 8192, 32, 256)
                sch.dma("sp", vv, VDv[:, :, h * 256:(h + 1) * 256], reads=allv, writes=[vbb[hi]])
                bt = v3(Fa[hi], 0, 6, TB)
                for idx in range(6):
                    delta = idx * 128 - 128
                    c0 = 639 - delta
                    src = bass.AP(ZT.tensor, h * 128 * RL + c0, [[RL - 1, 128], [1, TB]])
                    sch.dma("sp", bt[:, idx, :], src, reads=[db("ZT", h)], writes=[Fb[hi]])
                btf = Fa[hi][:, 0:3072]
                bhi = A[hi][:, 0:3072]
                blo = A[hi][:, 3072:6144]
                sch.op("dve", lambda e, btf=btf: e.tensor_scalar(out=btf, in0=btf, scalar1=float(1.0 / scale), scalar2=None,
                                                                 op0=ALU.mult), reads=[Fb[hi]], writes=[Fb[hi]])
                sch.op("dve", lambda e, btf=btf, bhi=bhi: e.tensor_copy(out=bhi, in_=btf), reads=[Fb[hi]], writes=[Ab[hi]])
                sch.op("dve", lambda e, btf=btf, bhi=bhi: e.tensor_tensor(out=btf, in0=btf, in1=bhi, op=ALU.subtract),
                       reads=[Fb[hi], Ab[hi]], writes=[Fb[hi]])
                sch.op("dve", lambda e, btf=btf, blo=blo: e.tensor_copy(out=blo, in_=btf), reads=[Fb[hi]], writes=[Ab[hi]])
            vloads(0)
            for h in range(8):
                hi = h % 2
                fi = hi
                vv = v3(W[1], hi * 8192, 32, 256)
                bt = v3(Fa[hi], 0, 6, TB)
                kv_ = []
                qv_ = []
                for gi in range(2):
                    g = 2 * h + gi
                    kvw = W[0][:, gi * 4096:(gi + 1) * 4096]
                    qvw = W[0][:, 8192 + gi * 4096:8192 + (gi + 1) * 4096]
                    sch.dma("sp", kvw.rearrange("p (t n) -> p t n", n=TB), KDv[:, :, g, :], reads=allk, writes=[kqb[gi]])
                    sch.dma("sp", qvw.rearrange("p (t n) -> p t n", n=TB), QDv[:, :, g, :], reads=allq, writes=[kqb[2 + gi]])
                    kv_.append(kvw)
                    qv_.append(qvw)
                if h + 1 < 8:
                    vloads(h + 1)
                for qb in range(NTB):
                    qs = slice(qb * TB, (qb + 1) * TB)
                    for gi in range(2):
                        obs = (2 + 2 * gi, 3 + 2 * gi)
                        smb = 6 + gi
                        sbank = {}

                        def emit_s(kt, gi=gi):
                            nonlocal sidx
                            b = sidx % 2
                            sidx += 1
                            sbank[kt] = b
                            ks = slice(kt * 128, (kt + 1) * 128)
                            delta_ = kt * 128 - qb * TB
                            near_ = -128 <= delta_ <= 512
                            sch.op("pe", lambda e, b=b, ks=ks, kk=kv_[gi], qq=qv_[gi], qs=qs, near_=near_: e.matmul(
                                PS[b][:, :], kk[:, ks], qq[:, qs], start=True, stop=(not near_)),
                                reads=[kqb[gi], kqb[2 + gi]], writes=[PSb[b]])
                            if near_:
                                ix = (delta_ + 128) // 128
                                for part in range(2):
                                    bsrc = A[hi][:, part * 3072 + ix * TB:part * 3072 + (ix + 1) * TB]
                                    sch.op("pe", lambda e, b=b, bsrc=bsrc, part=part: e.matmul(
                                        PS[b][:, :], ident, bsrc, start=False, stop=(part == 1)),
                                        reads=[cb, Ab[hi]], writes=[PSb[b]])
                        emit_s(0)
                        emit_s(1)
                        for kt in range(32):
                            b = sbank[kt]
                            pi = nxt("pt", 4)
                            delta = kt * 128 - qb * TB
                            if -128 <= delta <= 512:
                                sch.op("act", lambda e, b=b, pi=pi: e.activation(out=PTa[pi], in_=PS[b][:, :], func=AF.Exp,
                                                                                 scale=scale),
                                       reads=[PSb[b]], writes=[PTb[pi]])
                            else:
                                col = 16 + 2 * h + (0 if delta < 0 else 1)
                                sch.op("act", lambda e, b=b, pi=pi, col=col: e.activation(
                                    out=PTa[pi], in_=PS[b][:, :], func=AF.Exp, scale=scale, bias=small[:, col:col + 1]),
                                    reads=[PSb[b], smallb], writes=[PTb[pi]])
                            for dv in range(2):
                                sch.op("pe", lambda e, pi=pi, kt=kt, dv=dv, ob=obs[dv], vv=vv: e.matmul(
                                    PS[ob][:, :], vv[:, kt, dv * 128:(dv + 1) * 128], PTa[pi], start=(kt == 0), stop=(kt == 31)),
                                    reads=[vbb[hi], PTb[pi]], writes=[PSb[obs[dv]]])
                                if dv == 0 and kt + 2 < 32:
                                    emit_s(kt + 2)
                            sch.op("pe", lambda e, pi=pi, kt=kt, smb=smb: e.matmul(PS[smb][:, :], ones, PTa[pi],
                                                                                  start=(kt == 0), stop=(kt == 31)),
                                   reads=[cb, PTb[pi]], writes=[PSb[smb]])
                            if gi == 0 and kt == 6 and deferred:
                                deferred.pop(0)()
                    r0 = nxt("t", 6)
                    r1 = nxt("t", 6)
                    sch.op("dve", lambda e, r0=r0: e.reciprocal(out=Ta[r0], in_=PS[6][:, :]), reads=[PSb[6]], writes=[Tb_[r0]])
                    sch.op("dve", lambda e, r1=r1: e.reciprocal(out=Ta[r1], in_=PS[7][:, :]), reads=[PSb[7]], writes=[Tb_[r1]])
                    sch.op("dve", lambda e, r1=r1: e.tensor_scalar(out=Ta[r1], in0=Ta[r1], scalar1=small[:, 4:5], scalar2=None,
                                                                   op0=ALU.mult), reads=[Tb_[r1], smallb], writes=[Tb_[r1]])
                    ci = nxt("sf", 3)
                    cf = v3(SFa[ci], 0, 2, TB)
                    for dv in range(2):
                        t0 = nxt("t", 6)
                        sch.op("dve", lambda e, t0=t0, dv=dv, r0=r0: e.tensor_tensor(out=Ta[t0], in0=PS[2 + dv][:, :], in1=Ta[r0],
                                                                                     op=ALU.mult),
                               reads=[PSb[2 + dv], Tb_[r0]], writes=[Tb_[t0]])
                        sch.op("dve", lambda e, dv=dv, r1=r1, cf=cf: e.tensor_tensor(out=cf[:, dv, :], in0=PS[4 + dv][:, :],
                                                                                     in1=Ta[r1], op=ALU.mult),
                               reads=[PSb[4 + dv], Tb_[r1]], writes=[SFb[ci]])
                        sch.op("pool", lambda e, t0=t0, dv=dv, cf=cf: e.tensor_tensor(out=cf[:, dv, :], in0=cf[:, dv, :],
                                                                                      in1=Ta[t0], op=ALU.add),
                               reads=[SFb[ci], Tb_[t0]], writes=[SFb[ci]])
                    sqv = v3(A[hi], 6144, 2, TB)
                    for dv in range(2):
                        sch.op("pool", lambda e, dv=dv, cf=cf, sqv=sqv: e.tensor_tensor(out=sqv[:, dv, :], in0=cf[:, dv, :],
                                                                                        in1=cf[:, dv, :], op=ALU.mult),
                               reads=[SFb[ci]], writes=[sqb[hi]])

                    def part2(cf=cf, ci=ci, sqv=sqv, hi=hi, qb=qb, h=h):
                        for dv in range(2):
                            sch.op("pe", lambda e, dv=dv, sqv=sqv: e.matmul(PS[7][:, :], ones, sqv[:, dv, :], start=(dv == 0),
                                                                           stop=(dv == 1)),
                                   reads=[sqb[hi], cb], writes=[PSb[7]])
                        ti = nxt("t", 6)
                        rs = Ta[ti]
                        sch.op("dve", lambda e, rs=rs: e.tensor_scalar(out=rs, in0=PS[7][:, :], scalar1=1.0 / 256, scalar2=RMS_EPS,
                                                                      op0=ALU.mult, op1=ALU.add),
                               reads=[PSb[7]], writes=[Tb_[ti]])
                        sch.op("act", lambda e, rs=rs: e.activation(out=rs, in_=rs, func=AF.Sqrt), reads=[Tb_[ti]], writes=[Tb_[ti]])
                        sch.op("dve", lambda e, rs=rs: e.reciprocal(out=rs, in_=rs), reads=[Tb_[ti]], writes=[Tb_[ti]])
                        sch.op("dve", lambda e, rs=rs: e.tensor_scalar(out=rs, in0=rs, scalar1=float(1.0 - lambda_init), scalar2=None,
                                                                      op0=ALU.mult), reads=[Tb_[ti]], writes=[Tb_[ti]])
                        si = nxt("sb", 4)
                        ov = v3(SBa[si], 0, 2, TB)
                        for dv in range(2):
                            sch.op("dve", lambda e, dv=dv, cf=cf, ov=ov, rs=rs: e.scalar_tensor_tensor(
                                out=ov[:, dv, :], in0=cf[:, dv, :], scalar=sub_g(j, dv), in1=rs, op0=ALU.mult, op1=ALU.mult),
                                reads=[SFb[ci], Tb_[ti], cb], writes=[SBb[si]])
                        sch.dma("sp", OB[qb][:, 2 * h:2 * h + 2, :], ov, reads=[SBb[si]], writes=[db("OB", qb)])
                    deferred.append(part2)
            while deferred:
                deferred.pop(0)()

        XF_cur, XF_name = x_in, "x"
        for l in range(L):
            j = l // 2
            if l % 2 == 0:
                def epi_m1(sl, tb, gi, banks, j=j):
                    if gi < 2:
                        fi = nxt("sf", 3)
                        cf = v3(SFa[fi], 0, 4, TB)
                        for t_i, b in enumerate(banks):
                            copy_bank("act", cf[:, t_i, :], SFb[fi], b)
                        si = nxt("sb", 4)
                        ov = v3(SBa[si], 0, 4, TB)
                        gfn = (lambda c: qn_g(j, c)) if gi == 0 else (lambda c: kvn_g(j, c))
                        rms_epi(cf, SFb[fi], 4, TB, gfn, 1.0, banks[0], ov, SBb[si])
                        dst, nm = (CQN, "CQN") if gi == 0 else (CKVN, "CKVN")
                        sch.dma("sp", dst[tb], ov, reads=[SBb[si]], writes=[db(nm, tb)])
                    else:
                        si = nxt("sb", 4)
                        ov = SBa[si][0:64, 0:512]
                        rope_epi(banks[0], banks[1], tb, ov, SBb[si])
                        sch.dma("sp", KR[tb], ov, reads=[SBb[si]], writes=[db("KR", tb)])
                groups = [[(c * 128, 128) for c in range(0, 4)], [(c * 128, 128) for c in range(4, 8)],
                          [(1024, 64), (1088, 64)]]
                gemm([mla_win[j]], 16, 1152, groups, lambda tb: X0B[tb], "X0B", K16, epi_m1)

                m2s = {}

                def epi_m2(sl, tb, gi, banks):
                    h = gi
                    if h % 4 == 0:
                        m2s["n"] = nxt("sb", 4)
                        m2s["r"] = nxt("sb", 4)
                    si, s2 = m2s["n"], m2s["r"]
                    k = h % 4
                    copy_bank("act", SBa[si][:, k * 512:(k + 1) * 512], SBb[si], banks[0])
                    rope_epi(banks[1], banks[2], tb, SBa[s2][0:64, k * 512:(k + 1) * 512], SBb[s2])
                    if k == 3:
                        sch.dma("sp", QN[tb][:, h - 3:h + 1, :], v3(SBa[si], 0, 4, TB), reads=[SBb[si]], writes=[db("QN", tb)])
                        sch.dma("sp", QR[tb][:, h - 3:h + 1, :], v3(SBa[s2], 0, 4, TB)[0:64], reads=[SBb[s2]],
                                writes=[db("QR", tb)])
                groups = [[(h * 256, 128), (h * 256 + 128, 64), (h * 256 + 192, 64)] for h in range(16)]
                gemm([mla_wuq[j]], 4, 4096, groups, lambda tb: CQN[tb], "CQN", K4, epi_m2)
                gemm([mla_wuk[j]], 4, 2048, g4(2048), lambda tb: CKVN[tb], "CKVN", K4,
                     epi_store_bf16(lambda sl, tb, gi: (KN[tb][:, gi * 4:gi * 4 + 4, :], "KN")))
                gemm_v([mla_wuv[j]], 4, 2048, lambda tb: CKVN[tb], "CKVN", V, "V", lambda sl: 0)
                attn_mla()
                wo = mla_wo[j]
            else:
                def dst_qk(sl, tb, gi):
                    t0 = sl * 8 + gi * 4
                    if t0 < 16:
                        return QDd[tb][:, t0:t0 + 4, :], "QD"
                    return KDd[tb][:, t0 - 16:t0 - 12, :], "KD"
                gemm([diff_wqk[j][s_] for s_ in range(4)], 16, 1024, g4(1024), lambda tb: X0B[tb], "X0B", K16,
                     epi_store_bf16(dst_qk))
                gemm_v([diff_wv[j][0], diff_wv[j][1]], 16, 1024, lambda tb: X0B[tb], "X0B", VDd, "VD", lambda sl: sl * 1024)
                attn_diff(j, l)
                wo = diff_wo[j]
            gemm([wo[0], wo[1]], 16, 1024, g4(1024), lambda tb: OB[tb], "OB", K16, epi_resid(XF_cur, XF_name, 8, 4))

            def epi_f1(sl, tb, gi, banks):
                si = nxt("sb", 4)
                hv = v3(SBa[si], 0, 2, TB)
                for t in range(2):
                    ti = nxt("t", 6)
                    sch.op("act", lambda e, ti=ti, b=banks[t]: e.activation(out=Ta[ti], in_=PS[b][:, :], func=AF.Silu),
                           reads=[PSb[banks[t]]], writes=[Tb_[ti]])
                    sch.op("dve", lambda e, ti=ti, t=t, hv=hv, b=banks[2 + t]: e.tensor_tensor(
                        out=hv[:, t, :], in0=Ta[ti], in1=PS[b][:, :], op=ALU.mult),
                        reads=[Tb_[ti], PSb[banks[2 + t]]], writes=[SBb[si]])
                c0 = sl * 4 + gi * 2
                sch.dma("sp", H[tb][:, c0:c0 + 2, :], hv, reads=[SBb[si]], writes=[db("H", tb)])
            groups = [[(gi * 512 + k * 128, 128) for k in range(4)] for gi in range(2)]
            gemm([ffn_win[l][s_] for s_ in range(11)], 16, 1024, groups, lambda tb: X1B[tb], "X1B", K16, epi_f1,
                 pre=ln_pre(l, 0, X1F, "X1F", X1B, "X1B"))
            kg44 = [list(range(g * 11, g * 11 + 11)) for g in range(4)]
            gemm([ffn_wout[l][s_] for s_ in range(4)], 44, 512, g4(512), lambda tb: H[tb], "H", kg44,
                 epi_resid(X1F, "X1F", 4, 4), ksplit=True)
            last = (l == L - 1)
            XNd, XNn = (out_d, "out") if last else (X0F, "X0F")
            pre = ln_pre(l, 1, X2F, "X2F", X2B, "X2B")
            pend = Pend()
            for sl in range(2):
                if sl == 0:
                    spare = 1 - rr["w"]
                wi = nxt("w", 2)
                gv = v3(W[wi], 0, 16, 1024)
                pv = v3(W[wi], 16384, 2, 1024)
                for c0_ in range(0, 16, 4):
                    sch.dma("pool", gv[:, c0_:c0_ + 4, :], ple_wg[l][sl][:, c0_:c0_ + 4, :], writes=[Wb[wi]])
                sch.dma("pool", pv, ple_wp[l][sl], writes=[Wb[wi]])
                for tb in range(NTB):
                    if sl == 0 and tb == 0:
                        pend.flush()
                        pre(0, spare)
                    ai = load_a(X2B[tb], 16, db("X2B", tb))
                    av = v3(A[ai], 0, 16, TB)
                    pi_ = nxt("sb", 4)
                    ptv = v3(SBa[pi_], 0, 2, TB)
                    sch.dma("pool", ptv, p_in[l][tb], writes=[SBb[pi_]])
                    for gi in range(4):
                        base = 4 * (self.itc % 2)
                        self.itc += 1
                        banks = [base, base + 1, base + 2, base + 3]
                        for t in range(2):
                            co = gi * 256 + t * 128
                            for kc in range(16):
                                sch.op("pe", lambda e, b=banks[t], co=co, kc=kc, gv=gv, av=av: e.matmul(
                                    PS[b][:, :], gv[:, kc, co:co + 128], av[:, kc, :], start=(kc == 0), stop=(kc == 15)),
                                    reads=[Wb[wi], Ab[ai]], writes=[PSb[banks[t]]])
                            for kc in range(2):
                                sch.op("pe", lambda e, b=banks[2 + t], co=co, kc=kc, pv=pv, ptv=ptv: e.matmul(
                                    PS[b][:, :], pv[:, kc, co:co + 128], ptv[:, kc, :], start=(kc == 0), stop=(kc == 1)),
                                    reads=[Wb[wi], SBb[pi_]], writes=[PSb[banks[2 + t]]])

                        def epi(sl=sl, tb=tb, gi=gi, banks=banks):
                            c0 = sl * 8 + gi * 2
                            xi = nxt("sf", 3)
                            xv = v3(SFa[xi], 0, 2, TB)
                            sch.dma("sp", xv, X2F[tb][:, c0:c0 + 2, :], reads=[db("X2F", tb)], writes=[SFb[xi]])
                            si = nxt("sb", 4)
                            bv = v3(SBa[si], 0, 2, TB)
                            tis = []
                            for t in range(2):
                                ti = nxt("t", 6)
                                tis.append(ti)
                                sch.op("act", lambda e, ti=ti, b=banks[t]: e.activation(out=Ta[ti], in_=PS[b][:, :],
                                                                                        func=AF.Sigmoid),
                                       reads=[PSb[banks[t]]], writes=[Tb_[ti]])
                                sch.op("dve", lambda e, ti=ti, b=banks[2 + t]: e.tensor_tensor(out=Ta[ti], in0=Ta[ti],
                                                                                               in1=PS[b][:, :], op=ALU.mult),
                                       reads=[Tb_[ti], PSb[banks[2 + t]]], writes=[Tb_[ti]])
                            for t in range(2):
                                sch.op("dve", lambda e, t=t, ti=tis[t], xv=xv: e.tensor_tensor(out=xv[:, t, :], in0=xv[:, t, :],
                                                                                               in1=Ta[ti], op=ALU.add),
                                       reads=[Tb_[tis[t]], SFb[xi]], writes=[SFb[xi]])
                            sch.dma("sp", XNd[tb][:, c0:c0 + 2, :], xv, reads=[SFb[xi]], writes=[db(XNn, tb)])
                            if not last:
                                sch.op("act", lambda e, xv=xv, bv=bv: e.activation(out=bv, in_=xv, func=AF.Copy),
                                       reads=[SFb[xi]], writes=[SBb[si]])
                                sch.dma("sp", X0B[tb][:, c0:c0 + 2, :], bv, reads=[SBb[si]], writes=[db("X0B", tb)])
                        pend.push(epi)
                    if sl == 0 and tb + 1 < NTB:
                        pend.flush()
                        pre(tb + 1, spare)
            pend.flush()
            XF_cur, XF_name = X0F, "X0F"

        sch.finalize()
        with nc.Block() as block:
            @block.tensor
            def _(e):
                sch.emit(e, "pe")

            @block.scalar
            def _(e):
                sch.emit(e, "act")

            @block.vector
            def _(e):
                sch.emit(e, "dve")

            @block.gpsimd
            def _(e):
                sch.emit(e, "pool")

            @block.sync
            def _(e):
                sch.emit(e, "sp")
        self.stack.close()
        return self


def _pack(w, ncols):
    k, n = w.shape
    return np.ascontiguousarray(w.reshape(k // 128, 128, n // ncols, ncols).transpose(2, 1, 0, 3))


def _fm(a, c):
    return np.ascontiguousarray(a.reshape(NTB, TB, c, 128).transpose(0, 3, 2, 1))


def _prep_shared(inp):
    f = lambda a: np.asarray(a, dtype=np.float32)
    sw = np.concatenate([np.arange(32, 64), np.arange(0, 32)])
    o = {}
    w = f(inp["mla_w_in"])
    o["mla_win"] = np.stack([_pack(np.concatenate([w[j], w[j][:, 1024:1088][:, sw]], 1), 1152)[0] for j in range(2)])
    w = f(inp["mla_w_uq"]).reshape(2, 512, 16, 192)
    wq = np.concatenate([w[..., :128], w[..., 128:192], w[..., 128:192][..., sw]], -1).reshape(2, 512, 4096)
    o["mla_wuq"] = np.stack([_pack(wq[j], 4096)[0] for j in range(2)])
    w = f(inp["mla_w_ukv"]).reshape(2, 512, 16, 256)
    o["mla_wuk"] = np.stack([_pack(np.ascontiguousarray(w[j][..., :128]).reshape(512, 2048), 2048)[0] for j in range(2)])
    o["mla_wuv"] = np.stack([_pack(np.ascontiguousarray(w[j][..., 128:]).reshape(512, 2048), 2048)[0] for j in range(2)])
    o["mla_wo"] = np.stack([_pack(f(inp["mla_w_o"])[j], 1024) for j in range(2)])
    w = f(inp["diff_w_in"])
    o["diff_wqk"] = np.stack([_pack(np.ascontiguousarray(w[j][:, :4096]), 1024) for j in range(2)])
    o["diff_wv"] = np.stack([_pack(np.ascontiguousarray(w[j][:, 4096:]), 1024) for j in range(2)])
    o["diff_wo"] = np.stack([_pack(f(inp["diff_w_o"])[j], 1024) for j in range(2)])
    w = f(inp["ffn_w_in"])
    lst = []
    for l in range(DEPTH):
        g = w[l][:, :DFF].reshape(D, 11, 2, 1, 256)
        u = w[l][:, DFF:].reshape(D, 11, 2, 1, 256)
        lst.append(_pack(np.concatenate([g, u], 3).reshape(D, 11 * 1024), 1024))
    o["ffn_win"] = np.stack(lst)
    o["ffn_wout"] = np.stack([_pack(f(inp["ffn_w_out"])[l], 512) for l in range(DEPTH)])
    o["ple_wg"] = np.stack([_pack(f(inp["ple_w_gate"])[l], 1024) for l in range(DEPTH)])
    o["ple_wp"] = np.stack([_pack(f(inp["ple_w_proj"])[l], 1024) for l in range(DEPTH)])
    vecs = np.zeros((128, 284), np.float32)
    vecs[:, 0:128] = f(inp["ln_g"]).reshape(4, 2, 16, 128).transpose(3, 0, 1, 2).reshape(128, 128)
    vecs[:, 128:256] = f(inp["ln_b"]).reshape(4, 2, 16, 128).transpose(3, 0, 1, 2).reshape(128, 128)
    vecs[:, 256:264] = f(inp["mla_q_norm"]).reshape(2, 4, 128).transpose(2, 0, 1).reshape(128, 8)
    vecs[:, 264:272] = f(inp["mla_kv_norm"]).reshape(2, 4, 128).transpose(2, 0, 1).reshape(128, 8)
    vecs[:, 272:276] = f(inp["diff_sub_norm"]).reshape(2, 2, 128).transpose(2, 0, 1).reshape(128, 4)
    vecs[:, 276:284] = f(inp["diff_lambda"]).transpose(2, 0, 1).reshape(128, 8)
    o["vecs"] = vecs
    o["relb"] = np.ascontiguousarray(f(inp["rel_bias"]))
    rope, onehot = _const_tables()
    o["rope"] = rope
    o["onehot"] = onehot
    o["ident"] = np.eye(128, dtype=np.float32)
    return o


_NC_CACHE = {}


def _get_nc(debug=None, nlayers=DEPTH, stop_after=None):
    key = (tuple(debug or ()), nlayers, stop_after)
    if key not in _NC_CACHE:
        kb = KB(debug=debug, nlayers=nlayers)
        kb.stop_after = stop_after
        kb.build()
        _NC_CACHE[key] = kb.nc
    return _NC_CACHE[key]


def kernel(**inputs):
    shared = _prep_shared(inputs)
    x = np.asarray(inputs["x"], dtype=np.float32)
    p = np.asarray(inputs["p"], dtype=np.float32)
    in_maps = []
    for b in range(NCORES):
        m = dict(shared)
        m["x"] = _fm(x[b], 16)
        m["p"] = np.stack([_fm(p[l, b], 2) for l in range(DEPTH)])
        in_maps.append(m)
    nc = _get_nc()
    res = run_bass_kernel_spmd(nc, in_maps, core_ids=list(range(NCORES)))
    out = np.empty((NCORES, S, D), np.float32)
    for b in range(NCORES):
        o = np.asarray(res.results[b]["out"]).reshape(NTB, 128, 16, TB)
        out[b] = o.transpose(0, 3, 2, 1).reshape(S, D)
    return out
```

```python
import math
from contextlib import ExitStack
import numpy as np
import concourse.bass as bass
import concourse.mybir as mybir
from concourse.bass_utils import run_bass_kernel_spmd

F32 = mybir.dt.float32
BF16 = mybir.dt.bfloat16
AF = mybir.ActivationFunctionType
ALU = mybir.AluOpType

D = 2048
S = 4096
NTB = 8
TB = 512
DEPTH = 4
DFF = 5632
ALPHA = (2 * DEPTH) ** 0.25
LN_EPS = 1e-5
RMS_EPS = 1e-6
NCORES = 8
RL = 1280


class Buf:
    __slots__ = ("name", "w", "r", "parent", "kids", "wd")

    def __init__(self, name, parent=None):
        self.name = name
        self.w = None
        self.wd = {}
        self.r = {}
        self.parent = parent
        self.kids = []
        if parent is not None:
            parent.kids.append(self)


class Op:
    __slots__ = ("eng", "fn", "waits", "observed", "idx", "val", "dma_sem")

    def __init__(self, eng, fn):
        self.eng = eng
        self.fn = fn
        self.waits = []
        self.observed = False
        self.idx = 0
        self.val = 0
        self.dma_sem = None


ENGS = ["pe", "act", "dve", "pool", "sp"]


class Sched:
    def __init__(self, nc, stack, rings=None):
        rings = rings or {"sp": 28, "pool": 20}
        self.nc = nc
        self.ops = {e: [] for e in ENGS}
        self.esem = {e: stack.enter_context(nc.semaphore("es_" + e)) for e in ENGS}
        self.rings = {q: [stack.enter_context(nc.semaphore("rg_%s%d" % (q, i))) for i in range(n)]
                      for q, n in rings.items()}
        self.ring_use = {q: [0] * n for q, n in rings.items()}
        self.ring_pos = {q: 0 for q in rings}
        self.waited = {e: {} for e in ENGS}

    @staticmethod
    def _key(ev):
        return ev[1]

    @staticmethod
    def _ord(ev):
        return ev[2].idx + 1 if ev[0] == "e" else ev[2]

    def _collect(self, reads, writes):
        evs = []

        def wr(b):
            if b.w:
                evs.append(b.w)
            evs.extend(b.wd.values())
        for b in reads:
            wr(b)
            if b.parent is not None:
                wr(b.parent)
            for k in b.kids:
                wr(k)
        for b in writes:
            wr(b)
            evs.extend(b.r.values())
            if b.parent is not None:
                p = b.parent
                wr(p)
                evs.extend(p.r.values())
            for k in b.kids:
                wr(k)
                evs.extend(k.r.values())
        return evs

    def _waits(self, eng, evs):
        wd = self.waited[eng]
        best = {}
        for ev in evs:
            k = self._key(ev)
            if ev[0] == "e" and k == eng and eng in ("pe", "sp"):
                continue
            o = self._ord(ev)
            if wd.get(k, 0) >= o:
                continue
            if k not in best or self._ord(best[k]) < o:
                best[k] = ev
        out = []
        for k, ev in best.items():
            wd[k] = self._ord(ev)
            if ev[0] == "e":
                ev[2].observed = True
            out.append(ev)
        return out

    def _record(self, ev, reads, writes):
        k = self._key(ev)
        for b in reads:
            b.r[k] = ev
        for b in writes:
            if ev[0] == "d":
                b.wd[k] = ev
            else:
                b.w = ev
                b.wd = {}
            b.r = {}
            for kd in b.kids:
                kd.w = None
                kd.wd = {}
                kd.r = {}

    def op(self, eng, fn, reads=(), writes=()):
        o = Op(eng, fn)
        o.waits = self._waits(eng, self._collect(reads, writes))
        o.idx = len(self.ops[eng])
        self.ops[eng].append(o)
        self._record(("e", eng, o), reads, writes)
        return o

    def dma(self, q, out_ap, in_ap, reads=(), writes=()):
        n = len(self.rings[q])
        slot = self.ring_pos[q]
        self.ring_pos[q] = (slot + 1) % n
        prev = 16 * self.ring_use[q][slot]
        self.ring_use[q][slot] += 1
        evs = self._collect(reads, writes)
        if prev:
            evs.append(("d", (q, slot), prev))
        o = Op(q, lambda e: e.dma_start(out=out_ap, in_=in_ap))
        o.waits = self._waits(q, evs)
        o.idx = len(self.ops[q])
        o.dma_sem = self.rings[q][slot]
        self.ops[q].append(o)
        self._record(("d", (q, slot), prev + 16), reads, writes)
        return o

    def finalize(self):
        for e in ENGS:
            c = 0
            for o in self.ops[e]:
                if o.observed and o.dma_sem is None:
                    c += 1
                    o.val = c

    def emit(self, engobj, eng):
        for o in self.ops[eng]:
            for ev in o.waits:
                if ev[0] == "e":
                    engobj.wait_ge(self.esem[ev[1]], ev[2].val)
                else:
                    q, slot = ev[1]
                    engobj.wait_ge(self.rings[q][slot], ev[2])
            ins = o.fn(engobj)
            if o.dma_sem is not None:
                ins.then_inc(o.dma_sem, 16)
            elif o.observed:
                ins.then_inc(self.esem[eng], 1)
        if eng == "sp":
            for q, sems in self.rings.items():
                for slot, sem in enumerate(sems):
                    u = self.ring_use[q][slot]
                    if u:
                        engobj.wait_ge(sem, 16 * u)


def _t5_bucket_np(rel):
    nb = 16
    max_exact = 8
    ret = (rel > 0).astype(np.int32) * nb
    n = np.abs(rel)
    nf = np.maximum(n, 1).astype(np.float32)
    large = max_exact + (np.log(nf / np.float32(max_exact)) / np.float32(math.log(128 / max_exact))
                         * np.float32(nb - max_exact)).astype(np.int32)
    large = np.minimum(large, nb - 1)
    return ret + np.where(n < max_exact, n, large)


def _const_tables():
    pos = np.arange(S, dtype=np.float32)
    inv = (1.0 / (np.float32(10000.0) ** (np.arange(0, 64, 2, dtype=np.float32) / np.float32(64)))).astype(np.float32)
    ang = pos[:, None] * inv[None, :]
    cos = np.cos(ang).astype(np.float32).T
    sin = np.sin(ang).astype(np.float32).T
    cos2 = np.concatenate([cos, cos], 0)
    sins = np.concatenate([-sin, sin], 0)
    rope = np.stack([cos2, sins], 1)
    rope = rope.reshape(64, 2, NTB, TB).transpose(2, 0, 1, 3)
    i = np.arange(RL)
    rel = (RL - 1 - i) - 640
    bk = _t5_bucket_np(rel)
    onehot = np.zeros((32, RL), np.float32)
    onehot[bk, i] = 1.0
    return np.ascontiguousarray(rope), onehot


class KB:
    def __init__(self, debug=None, nlayers=DEPTH):
        self.debug = debug or ()
        self.nlayers = nlayers
        self.nc = bass.Bass("TRN2", target_bir_lowering=False)
        self.stack = ExitStack()
        self.dbufs = {}
        self.dt = {}
        self.stop_after = None

    def din(self, name, shape):
        t = self.nc.dram_tensor(name, list(shape), F32, kind="ExternalInput")
        self.dt[name] = t.ap()
        return self.dt[name]

    def dscr(self, name, shape, dtype, out=False):
        kind = "ExternalOutput" if (out or name in self.debug) else "Internal"
        t = self.nc.dram_tensor(name, list(shape), dtype, kind=kind)
        self.dt[name] = t.ap()
        return self.dt[name]

    def db(self, name, i=0):
        k = (name, i)
        if k not in self.dbufs:
            self.dbufs[k] = Buf("%s_%d" % (name, i))
        return self.dbufs[k]

    def sb(self, name, shape, dtype):
        return self.stack.enter_context(self.nc.sbuf_tensor(name, list(shape), dtype))

    def build(self):
        nc = self.nc
        st = self.stack
        L = self.nlayers
        x_in = self.din("x", [NTB, 128, 16, TB])
        p_in = self.din("p", [DEPTH, NTB, 128, 2, TB])
        mla_win = self.din("mla_win", [2, 128, 16, 1152])
        mla_wuq = self.din("mla_wuq", [2, 128, 4, 4096])
        mla_wuk = self.din("mla_wuk", [2, 128, 4, 2048])
        mla_wuv = self.din("mla_wuv", [2, 128, 4, 2048])
        mla_wo = self.din("mla_wo", [2, 2, 128, 16, 1024])
        diff_wqk = self.din("diff_wqk", [2, 4, 128, 16, 1024])
        diff_wv = self.din("diff_wv", [2, 2, 128, 16, 1024])
        diff_wo = self.din("diff_wo", [2, 2, 128, 16, 1024])
        ffn_win = self.din("ffn_win", [DEPTH, 11, 128, 16, 1024])
        ffn_wout = self.din("ffn_wout", [DEPTH, 4, 128, 44, 512])
        ple_wg = self.din("ple_wg", [DEPTH, 2, 128, 16, 1024])
        ple_wp = self.din("ple_wp", [DEPTH, 2, 128, 2, 1024])
        vecs_in = self.din("vecs", [128, 284])
        relb_in = self.din("relb", [32, 8])
        rope_in = self.din("rope", [NTB, 64, 2, TB])
        onehot_in = self.din("onehot", [32, RL])
        ident_in = self.din("ident", [128, 128])

        out_d = self.dscr("out", [NTB, 128, 16, TB], F32, out=True)
        X0F = self.dscr("X0F", [NTB, 128, 16, TB], F32)
        X1F = self.dscr("X1F", [NTB, 128, 16, TB], F32)
        X2F = self.dscr("X2F", [NTB, 128, 16, TB], F32)
        X0B = self.dscr("X0B", [NTB, 128, 16, TB], BF16)
        X1B = self.dscr("X1B", [NTB, 128, 16, TB], BF16)
        X2B = self.dscr("X2B", [NTB, 128, 16, TB], BF16)
        Z = self.dscr("Z", [NTB, 128, 16, TB], F32)
        OB = self.dscr("OB", [NTB, 128, 16, TB], BF16)
        H = self.dscr("H", [NTB, 128, 44, TB], BF16)
        CQN = self.dscr("CQN", [NTB, 128, 4, TB], BF16)
        CKVN = self.dscr("CKVN", [NTB, 128, 4, TB], BF16)
        KR = self.dscr("KR", [NTB, 64, TB], BF16)
        QN = self.dscr("QN", [NTB, 128, 16, TB], BF16)
        QR = self.dscr("QR", [NTB, 64, 16, TB], BF16)
        KN = self.dscr("KN", [NTB, 128, 16, TB], BF16)
        V = self.dscr("V", [32, 128, 2048], BF16)
        RD = self.dscr("RD", [8, RL], F32)
        ZT = self.dscr("ZT", [8, 128, RL], F32)

        Wt = [self.sb("W%d" % i, [128, 18432], BF16) for i in range(2)]
        At = [self.sb("A%d" % i, [128, 8192], BF16) for i in range(2)]
        Ft = [self.sb("F%d" % i, [128, 4096], F32) for i in range(2)]
        SBt = [self.sb("SB%d" % i, [128, 2048], BF16) for i in range(4)]
        SFt = [self.sb("SF%d" % i, [128, 2048], F32) for i in range(3)]
        Tt = [self.sb("T%d" % i, [128, 512], F32) for i in range(6)]
        PTt = [self.sb("PT%d" % i, [128, 512], BF16) for i in range(4)]
        STt = [self.sb("ST%d" % i, [128, 128], F32) for i in range(16)]
        RTt = [self.sb("RT%d" % i, [64, 1024], F32) for i in range(1)]
        ones_t = self.sb("ones", [128, 128], BF16)
        ones32_t = self.sb("ones32", [128, 128], F32)
        ident_t = self.sb("ident_sb", [128, 128], BF16)
        vecs_t = self.sb("vecs_sb", [128, 284], F32)
        small_t = self.sb("small", [128, 64], F32)
        relb_t = self.sb("relb_sb", [32, 8], F32)
        PSt = [st.enter_context(nc.psum_tensor("ps%d" % i, [128, 512], F32)) for i in range(8)]

        sch = Sched(nc, st)
        self.sch = sch
        Wb = [Buf("W0"), Buf("W1")]
        Ab = [Buf("A0"), Buf("A1")]
        Fb = [Buf("F0"), Buf("F1")]
        SBb = [Buf("SB%d" % i) for i in range(4)]
        SFb = [Buf("SF%d" % i) for i in range(3)]
        Tb_ = [Buf("T%d" % i) for i in range(6)]
        PTb = [Buf("PT%d" % i) for i in range(4)]
        STb = [Buf("ST%d" % i) for i in range(16)]
        RTb = [Buf("RT%d" % i) for i in range(1)]
        PSb = [Buf("PS%d" % i) for i in range(8)]
        cb = Buf("consts")
        smallb = Buf("small")
        W = [t[:, :] for t in Wt]
        A = [t[:, :] for t in At]
        Fa = [t[:, :] for t in Ft]
        SBa = [t[:, :] for t in SBt]
        SFa = [t[:, :] for t in SFt]
        Ta = [t[:, :] for t in Tt]
        PTa = [t[:, :] for t in PTt]
        STa = [t[:, :] for t in STt]
        RTa = [t[:, :] for t in RTt]
        PS = [t[:, :] for t in PSt]
        ones = ones_t[:, :]
        ones32 = ones32_t[:, :]
        vecs = vecs_t[:, :]
        small = small_t[:, :]

        rr = {"fqa": 0, "fqb": 0, "sb": 0, "sf": 0, "t": 0, "pt": 0, "st": 0, "w": 0, "a": 0, "f": 0, "rt": 0}

        def nxt(kind, n):
            i = rr[kind]
            rr[kind] = (i + 1) % n
            return i

        def v3(ap, off, c, n):
            return ap[:, off:off + c * n].rearrange("p (c n) -> p c n", n=n)

        def ln_g(l, s, c):
            return vecs[:, (l * 2 + s) * 16 + c:(l * 2 + s) * 16 + c + 1]

        def ln_b(l, s, c):
            return vecs[:, 128 + (l * 2 + s) * 16 + c:128 + (l * 2 + s) * 16 + c + 1]

        def qn_g(j, c):
            return vecs[:, 256 + j * 4 + c:256 + j * 4 + c + 1]

        def kvn_g(j, c):
            return vecs[:, 264 + j * 4 + c:264 + j * 4 + c + 1]

        def sub_g(j, c):
            return vecs[:, 272 + j * 2 + c:272 + j * 2 + c + 1]

        def lam_c(j, c):
            return vecs[:, 276 + j * 4 + c:276 + j * 4 + c + 1]

        sch.op("dve", lambda e: e.memset(ones, 1.0), writes=[cb])
        sch.op("dve", lambda e: e.memset(ones32, 1.0), writes=[cb])
        sch.dma("sp", vecs, vecs_in[:, :], writes=[cb])
        sch.dma("pool", ident_t[:, :], ident_in[:, :], writes=[cb])
        ident = ident_t[:, :]
        sch.dma("sp", relb_t[:, :], relb_in[:, :], writes=[cb])
        oh_ap = Ft[0][0:32, 0:RL]
        rrow_ap = Ft[1][0:8, 0:RL]
        sch.dma("sp", oh_ap, onehot_in[:, :], writes=[Fb[0]])
        for tb in range(NTB):
            sch.dma("pool", X0B[tb], x_in[tb], writes=[self.db("X0B", tb)])

        has_diff = L >= 2
        if has_diff:
            for c0 in range(0, RL, 512):
                w_ = min(512, RL - c0)
                sch.op("pe", lambda e, c0=c0, w_=w_: e.matmul(PS[0][0:8, 0:w_], relb_t[:, :], oh_ap[:, c0:c0 + w_],
                                                              start=True, stop=True),
                       reads=[cb, Fb[0]], writes=[PSb[0]])
                sch.op("dve", lambda e, c0=c0, w_=w_: e.tensor_copy(out=rrow_ap[:, c0:c0 + w_], in_=PS[0][0:8, 0:w_]),
                       reads=[PSb[0]], writes=[Fb[1]])
            sch.dma("sp", RD[:, :], rrow_ap, reads=[Fb[1]], writes=[self.db("RD")])
            for h in range(8):
                src = bass.AP(RD.tensor, h * RL, [[0, 128], [1, RL]])
                sch.dma("sp", ZT[h], src, reads=[self.db("RD")], writes=[self.db("ZT", h)])
            for h in range(8):
                for s_, row in ((0, 15), (1, 31)):
                    src = bass.AP(relb_in.tensor, row * 8 + h, [[0, 128], [1, 1]])
                    sch.dma("sp", small[:, 16 + 2 * h + s_:17 + 2 * h + s_], src, writes=[smallb])

        db = self.db
        PPd = self.dscr("PP", [NTB, 128, 16, TB], F32)
        QDd = self.dscr("QD", [NTB, 128, 16, TB], BF16)
        KDd = self.dscr("KD", [NTB, 128, 16, TB], BF16)
        VDd = self.dscr("VD", [32, 128, 2048], BF16)
        self.itc = 0
        self.rt_cache = {}

        def load_w(src_ap, kc, ncols):
            wi = nxt("w", 2)
            dst = v3(W[wi], 0, kc, ncols)
            step = max(1, 4096 // ncols)
            for c0 in range(0, kc, step):
                c1 = min(kc, c0 + step)
                sch.dma("pool", dst[:, c0:c1, :], src_ap[:, c0:c1, :], writes=[Wb[wi]])
            return wi

        def load_a(src_ap, kc, srcb, q="sp"):
            ai = nxt("a", 2)
            sch.dma(q, v3(A[ai], 0, kc, TB), src_ap, reads=[srcb], writes=[Ab[ai]])
            return ai

        class Pend:
            def __init__(self):
                self.p = None

            def push(self, fn):
                if self.p is not None:
                    self.p()
                self.p = fn

            def flush(self):
                self.push(None)

        def gemm(wslabs, kc_total, ncols, groups, a_src, a_name, kgroups, epi, aq="sp", pre=None, ksplit=False):
            pend = Pend()
            for sl, wsrc in enumerate(wslabs):
                if sl == 0 and pre is not None:
                    spare = 1 - rr["w"]
                if ksplit:
                    half = kc_total // 2
                    wvs = []
                    for hi_ in range(2):
                        dstv = v3(W[hi_], 0, half, ncols)
                        step = max(1, 4096 // ncols)
                        for c0 in range(0, half, step):
                            c1 = min(half, c0 + step)
                            sch.dma("pool", dstv[:, c0:c1, :], wsrc[:, hi_ * half + c0:hi_ * half + c1, :], writes=[Wb[hi_]])
                        wvs.append(dstv)
                else:
                    wi = load_w(wsrc, kc_total, ncols)
                    wv = v3(W[wi], 0, kc_total, ncols)
                for tb in range(NTB):
                    if sl == 0 and pre is not None and tb == 0:
                        pend.flush()
                        pre(0, spare)
                    single = len(kgroups) == 1
                    if single:
                        ai0 = load_a(a_src(tb), kc_total, db(a_name, tb), q=aq)
                    for gi, tiles in enumerate(groups):
                        base = 4 * (self.itc % 2)
                        self.itc += 1
                        banks = [base + i for i in range(len(tiles))]
                        for kgi, kcs in enumerate(kgroups):
                            if single:
                                ai = ai0
                                av = v3(A[ai], 0, kc_total, TB)
                            else:
                                ai = load_a(a_src(tb)[:, kcs[0]:kcs[-1] + 1, :], len(kcs), db(a_name, tb), q=aq)
                                av = v3(A[ai], 0, len(kcs), TB)
                            for t_i, (co, wd) in enumerate(tiles):
                                for i, kc in enumerate(kcs):
                                    ia = kc if single else i
                                    if ksplit:
                                        wi = kc // half
                                        wv = wvs[wi]
                                        kcw = kc % half
                                    else:
                                        kcw = kc
                                    sch.op("pe", lambda e, b=banks[t_i], co=co, wd=wd, kc=kcw, ia=ia, wv=wv, av=av,
                                           st_=(kgi == 0 and i == 0),
                                           sp_=(kgi == len(kgroups) - 1 and i == len(kcs) - 1):
                                           e.matmul(PS[b][0:wd, :], wv[:, kc, co:co + wd], av[:, ia, :], start=st_, stop=sp_),
                                           reads=[Wb[wi], Ab[ai]], writes=[PSb[banks[t_i]]])
                        pend.push(lambda sl=sl, tb=tb, gi=gi, banks=banks: epi(sl, tb, gi, banks))
                    if sl == 0 and pre is not None and tb + 1 < NTB:
                        pend.flush()
                        pre(tb + 1, spare)
            pend.flush()

        def copy_bank(eng, out_ap, outb, bank, rows=128):
            if eng == "act":
                sch.op("act", lambda e: e.activation(out=out_ap, in_=PS[bank][0:rows, :], func=AF.Copy),
                       reads=[PSb[bank]], writes=[outb])
            else:
                sch.op("dve", lambda e: e.tensor_copy(out=out_ap, in_=PS[bank][0:rows, :]),
                       reads=[PSb[bank]], writes=[outb])

        def epi_store_bf16(dst_fn):
            def epi(sl, tb, gi, banks):
                si = nxt("sb", 4)
                sv = v3(SBa[si], 0, len(banks), TB)
                for t_i, b in enumerate(banks):
                    copy_bank("act" if t_i % 2 == 0 else "dve", sv[:, t_i, :], SBb[si], b)
                dap, dname = dst_fn(sl, tb, gi)
                sch.dma("sp", dap, sv, reads=[SBb[si]], writes=[db(dname, tb)])
            return epi

        def rms_epi(cf_ap, cfb, nt, n, gfn, extra, stat_bank, out_ap, outb):
            for j in range(nt):
                pi = nxt("pt", 4)
                sqv = PTa[pi][:, 0:n]
                sch.op("pool", lambda e, j=j, sqv=sqv: e.tensor_tensor(out=sqv, in0=cf_ap[:, j, :], in1=cf_ap[:, j, :],
                                                                      op=ALU.mult), reads=[cfb], writes=[PTb[pi]])
                sch.op("pe", lambda e, j=j, sqv=sqv: e.matmul(PS[stat_bank][:, 0:n], ones, sqv, start=(j == 0),
                                                             stop=(j == nt - 1)),
                       reads=[PTb[pi], cb], writes=[PSb[stat_bank]])
            ti = nxt("t", 6)
            rs = Ta[ti][:, 0:n]
            sch.op("dve", lambda e: e.tensor_scalar(out=rs, in0=PS[stat_bank][:, 0:n], scalar1=1.0 / (nt * 128),
                                                    scalar2=RMS_EPS, op0=ALU.mult, op1=ALU.add),
                   reads=[PSb[stat_bank]], writes=[Tb_[ti]])
            sch.op("act", lambda e: e.activation(out=rs, in_=rs, func=AF.Sqrt), reads=[Tb_[ti]], writes=[Tb_[ti]])
            sch.op("dve", lambda e: e.reciprocal(out=rs, in_=rs), reads=[Tb_[ti]], writes=[Tb_[ti]])
            if extra != 1.0:
                sch.op("dve", lambda e: e.tensor_scalar(out=rs, in0=rs, scalar1=float(extra), scalar2=None, op0=ALU.mult),
                       reads=[Tb_[ti]], writes=[Tb_[ti]])
            for j in range(nt):
                sch.op("dve", lambda e, j=j: e.scalar_tensor_tensor(out=out_ap[:, j, :], in0=cf_ap[:, j, :], scalar=gfn(j),
                                                                     in1=rs, op0=ALU.mult, op1=ALU.mult),
                       reads=[cfb, Tb_[ti], cb], writes=[outb])

        def rope_epi(bank_x, bank_sw, tb, out_ap, outb):
            if self.rt_cache.get("tb") != tb:
                ri = nxt("rt", 1)
                sch.dma("sp", RTa[ri].rearrange("p (c n) -> p c n", n=TB), rope_in[tb], writes=[RTb[ri]])
                self.rt_cache = {"tb": tb, "ri": ri}
            ri = self.rt_cache["ri"]
            rt = RTa[ri].rearrange("p (c n) -> p c n", n=TB)
            t1 = nxt("t", 6)
            t2 = nxt("t", 6)
            sch.op("dve", lambda e: e.tensor_tensor(out=Ta[t1][0:64, :], in0=PS[bank_x][0:64, :], in1=rt[:, 0, :], op=ALU.mult),
                   reads=[PSb[bank_x], RTb[ri]], writes=[Tb_[t1]])
            sch.op("dve", lambda e: e.tensor_tensor(out=Ta[t2][0:64, :], in0=PS[bank_sw][0:64, :], in1=rt[:, 1, :], op=ALU.mult),
                   reads=[PSb[bank_sw], RTb[ri]], writes=[Tb_[t2]])
            sch.op("pool", lambda e: e.tensor_tensor(out=out_ap, in0=Ta[t1][0:64, :], in1=Ta[t2][0:64, :], op=ALU.add),
                   reads=[Tb_[t1], Tb_[t2]], writes=[outb])

        lnb = {}

        fq = [Buf("fq%d" % i, Fb[i // 2]) for i in range(4)]
        HBQ = 128
        NQ = TB // HBQ
        lnstat = {}

        def fq_ap(qi):
            return Fa[qi // 2][:, (qi % 2) * 2048:(qi % 2) * 2048 + 2048]

        NG = NTB * NQ

        def ln_pre(l, s, XFd, XFn, XBd, XBn):
            def zview(g):
                return fq_ap(g % 4).rearrange("p (c n) -> p c n", n=HBQ)

            def L(g):
                tb, hf = g // NQ, g % NQ
                sch.dma("sp", zview(g), Z[tb][:, :, hf * HBQ:(hf + 1) * HBQ], reads=[db("Z", tb)], writes=[fq[g % 4]])

            def A1(g, spare):
                wq = g % 4
                key = (spare, wq)
                if key not in lnb:
                    lnb[key] = (Buf("lnzb%d%d" % key, Wb[spare]), Buf("lnzq%d%d" % key, Wb[spare]))
                zbb, zqb = lnb[key]
                off = wq * 4096
                fa = fq_ap(wq)
                zb = W[spare][:, off:off + 2048]
                zq = W[spare][:, off + 2048:off + 4096]
                zbv = v3(W[spare], off, 16, HBQ)
                zqv = v3(W[spare], off + 2048, 16, HBQ)
                sch.op("dve", lambda e: e.tensor_copy(out=zb, in_=fa), reads=[fq[wq]], writes=[zbb])
                sch.op("act", lambda e: e.activation(out=zq, in_=fa, func=AF.Square), reads=[fq[wq]], writes=[zqb])
                b0 = 2 * wq
                for c in range(16):
                    sch.op("pe", lambda e, c=c: e.matmul(PS[b0][:, 0:HBQ], ones, zbv[:, c, :], start=(c == 0), stop=(c == 15)),
                           reads=[zbb, cb], writes=[PSb[b0]])
                for c in range(16):
                    sch.op("pe", lambda e, c=c: e.matmul(PS[b0 + 1][:, 0:HBQ], ones, zqv[:, c, :], start=(c == 0), stop=(c == 15)),
                           reads=[zqb, cb], writes=[PSb[b0 + 1]])

            def A2(g):
                b0 = 2 * (g % 4)
                si = nxt("st", 16)
                sj = nxt("st", 16)
                lnstat[g] = (si, sj)
                mean = STa[si][:, 0:HBQ]
                rstd = STa[sj][:, 0:HBQ]
                sch.op("dve", lambda e: e.tensor_scalar(out=mean, in0=PS[b0][:, 0:HBQ], scalar1=1.0 / D, scalar2=None,
                                                        op0=ALU.mult), reads=[PSb[b0]], writes=[STb[si]])
                sch.op("dve", lambda e: e.tensor_tensor(out=rstd, in0=mean, in1=mean, op=ALU.mult),
                       reads=[STb[si]], writes=[STb[sj]])
                sch.op("dve", lambda e: e.scalar_tensor_tensor(out=rstd, in0=PS[b0 + 1][:, 0:HBQ], scalar=1.0 / D, in1=rstd,
                                                               op0=ALU.mult, op1=ALU.subtract),
                       reads=[PSb[b0 + 1], STb[sj]], writes=[STb[sj]])
                sch.op("dve", lambda e: e.tensor_scalar(out=rstd, in0=rstd, scalar1=LN_EPS, scalar2=None, op0=ALU.add),
                       reads=[STb[sj]], writes=[STb[sj]])

            def A2b(g):
                si, sj = lnstat[g]
                rstd = STa[sj][:, 0:HBQ]
                sch.op("act", lambda e: e.activation(out=rstd, in_=rstd, func=AF.Sqrt), reads=[STb[sj]], writes=[STb[sj]])
                sch.op("dve", lambda e: e.reciprocal(out=rstd, in_=rstd), reads=[STb[sj]], writes=[STb[sj]])

            def B1(g):
                qi = g % 4
                fbuf = fq[qi]
                z = zview(g)
                si, sj = lnstat[g]
                mean_b = STa[si][:, 0:HBQ].unsqueeze(1).to_broadcast([128, 16, HBQ])
                rstd_b = STa[sj][:, 0:HBQ].unsqueeze(1).to_broadcast([128, 16, HBQ])
                sch.op("dve", lambda e: e.tensor_tensor(out=z, in0=z, in1=mean_b, op=ALU.subtract),
                       reads=[fbuf, STb[si]], writes=[fbuf])
                sch.op("dve", lambda e: e.tensor_tensor(out=z, in0=z, in1=rstd_b, op=ALU.mult),
                       reads=[fbuf, STb[sj]], writes=[fbuf])

            def B(g):
                tb, hf = g // NQ, g % NQ
                qi = g % 4
                fbuf = fq[qi]
                fa = fq_ap(qi)
                z = zview(g)
                for c in range(16):
                    sch.op("act", lambda e, c=c: e.activation(out=z[:, c, :], in_=z[:, c, :], func=AF.Identity,
                                                              bias=ln_b(l, s, c), scale=ln_g(l, s, c)),
                           reads=[fbuf, cb], writes=[fbuf])
                oi = nxt("sb", 4)
                sch.op("act", lambda e: e.activation(out=SBa[oi], in_=fa, func=AF.Copy), reads=[fbuf], writes=[SBb[oi]])
                sch.dma("sp", XFd[tb][:, :, hf * HBQ:(hf + 1) * HBQ], z, reads=[fbuf], writes=[db(XFn, tb)])
                sch.dma("sp", XBd[tb][:, :, hf * HBQ:(hf + 1) * HBQ], v3(SBa[oi], 0, 16, HBQ), reads=[SBb[oi]],
                        writes=[db(XBn, tb)])

            a1_done = set()

            def steps(g_lo, g_hi, spare):
                for g in range(g_lo, g_hi):
                    if 0 <= g + 3 < NG:
                        L(g + 3)
                    if 0 <= g < NG:
                        B1(g)
                    if 0 <= g + 1 < NG and (g + 1) not in a1_done:
                        A1(g + 1, spare)
                        a1_done.add(g + 1)
                    if 0 <= g + 2 < NG and g != g_hi - 1:
                        A1(g + 2, spare)
                        a1_done.add(g + 2)
                    if 0 <= g + 1 < NG:
                        A2(g + 1)
                    if 0 <= g < NG:
                        B(g)
                    if 0 <= g + 1 < NG:
                        A2b(g + 1)

            def hook(tb, spare):
                if tb == 0:
                    steps(-3, 2 * NQ, spare)
                elif tb + 1 < NTB:
                    steps((tb + 1) * NQ, (tb + 2) * NQ, spare)
            return hook

        def epi_resid(XFd, XFn, tiles_per_slab, group_tiles):
            def epi(sl, tb, gi, banks):
                c0 = sl * tiles_per_slab + gi * group_tiles
                nt = len(banks)
                xi = nxt("sf", 3)
                xf = v3(SFa[xi], 0, nt, TB)
                sch.dma("sp", xf, XFd[tb][:, c0:c0 + nt, :], reads=[db(XFn, tb)], writes=[SFb[xi]])
                for t_i in range(nt):
                    sch.op("dve", lambda e, t_i=t_i, xf=xf, b=banks[t_i]: e.scalar_tensor_tensor(
                        out=xf[:, t_i, :], in0=xf[:, t_i, :], scalar=float(ALPHA), in1=PS[b][:, :],
                        op0=ALU.mult, op1=ALU.add), reads=[SFb[xi], PSb[banks[t_i]]], writes=[SFb[xi]])
                sch.dma("sp", Z[tb][:, c0:c0 + nt, :], xf, reads=[SFb[xi]], writes=[db("Z", tb)])
            return epi

        def g4(n):
            return [[(c * 128, 128) for c in range(g * 4, g * 4 + 4)] for g in range(n // 512)]

        K16 = [list(range(16))]
        K4 = [list(range(4))]

        def gemm_v(wslabs, kc_total, ncols, a_src, a_name, Vd, vname, vcol0_fn):
            pend = Pend()
            for sl, wsrc in enumerate(wslabs):
                wi = load_w(wsrc, kc_total, ncols)
                wv = v3(W[wi], 0, kc_total, ncols)
                for tb in range(NTB):
                    ai = load_a(a_src(tb), kc_total, db(a_name, tb))
                    av = v3(A[ai], 0, kc_total, TB)
                    ncg = ncols // 512
                    for tt in range(4):
                        bl = []
                        for cg in range(ncg):
                            b = self.itc % 8
                            self.itc += 1
                            bl.append(b)
                            for kc in range(kc_total):
                                sch.op("pe", lambda e, b=b, kc=kc, tt=tt, cg=cg, wv=wv, av=av: e.matmul(
                                    PS[b][:, :], av[:, kc, tt * 128:(tt + 1) * 128], wv[:, kc, cg * 512:(cg + 1) * 512],
                                    start=(kc == 0), stop=(kc == kc_total - 1)),
                                    reads=[Wb[wi], Ab[ai]], writes=[PSb[b]])

                        def epi(sl=sl, tb=tb, tt=tt, bl=bl, ncg=ncg):
                            si = nxt("sb", 4)
                            for cg, b in enumerate(bl):
                                copy_bank("act" if cg % 2 == 0 else "dve", SBa[si][:, cg * 512:(cg + 1) * 512], SBb[si], b)
                            col = vcol0_fn(sl)
                            sch.dma("sp", Vd[tb * 4 + tt][:, col:col + ncg * 512], SBa[si][:, 0:ncg * 512], reads=[SBb[si]],
                                    writes=[db(vname, tb)])
                        pend.push(epi)
            pend.flush()

        def attn_mla():
            scale = float(192 ** -0.5)
            KNv = KN.rearrange("t p c n -> p t c n")
            QNv = QN.rearrange("t p c n -> p t c n")
            QRv = QR.rearrange("t p c n -> p t c n")
            Vv = V.rearrange("t p f -> p t f")
            krb = Buf("krb", Ab[0])
            krv = A[0][0:64, 0:4096]
            sch.dma("sp", krv.rearrange("p (t n) -> p t n", n=TB), KR.rearrange("t p n -> p t n"),
                    reads=[db("KR", t) for t in range(NTB)], writes=[krb])
            knb = [Buf("knb%d" % i, Wb[0]) for i in range(2)]
            vb = [Buf("vb%d" % i, Wb[0]) for i in range(2)]
            qnb = [Buf("qnb%d" % i, Wb[1]) for i in range(2)]
            qrb = [Buf("qrb%d" % i, Wb[1]) for i in range(2)]
            allk = [db("KN", t) for t in range(NTB)]
            allv = [db("V", t) for t in range(NTB)]
            allq = [db("QN", t) for t in range(NTB)]
            allqr = [db("QR", t) for t in range(NTB)]
            cnt = 0
            sidx = 0
            accb = [[Buf("accD%d" % i, Fb[i]), Buf("accP%d" % i, Fb[i])] for i in range(2)]
            acca = [[Fa[i][:, 0:512], Fa[i][:, 512:1024]] for i in range(2)]

            def views(h):
                i = h % 2
                return (W[0][:, i * 4096:(i + 1) * 4096], v3(W[0], 8192 + i * 4096, 32, 128),
                        W[1][:, i * 4096:(i + 1) * 4096], W[1][0:64, 8192 + i * 4096:8192 + (i + 1) * 4096])

            def loads(h):
                i = h % 2
                knv, vv, qnv, qrv = views(h)
                sch.dma("sp", knv.rearrange("p (t n) -> p t n", n=TB), KNv[:, :, h, :], reads=allk, writes=[knb[i]])
                sch.dma("sp", qnv.rearrange("p (t n) -> p t n", n=TB), QNv[:, :, h, :], reads=allq, writes=[qnb[i]])
                sch.dma("sp", qrv.rearrange("p (t n) -> p t n", n=TB), QRv[:, :, h, :], reads=allqr, writes=[qrb[i]])
                sch.dma("sp", vv, Vv[:, :, h * 128:(h + 1) * 128], reads=allv, writes=[vb[i]])
            loads(0)
            for h in range(16):
                i = h % 2
                knv, vv, qnv, qrv = views(h)
                if h + 1 < 16:
                    loads(h + 1)
                for qb in range(NTB):
                    ob = 3 + (cnt % 2)
                    smb = 5 + (cnt % 2)
                    cnt += 1
                    qs = slice(qb * TB, (qb + 1) * TB)
                    sbank = {}

                    def emit_s(kt):
                        nonlocal sidx
                        b = sidx % 3
                        sidx += 1
                        sbank[kt] = b
                        ks = slice(kt * 128, (kt + 1) * 128)
                        sch.op("pe", lambda e, b=b, ks=ks, knv=knv, qnv=qnv, qs=qs: e.matmul(PS[b][:, :], knv[:, ks], qnv[:, qs], start=True, stop=False),
                               reads=[knb[i], qnb[i]], writes=[PSb[b]])
                        sch.op("pe", lambda e, b=b, ks=ks, qrv=qrv, qs=qs: e.matmul(PS[b][:, :], krv[:, ks], qrv[:, qs], start=False, stop=True),
                               reads=[krb, qrb[i]], writes=[PSb[b]])
                    emit_s(0)
                    emit_s(1)
                    for kt in range(32):
                        if kt + 2 < 32:
                            emit_s(kt + 2)
                        b = sbank[kt]
                        pi = nxt("pt", 4)
                        sch.op("act", lambda e, b=b, pi=pi: e.activation(out=PTa[pi], in_=PS[b][:, :], func=AF.Exp, scale=scale),
                               reads=[PSb[b]], writes=[PTb[pi]])
                        sch.op("pe", lambda e, pi=pi, kt=kt, ob=ob, vv=vv: e.matmul(PS[ob][:, :], vv[:, kt, :], PTa[pi],
                                                                            start=(kt == 0), stop=(kt == 31)),
                               reads=[vb[i], PTb[pi]], writes=[PSb[ob]])
                        par = cnt % 2
                        which = kt % 2
                        eng = "dve"
                        acc_ap = acca[par][which]
                        if kt < 2:
                            sch.op(eng, lambda e, pi=pi, acc_ap=acc_ap: e.tensor_copy(out=acc_ap, in_=PTa[pi]),
                                   reads=[PTb[pi]], writes=[accb[par][which]])
                        else:
                            sch.op(eng, lambda e, pi=pi, acc_ap=acc_ap: e.tensor_tensor(out=acc_ap, in0=acc_ap, in1=PTa[pi],
                                                                                       op=ALU.add),
                                   reads=[PTb[pi], accb[par][which]], writes=[accb[par][which]])
                    for which in range(2):
                        sch.op("pe", lambda e, which=which, smb=smb, a_=acca[cnt % 2][which]: e.matmul(
                            PS[smb][:, :], ones32, a_, start=(which == 0), stop=(which == 1)),
                            reads=[cb, accb[cnt % 2][which]], writes=[PSb[smb]])
                    ti = nxt("t", 6)
                    si = nxt("sb", 4)
                    sch.op("dve", lambda e, ti=ti, smb=smb: e.reciprocal(out=Ta[ti], in_=PS[smb][:, :]),
                           reads=[PSb[smb]], writes=[Tb_[ti]])
                    sch.op("dve", lambda e, ti=ti, si=si, ob=ob: e.tensor_tensor(out=SBa[si][:, 0:512], in0=PS[ob][:, :],
                                                                                in1=Ta[ti], op=ALU.mult),
                           reads=[PSb[ob], Tb_[ti]], writes=[SBb[si]])
                    sch.dma("sp", OB[qb][:, h, :], SBa[si][:, 0:512], reads=[SBb[si]], writes=[db("OB", qb)])

        def attn_diff(j, layer_idx):
            scale = float(128 ** -0.5)
            lambda_init = 0.8 - 0.6 * math.exp(-0.3 * layer_idx)
            sch.op("dve", lambda e: e.tensor_tensor(out=small[:, 0:1], in0=lam_c(j, 0), in1=lam_c(j, 1), op=ALU.mult),
                   reads=[cb], writes=[smallb])
            sch.op("dve", lambda e: e.tensor_tensor(out=small[:, 1:2], in0=lam_c(j, 2), in1=lam_c(j, 3), op=ALU.mult),
                   reads=[cb, smallb], writes=[smallb])
            sch.op("pe", lambda e: e.matmul(PS[7][:, 0:2], ones32, small[:, 0:2], start=True, stop=True),
                   reads=[cb, smallb], writes=[PSb[7]])
            sch.op("act", lambda e: e.activation(out=small[:, 2:4], in_=PS[7][:, 0:2], func=AF.Exp),
                   reads=[PSb[7], smallb], writes=[smallb])
            sch.op("dve", lambda e: e.tensor_tensor(out=small[:, 4:5], in0=small[:, 3:4], in1=small[:, 2:3], op=ALU.subtract),
                   reads=[smallb], writes=[smallb])
            sch.op("dve", lambda e: e.tensor_scalar(out=small[:, 4:5], in0=small[:, 4:5], scalar1=-float(lambda_init),
                                                    scalar2=None, op0=ALU.add), reads=[smallb], writes=[smallb])
            QDv = QDd.rearrange("t p c n -> p t c n")
            KDv = KDd.rearrange("t p c n -> p t c n")
            VDv = VDd.rearrange("t p f -> p t f")
            allq = [db("QD", t) for t in range(NTB)]
            allk = [db("KD", t) for t in range(NTB)]
            allv = [db("VD", t) for t in range(NTB)]
            kqb = [Buf("kq%d" % i, Wb[0]) for i in range(4)]
            vbb = [Buf("vd%d" % i, Wb[1]) for i in range(2)]
            sqb = [Buf("sq%d" % i, Ab[i]) for i in range(2)]
            deferred = []
            sidx = 0
            def vloads(h):
                hi = h % 2
                vv = v3(W[1], hi * 8192, 32, 256)
                sch.dma("sp", vv, VDv[:, :, h * 256:(h + 1) * 256], reads=allv, writes=[vbb[hi]])
                bt = v3(Fa[hi], 0, 6, TB)
                for idx in range(6):
                    delta = idx * 128 - 128
                    c0 = 639 - delta
                    src = bass.AP(ZT.tensor, h * 128 * RL + c0, [[RL - 1, 128], [1, TB]])
                    sch.dma("sp", bt[:, idx, :], src, reads=[db("ZT", h)], writes=[Fb[hi]])
                btf = Fa[hi][:, 0:3072]
                bhi = A[hi][:, 0:3072]
                blo = A[hi][:, 3072:6144]
                sch.op("dve", lambda e, btf=btf: e.tensor_scalar(out=btf, in0=btf, scalar1=float(1.0 / scale), scalar2=None,
                                                                 op0=ALU.mult), reads=[Fb[hi]], writes=[Fb[hi]])
                sch.op("dve", lambda e, btf=btf, bhi=bhi: e.tensor_copy(out=bhi, in_=btf), reads=[Fb[hi]], writes=[Ab[hi]])
                sch.op("dve", lambda e, btf=btf, bhi=bhi: e.tensor_tensor(out=btf, in0=btf, in1=bhi, op=ALU.subtract),
                       reads=[Fb[hi], Ab[hi]], writes=[Fb[hi]])
                sch.op("dve", lambda e, btf=btf, blo=blo: e.tensor_copy(out=blo, in_=btf), reads=[Fb[hi]], writes=[Ab[hi]])
            vloads(0)
            for h in range(8):
                hi = h % 2
                fi = hi
                vv = v3(W[1], hi * 8192, 32, 256)
                bt = v3(Fa[hi], 0, 6, TB)
                kv_ = []
                qv_ = []
                for gi in range(2):
                    g = 2 * h + gi
                    kvw = W[0][:, gi * 4096:(gi + 1) * 4096]
                    qvw = W[0][:, 8192 + gi * 4096:8192 + (gi + 1) * 4096]
                    sch.dma("sp", kvw.rearrange("p (t n) -> p t n", n=TB), KDv[:, :, g, :], reads=allk, writes=[kqb[gi]])
                    sch.dma("sp", qvw.rearrange("p (t n) -> p t n", n=TB), QDv[:, :, g, :], reads=allq, writes=[kqb[2 + gi]])
                    kv_.append(kvw)
                    qv_.append(qvw)
                if h + 1 < 8:
                    vloads(h + 1)
                for qb in range(NTB):
                    qs = slice(qb * TB, (qb + 1) * TB)
                    for gi in range(2):
                        obs = (2 + 2 * gi, 3 + 2 * gi)
                        smb = 6 + gi
                        sbank = {}

                        def emit_s(kt, gi=gi):
                            nonlocal sidx
                            b = sidx % 2
                            sidx += 1
                            sbank[kt] = b
                            ks = slice(kt * 128, (kt + 1) * 128)
                            delta_ = kt * 128 - qb * TB
                            near_ = -128 <= delta_ <= 512
                            sch.op("pe", lambda e, b=b, ks=ks, kk=kv_[gi], qq=qv_[gi], qs=qs, near_=near_: e.matmul(
                                PS[b][:, :], kk[:, ks], qq[:, qs], start=True, stop=(not near_)),
                                reads=[kqb[gi], kqb[2 + gi]], writes=[PSb[b]])
                            if near_:
                                ix = (delta_ + 128) // 128
                                for part in range(2):
                                    bsrc = A[hi][:, part * 3072 + ix * TB:part * 3072 + (ix + 1) * TB]
                                    sch.op("pe", lambda e, b=b, bsrc=bsrc, part=part: e.matmul(
                                        PS[b][:, :], ident, bsrc, start=False, stop=(part == 1)),
                                        reads=[cb, Ab[hi]], writes=[PSb[b]])
                        emit_s(0)
                        emit_s(1)
                        for kt in range(32):
                            b = sbank[kt]
                            pi = nxt("pt", 4)
                            delta = kt * 128 - qb * TB
                            if -128 <= delta <= 512:
                                sch.op("act", lambda e, b=b, pi=pi: e.activation(out=PTa[pi], in_=PS[b][:, :], func=AF.Exp,
                                                                                 scale=scale),
                                       reads=[PSb[b]], writes=[PTb[pi]])
                            else:
                                col = 16 + 2 * h + (0 if delta < 0 else 1)
                                sch.op("act", lambda e, b=b, pi=pi, col=col: e.activation(
                                    out=PTa[pi], in_=PS[b][:, :], func=AF.Exp, scale=scale, bias=small[:, col:col + 1]),
                                    reads=[PSb[b], smallb], writes=[PTb[pi]])
                            for dv in range(2):
                                sch.op("pe", lambda e, pi=pi, kt=kt, dv=dv, ob=obs[dv], vv=vv: e.matmul(
                                    PS[ob][:, :], vv[:, kt, dv * 128:(dv + 1) * 128], PTa[pi], start=(kt == 0), stop=(kt == 31)),
                                    reads=[vbb[hi], PTb[pi]], writes=[PSb[obs[dv]]])
                                if dv == 0 and kt + 2 < 32:
                                    emit_s(kt + 2)
                            sch.op("pe", lambda e, pi=pi, kt=kt, smb=smb: e.matmul(PS[smb][:, :], ones, PTa[pi],
                                                                                  start=(kt == 0), stop=(kt == 31)),
                                   reads=[cb, PTb[pi]], writes=[PSb[smb]])
                            if gi == 0 and kt == 6 and deferred:
                                deferred.pop(0)()
                    r0 = nxt("t", 6)
                    r1 = nxt("t", 6)
                    sch.op("dve", lambda e, r0=r0: e.reciprocal(out=Ta[r0], in_=PS[6][:, :]), reads=[PSb[6]], writes=[Tb_[r0]])
                    sch.op("dve", lambda e, r1=r1: e.reciprocal(out=Ta[r1], in_=PS[7][:, :]), reads=[PSb[7]], writes=[Tb_[r1]])
                    sch.op("dve", lambda e, r1=r1: e.tensor_scalar(out=Ta[r1], in0=Ta[r1], scalar1=small[:, 4:5], scalar2=None,
                                                                   op0=ALU.mult), reads=[Tb_[r1], smallb], writes=[Tb_[r1]])
                    ci = nxt("sf", 3)
                    cf = v3(SFa[ci], 0, 2, TB)
                    for dv in range(2):
                        t0 = nxt("t", 6)
                        sch.op("dve", lambda e, t0=t0, dv=dv, r0=r0: e.tensor_tensor(out=Ta[t0], in0=PS[2 + dv][:, :], in1=Ta[r0],
                                                                                     op=ALU.mult),
                               reads=[PSb[2 + dv], Tb_[r0]], writes=[Tb_[t0]])
                        sch.op("dve", lambda e, dv=dv, r1=r1, cf=cf: e.tensor_tensor(out=cf[:, dv, :], in0=PS[4 + dv][:, :],
                                                                                     in1=Ta[r1], op=ALU.mult),
                               reads=[PSb[4 + dv], Tb_[r1]], writes=[SFb[ci]])
                        sch.op("pool", lambda e, t0=t0, dv=dv, cf=cf: e.tensor_tensor(out=cf[:, dv, :], in0=cf[:, dv, :],
                                                                                      in1=Ta[t0], op=ALU.add),
                               reads=[SFb[ci], Tb_[t0]], writes=[SFb[ci]])
                    sqv = v3(A[hi], 6144, 2, TB)
                    for dv in range(2):
                        sch.op("pool", lambda e, dv=dv, cf=cf, sqv=sqv: e.tensor_tensor(out=sqv[:, dv, :], in0=cf[:, dv, :],
                                                                                        in1=cf[:, dv, :], op=ALU.mult),
                               reads=[SFb[ci]], writes=[sqb[hi]])

                    def part2(cf=cf, ci=ci, sqv=sqv, hi=hi, qb=qb, h=h):
                        for dv in range(2):
                            sch.op("pe", lambda e, dv=dv, sqv=sqv: e.matmul(PS[7][:, :], ones, sqv[:, dv, :], start=(dv == 0),
                                                                           stop=(dv == 1)),
                                   reads=[sqb[hi], cb], writes=[PSb[7]])
                        ti = nxt("t", 6)
                        rs = Ta[ti]
                        sch.op("dve", lambda e, rs=rs: e.tensor_scalar(out=rs, in0=PS[7][:, :], scalar1=1.0 / 256, scalar2=RMS_EPS,
                                                                      op0=ALU.mult, op1=ALU.add),
                               reads=[PSb[7]], writes=[Tb_[ti]])
                        sch.op("act", lambda e, rs=rs: e.activation(out=rs, in_=rs, func=AF.Sqrt), reads=[Tb_[ti]], writes=[Tb_[ti]])
                        sch.op("dve", lambda e, rs=rs: e.reciprocal(out=rs, in_=rs), reads=[Tb_[ti]], writes=[Tb_[ti]])
                        sch.op("dve", lambda e, rs=rs: e.tensor_scalar(out=rs, in0=rs, scalar1=float(1.0 - lambda_init), scalar2=None,
                                                                      op0=ALU.mult), reads=[Tb_[ti]], writes=[Tb_[ti]])
                        si = nxt("sb", 4)
                        ov = v3(SBa[si], 0, 2, TB)
                        for dv in range(2):
                            sch.op("dve", lambda e, dv=dv, cf=cf, ov=ov, rs=rs: e.scalar_tensor_tensor(
                                out=ov[:, dv, :], in0=cf[:, dv, :], scalar=sub_g(j, dv), in1=rs, op0=ALU.mult, op1=ALU.mult),
                                reads=[SFb[ci], Tb_[ti], cb], writes=[SBb[si]])
                        sch.dma("sp", OB[qb][:, 2 * h:2 * h + 2, :], ov, reads=[SBb[si]], writes=[db("OB", qb)])
                    deferred.append(part2)
            while deferred:
                deferred.pop(0)()

        XF_cur, XF_name = x_in, "x"
        for l in range(L):
            j = l // 2
            if l % 2 == 0:
                def epi_m1(sl, tb, gi, banks, j=j):
                    if gi < 2:
                        fi = nxt("sf", 3)
                        cf = v3(SFa[fi], 0, 4, TB)
                        for t_i, b in enumerate(banks):
                            copy_bank("act", cf[:, t_i, :], SFb[fi], b)
                        si = nxt("sb", 4)
                        ov = v3(SBa[si], 0, 4, TB)
                        gfn = (lambda c: qn_g(j, c)) if gi == 0 else (lambda c: kvn_g(j, c))
                        rms_epi(cf, SFb[fi], 4, TB, gfn, 1.0, banks[0], ov, SBb[si])
                        dst, nm = (CQN, "CQN") if gi == 0 else (CKVN, "CKVN")
                        sch.dma("sp", dst[tb], ov, reads=[SBb[si]], writes=[db(nm, tb)])
                    else:
                        si = nxt("sb", 4)
                        ov = SBa[si][0:64, 0:512]
                        rope_epi(banks[0], banks[1], tb, ov, SBb[si])
                        sch.dma("sp", KR[tb], ov, reads=[SBb[si]], writes=[db("KR", tb)])
                groups = [[(c * 128, 128) for c in range(0, 4)], [(c * 128, 128) for c in range(4, 8)],
                          [(1024, 64), (1088, 64)]]
                gemm([mla_win[j]], 16, 1152, groups, lambda tb: X0B[tb], "X0B", K16, epi_m1)

                m2s = {}

                def epi_m2(sl, tb, gi, banks):
                    h = gi
                    if h % 4 == 0:
                        m2s["n"] = nxt("sb", 4)
                        m2s["r"] = nxt("sb", 4)
                    si, s2 = m2s["n"], m2s["r"]
                    k = h % 4
                    copy_bank("act", SBa[si][:, k * 512:(k + 1) * 512], SBb[si], banks[0])
                    rope_epi(banks[1], banks[2], tb, SBa[s2][0:64, k * 512:(k + 1) * 512], SBb[s2])
                    if k == 3:
                        sch.dma("sp", QN[tb][:, h - 3:h + 1, :], v3(SBa[si], 0, 4, TB), reads=[SBb[si]], writes=[db("QN", tb)])
                        sch.dma("sp", QR[tb][:, h - 3:h + 1, :], v3(SBa[s2], 0, 4, TB)[0:64], reads=[SBb[s2]],
                                writes=[db("QR", tb)])
                groups = [[(h * 256, 128), (h * 256 + 128, 64), (h * 256 + 192, 64)] for h in range(16)]
                gemm([mla_wuq[j]], 4, 4096, groups, lambda tb: CQN[tb], "CQN", K4, epi_m2)
                gemm([mla_wuk[j]], 4, 2048, g4(2048), lambda tb: CKVN[tb], "CKVN", K4,
                     epi_store_bf16(lambda sl, tb, gi: (KN[tb][:, gi * 4:gi * 4 + 4, :], "KN")))
                gemm_v([mla_wuv[j]], 4, 2048, lambda tb: CKVN[tb], "CKVN", V, "V", lambda sl: 0)
                attn_mla()
                wo = mla_wo[j]
            else:
                def dst_qk(sl, tb, gi):
                    t0 = sl * 8 + gi * 4
                    if t0 < 16:
                        return QDd[tb][:, t0:t0 + 4, :], "QD"
                    return KDd[tb][:, t0 - 16:t0 - 12, :], "KD"
                gemm([diff_wqk[j][s_] for s_ in range(4)], 16, 1024, g4(1024), lambda tb: X0B[tb], "X0B", K16,
                     epi_store_bf16(dst_qk))
                gemm_v([diff_wv[j][0], diff_wv[j][1]], 16, 1024, lambda tb: X0B[tb], "X0B", VDd, "VD", lambda sl: sl * 1024)
                attn_diff(j, l)
                wo = diff_wo[j]
            gemm([wo[0], wo[1]], 16, 1024, g4(1024), lambda tb: OB[tb], "OB", K16, epi_resid(XF_cur, XF_name, 8, 4))

            def epi_f1(sl, tb, gi, banks):
                si = nxt("sb", 4)
                hv = v3(SBa[si], 0, 2, TB)
                for t in range(2):
                    ti = nxt("t", 6)
                    sch.op("act", lambda e, ti=ti, b=banks[t]: e.activation(out=Ta[ti], in_=PS[b][:, :], func=AF.Silu),
                           reads=[PSb[banks[t]]], writes=[Tb_[ti]])
                    sch.op("dve", lambda e, ti=ti, t=t, hv=hv, b=banks[2 + t]: e.tensor_tensor(
                        out=hv[:, t, :], in0=Ta[ti], in1=PS[b][:, :], op=ALU.mult),
                        reads=[Tb_[ti], PSb[banks[2 + t]]], writes=[SBb[si]])
                c0 = sl * 4 + gi * 2
                sch.dma("sp", H[tb][:, c0:c0 + 2, :], hv, reads=[SBb[si]], writes=[db("H", tb)])
            groups = [[(gi * 512 + k * 128, 128) for k in range(4)] for gi in range(2)]
            gemm([ffn_win[l][s_] for s_ in range(11)], 16, 1024, groups, lambda tb: X1B[tb], "X1B", K16, epi_f1,
                 pre=ln_pre(l, 0, X1F, "X1F", X1B, "X1B"))
            kg44 = [list(range(g * 11, g * 11 + 11)) for g in range(4)]
            gemm([ffn_wout[l][s_] for s_ in range(4)], 44, 512, g4(512), lambda tb: H[tb], "H", kg44,
                 epi_resid(X1F, "X1F", 4, 4), ksplit=True)
            last = (l == L - 1)
            XNd, XNn = (out_d, "out") if last else (X0F, "X0F")
            pre = ln_pre(l, 1, X2F, "X2F", X2B, "X2B")
            pend = Pend()
            for sl in range(2):
                if sl == 0:
                    spare = 1 - rr["w"]
                wi = nxt("w", 2)
                gv = v3(W[wi], 0, 16, 1024)
                pv = v3(W[wi], 16384, 2, 1024)
                for c0_ in range(0, 16, 4):
                    sch.dma("pool", gv[:, c0_:c0_ + 4, :], ple_wg[l][sl][:, c0_:c0_ + 4, :], writes=[Wb[wi]])
                sch.dma("pool", pv, ple_wp[l][sl], writes=[Wb[wi]])
                for tb in range(NTB):
                    if sl == 0 and tb == 0:
                        pend.flush()
                        pre(0, spare)
                    ai = load_a(X2B[tb], 16, db("X2B", tb))
                    av = v3(A[ai], 0, 16, TB)
                    pi_ = nxt("sb", 4)
                    ptv = v3(SBa[pi_], 0, 2, TB)
                    sch.dma("pool", ptv, p_in[l][tb], writes=[SBb[pi_]])
                    for gi in range(4):
                        base = 4 * (self.itc % 2)
                        self.itc += 1
                        banks = [base, base + 1, base + 2, base + 3]
                        for t in range(2):
                            co = gi * 256 + t * 128
                            for kc in range(16):
                                sch.op("pe", lambda e, b=banks[t], co=co, kc=kc, gv=gv, av=av: e.matmul(
                                    PS[b][:, :], gv[:, kc, co:co + 128], av[:, kc, :], start=(kc == 0), stop=(kc == 15)),
                                    reads=[Wb[wi], Ab[ai]], writes=[PSb[banks[t]]])
                            for kc in range(2):
                                sch.op("pe", lambda e, b=banks[2 + t], co=co, kc=kc, pv=pv, ptv=ptv: e.matmul(
                                    PS[b][:, :], pv[:, kc, co:co + 128], ptv[:, kc, :], start=(kc == 0), stop=(kc == 1)),
                                    reads=[Wb[wi], SBb[pi_]], writes=[PSb[banks[2 + t]]])

                        def epi(sl=sl, tb=tb, gi=gi, banks=banks):
                            c0 = sl * 8 + gi * 2
                            xi = nxt("sf", 3)
                            xv = v3(SFa[xi], 0, 2, TB)
                            sch.dma("sp", xv, X2F[tb][:, c0:c0 + 2, :], reads=[db("X2F", tb)], writes=[SFb[xi]])
                            si = nxt("sb", 4)
                            bv = v3(SBa[si], 0, 2, TB)
                            tis = []
                            for t in range(2):
                                ti = nxt("t", 6)
                                tis.append(ti)
                                sch.op("act", lambda e, ti=ti, b=banks[t]: e.activation(out=Ta[ti], in_=PS[b][:, :],
                                                                                        func=AF.Sigmoid),
                                       reads=[PSb[banks[t]]], writes=[Tb_[ti]])
                                sch.op("dve", lambda e, ti=ti, b=banks[2 + t]: e.tensor_tensor(out=Ta[ti], in0=Ta[ti],
                                                                                               in1=PS[b][:, :], op=ALU.mult),
                                       reads=[Tb_[ti], PSb[banks[2 + t]]], writes=[Tb_[ti]])
                            for t in range(2):
                                sch.op("dve", lambda e, t=t, ti=tis[t], xv=xv: e.tensor_tensor(out=xv[:, t, :], in0=xv[:, t, :],
                                                                                               in1=Ta[ti], op=ALU.add),
                                       reads=[Tb_[tis[t]], SFb[xi]], writes=[SFb[xi]])
                            sch.dma("sp", XNd[tb][:, c0:c0 + 2, :], xv, reads=[SFb[xi]], writes=[db(XNn, tb)])
                            if not last:
                                sch.op("act", lambda e, xv=xv, bv=bv: e.activation(out=bv, in_=xv, func=AF.Copy),
                                       reads=[SFb[xi]], writes=[SBb[si]])
                                sch.dma("sp", X0B[tb][:, c0:c0 + 2, :], bv, reads=[SBb[si]], writes=[db("X0B", tb)])
                        pend.push(epi)
                    if sl == 0 and tb + 1 < NTB:
                        pend.flush()
                        pre(tb + 1, spare)
            pend.flush()
            XF_cur, XF_name = X0F, "X0F"

        sch.finalize()
        with nc.Block() as block:
            @block.tensor
            def _(e):
                sch.emit(e, "pe")

            @block.scalar
            def _(e):
                sch.emit(e, "act")

            @block.vector
            def _(e):
                sch.emit(e, "dve")

            @block.gpsimd
            def _(e):
                sch.emit(e, "pool")

            @block.sync
            def _(e):
                sch.emit(e, "sp")
        self.stack.close()
        return self


def _pack(w, ncols):
    k, n = w.shape
    return np.ascontiguousarray(w.reshape(k // 128, 128, n // ncols, ncols).transpose(2, 1, 0, 3))


def _fm(a, c):
    return np.ascontiguousarray(a.reshape(NTB, TB, c, 128).transpose(0, 3, 2, 1))


def _prep_shared(inp):
    f = lambda a: np.asarray(a, dtype=np.float32)
    sw = np.concatenate([np.arange(32, 64), np.arange(0, 32)])
    o = {}
    w = f(inp["mla_w_in"])
    o["mla_win"] = np.stack([_pack(np.concatenate([w[j], w[j][:, 1024:1088][:, sw]], 1), 1152)[0] for j in range(2)])
    w = f(inp["mla_w_uq"]).reshape(2, 512, 16, 192)
    wq = np.concatenate([w[..., :128], w[..., 128:192], w[..., 128:192][..., sw]], -1).reshape(2, 512, 4096)
    o["mla_wuq"] = np.stack([_pack(wq[j], 4096)[0] for j in range(2)])
    w = f(inp["mla_w_ukv"]).reshape(2, 512, 16, 256)
    o["mla_wuk"] = np.stack([_pack(np.ascontiguousarray(w[j][..., :128]).reshape(512, 2048), 2048)[0] for j in range(2)])
    o["mla_wuv"] = np.stack([_pack(np.ascontiguousarray(w[j][..., 128:]).reshape(512, 2048), 2048)[0] for j in range(2)])
    o["mla_wo"] = np.stack([_pack(f(inp["mla_w_o"])[j], 1024) for j in range(2)])
    w = f(inp["diff_w_in"])
    o["diff_wqk"] = np.stack([_pack(np.ascontiguousarray(w[j][:, :4096]), 1024) for j in range(2)])
    o["diff_wv"] = np.stack([_pack(np.ascontiguousarray(w[j][:, 4096:]), 1024) for j in range(2)])
    o["diff_wo"] = np.stack([_pack(f(inp["diff_w_o"])[j], 1024) for j in range(2)])
    w = f(inp["ffn_w_in"])
    lst = []
    for l in range(DEPTH):
        g = w[l][:, :DFF].reshape(D, 11, 2, 1, 256)
        u = w[l][:, DFF:].reshape(D, 11, 2, 1, 256)
        lst.append(_pack(np.concatenate([g, u], 3).reshape(D, 11 * 1024), 1024))
    o["ffn_win"] = np.stack(lst)
    o["ffn_wout"] = np.stack([_pack(f(inp["ffn_w_out"])[l], 512) for l in range(DEPTH)])
    o["ple_wg"] = np.stack([_pack(f(inp["ple_w_gate"])[l], 1024) for l in range(DEPTH)])
    o["ple_wp"] = np.stack([_pack(f(inp["ple_w_proj"])[l], 1024) for l in range(DEPTH)])
    vecs = np.zeros((128, 284), np.float32)
    vecs[:, 0:128] = f(inp["ln_g"]).reshape(4, 2, 16, 128).transpose(3, 0, 1, 2).reshape(128, 128)
    vecs[:, 128:256] = f(inp["ln_b"]).reshape(4, 2, 16, 128).transpose(3, 0, 1, 2).reshape(128, 128)
    vecs[:, 256:264] = f(inp["mla_q_norm"]).reshape(2, 4, 128).transpose(2, 0, 1).reshape(128, 8)
    vecs[:, 264:272] = f(inp["mla_kv_norm"]).reshape(2, 4, 128).transpose(2, 0, 1).reshape(128, 8)
    vecs[:, 272:276] = f(inp["diff_sub_norm"]).reshape(2, 2, 128).transpose(2, 0, 1).reshape(128, 4)
    vecs[:, 276:284] = f(inp["diff_lambda"]).transpose(2, 0, 1).reshape(128, 8)
    o["vecs"] = vecs
    o["relb"] = np.ascontiguousarray(f(inp["rel_bias"]))
    rope, onehot = _const_tables()
    o["rope"] = rope
    o["onehot"] = onehot
    o["ident"] = np.eye(128, dtype=np.float32)
    return o


_NC_CACHE = {}


def _get_nc(debug=None, nlayers=DEPTH, stop_after=None):
    key = (tuple(debug or ()), nlayers, stop_after)
    if key not in _NC_CACHE:
        kb = KB(debug=debug, nlayers=nlayers)
        kb.stop_after = stop_after
        kb.build()
        _NC_CACHE[key] = kb.nc
    return _NC_CACHE[key]


def kernel(**inputs):
    shared = _prep_shared(inputs)
    x = np.asarray(inputs["x"], dtype=np.float32)
    p = np.asarray(inputs["p"], dtype=np.float32)
    in_maps = []
    for b in range(NCORES):
        m = dict(shared)
        m["x"] = _fm(x[b], 16)
        m["p"] = np.stack([_fm(p[l, b], 2) for l in range(DEPTH)])
        in_maps.append(m)
    nc = _get_nc()
    res = run_bass_kernel_spmd(nc, in_maps, core_ids=list(range(NCORES)))
    out = np.empty((NCORES, S, D), np.float32)
    for b in range(NCORES):
        o = np.asarray(res.results[b]["out"]).reshape(NTB, 128, 16, TB)
        out[b] = o.transpose(0, 3, 2, 1).reshape(S, D)
    return out
```

```python
import math
from contextlib import ExitStack
import numpy as np
import concourse.bass as bass
import concourse.mybir as mybir
from concourse.bass_utils import run_bass_kernel_spmd

F32 = mybir.dt.float32
BF16 = mybir.dt.bfloat16
AF = mybir.ActivationFunctionType
ALU = mybir.AluOpType

D = 2048
S = 4096
NTB = 8
TB = 512
DEPTH = 4
DFF = 5632
ALPHA = (2 * DEPTH) ** 0.25
LN_EPS = 1e-5
RMS_EPS = 1e-6
NCORES = 8
RL = 1280


class Buf:
    __slots__ = ("name", "w", "r", "parent", "kids", "wd")

    def __init__(self, name, parent=None):
        self.name = name
        self.w = None
        self.wd = {}
        self.r = {}
        self.parent = parent
        self.kids = []
        if parent is not None:
            parent.kids.append(self)


class Op:
    __slots__ = ("eng", "fn", "waits", "observed", "idx", "val", "dma_sem")

    def __init__(self, eng, fn):
        self.eng = eng
        self.fn = fn
        self.waits = []
        self.observed = False
        self.idx = 0
        self.val = 0
        self.dma_sem = None


ENGS = ["pe", "act", "dve", "pool", "sp"]


class Sched:
    def __init__(self, nc, stack, rings=None):
        rings = rings or {"sp": 28, "pool": 20}
        self.nc = nc
        self.ops = {e: [] for e in ENGS}
        self.esem = {e: stack.enter_context(nc.semaphore("es_" + e)) for e in ENGS}
        self.rings = {q: [stack.enter_context(nc.semaphore("rg_%s%d" % (q, i))) for i in range(n)]
                      for q, n in rings.items()}
        self.ring_use = {q: [0] * n for q, n in rings.items()}
        self.ring_pos = {q: 0 for q in rings}
        self.waited = {e: {} for e in ENGS}

    @staticmethod
    def _key(ev):
        return ev[1]

    @staticmethod
    def _ord(ev):
        return ev[2].idx + 1 if ev[0] == "e" else ev[2]

    def _collect(self, reads, writes):
        evs = []

        def wr(b):
            if b.w:
                evs.append(b.w)
            evs.extend(b.wd.values())
        for b in reads:
            wr(b)
            if b.parent is not None:
                wr(b.parent)
            for k in b.kids:
                wr(k)
        for b in writes:
            wr(b)
            evs.extend(b.r.values())
            if b.parent is not None:
                p = b.parent
                wr(p)
                evs.extend(p.r.values())
            for k in b.kids:
                wr(k)
                evs.extend(k.r.values())
        return evs

    def _waits(self, eng, evs):
        wd = self.waited[eng]
        best = {}
        for ev in evs:
            k = self._key(ev)
            if ev[0] == "e" and k == eng and eng in ("pe", "sp"):
                continue
            o = self._ord(ev)
            if wd.get(k, 0) >= o:
                continue
            if k not in best or self._ord(best[k]) < o:
                best[k] = ev
        out = []
        for k, ev in best.items():
            wd[k] = self._ord(ev)
            if ev[0] == "e":
                ev[2].observed = True
            out.append(ev)
        return out

    def _record(self, ev, reads, writes):
        k = self._key(ev)
        for b in reads:
            b.r[k] = ev
        for b in writes:
            if ev[0] == "d":
                b.wd[k] = ev
            else:
                b.w = ev
                b.wd = {}
            b.r = {}
            for kd in b.kids:
                kd.w = None
                kd.wd = {}
                kd.r = {}

    def op(self, eng, fn, reads=(), writes=()):
        o = Op(eng, fn)
        o.waits = self._waits(eng, self._collect(reads, writes))
        o.idx = len(self.ops[eng])
        self.ops[eng].append(o)
        self._record(("e", eng, o), reads, writes)
        return o

    def dma(self, q, out_ap, in_ap, reads=(), writes=()):
        n = len(self.rings[q])
        slot = self.ring_pos[q]
        self.ring_pos[q] = (slot + 1) % n
        prev = 16 * self.ring_use[q][slot]
        self.ring_use[q][slot] += 1
        evs = self._collect(reads, writes)
        if prev:
            evs.append(("d", (q, slot), prev))
        o = Op(q, lambda e: e.dma_start(out=out_ap, in_=in_ap))
        o.waits = self._waits(q, evs)
        o.idx = len(self.ops[q])
        o.dma_sem = self.rings[q][slot]
        self.ops[q].append(o)
        self._record(("d", (q, slot), prev + 16), reads, writes)
        return o

    def finalize(self):
        for e in ENGS:
            c = 0
            for o in self.ops[e]:
                if o.observed and o.dma_sem is None:
                    c += 1
                    o.val = c

    def emit(self, engobj, eng):
        for o in self.ops[eng]:
            for ev in o.waits:
                if ev[0] == "e":
                    engobj.wait_ge(self.esem[ev[1]], ev[2].val)
                else:
                    q, slot = ev[1]
                    engobj.wait_ge(self.rings[q][slot], ev[2])
            ins = o.fn(engobj)
            if o.dma_sem is not None:
                ins.then_inc(o.dma_sem, 16)
            elif o.observed:
                ins.then_inc(self.esem[eng], 1)
        if eng == "sp":
            for q, sems in self.rings.items():
                for slot, sem in enumerate(sems):
                    u = self.ring_use[q][slot]
                    if u:
                        engobj.wait_ge(sem, 16 * u)


def _t5_bucket_np(rel):
    nb = 16
    max_exact = 8
    ret = (rel > 0).astype(np.int32) * nb
    n = np.abs(rel)
    nf = np.maximum(n, 1).astype(np.float32)
    large = max_exact + (np.log(nf / np.float32(max_exact)) / np.float32(math.log(128 / max_exact))
                         * np.float32(nb - max_exact)).astype(np.int32)
    large = np.minimum(large, nb - 1)
    return ret + np.where(n < max_exact, n, large)


def _const_tables():
    pos = np.arange(S, dtype=np.float32)
    inv = (1.0 / (np.float32(10000.0) ** (np.arange(0, 64, 2, dtype=np.float32) / np.float32(64)))).astype(np.float32)
    ang = pos[:, None] * inv[None, :]
    cos = np.cos(ang).astype(np.float32).T
    sin = np.sin(ang).astype(np.float32).T
    cos2 = np.concatenate([cos, cos], 0)
    sins = np.concatenate([-sin, sin], 0)
    rope = np.stack([cos2, sins], 1)
    rope = rope.reshape(64, 2, NTB, TB).transpose(2, 0, 1, 3)
    i = np.arange(RL)
    rel = (RL - 1 - i) - 640
    bk = _t5_bucket_np(rel)
    onehot = np.zeros((32, RL), np.float32)
    onehot[bk, i] = 1.0
    return np.ascontiguousarray(rope), onehot


class KB:
    def __init__(self, debug=None, nlayers=DEPTH):
        self.debug = debug or ()
        self.nlayers = nlayers
        self.nc = bass.Bass("TRN2", target_bir_lowering=False)
        self.stack = ExitStack()
        self.dbufs = {}
        self.dt = {}
        self.stop_after = None

    def din(self, name, shape):
        t = self.nc.dram_tensor(name, list(shape), F32, kind="ExternalInput")
        self.dt[name] = t.ap()
        return self.dt[name]

    def dscr(self, name, shape, dtype, out=False):
        kind = "ExternalOutput" if (out or name in self.debug) else "Internal"
        t = self.nc.dram_tensor(name, list(shape), dtype, kind=kind)
        self.dt[name] = t.ap()
        return self.dt[name]

    def db(self, name, i=0):
        k = (name, i)
        if k not in self.dbufs:
            self.dbufs[k] = Buf("%s_%d" % (name, i))
        return self.dbufs[k]

    def sb(self, name, shape, dtype):
        return self.stack.enter_context(self.nc.sbuf_tensor(name, list(shape), dtype))

    def build(self):
        nc = self.nc
        st = self.stack
        L = self.nlayers
        x_in = self.din("x", [NTB, 128, 16, TB])
        p_in = self.din("p", [DEPTH, NTB, 128, 2, TB])
        mla_win = self.din("mla_win", [2, 128, 16, 1152])
        mla_wuq = self.din("mla_wuq", [2, 128, 4, 4096])
        mla_wuk = self.din("mla_wuk", [2, 128, 4, 2048])
        mla_wuv = self.din("mla_wuv", [2, 128, 4, 2048])
        mla_wo = self.din("mla_wo", [2, 2, 128, 16, 1024])
        diff_wqk = self.din("diff_wqk", [2, 4, 128, 16, 1024])
        diff_wv = self.din("diff_wv", [2, 2, 128, 16, 1024])
        diff_wo = self.din("diff_wo", [2, 2, 128, 16, 1024])
        ffn_win = self.din("ffn_win", [DEPTH, 11, 128, 16, 1024])
        ffn_wout = self.din("ffn_wout", [DEPTH, 4, 128, 44, 512])
        ple_wg = self.din("ple_wg", [DEPTH, 2, 128, 16, 1024])
        ple_wp = self.din("ple_wp", [DEPTH, 2, 128, 2, 1024])
        vecs_in = self.din("vecs", [128, 284])
        relb_in = self.din("relb", [32, 8])
        rope_in = self.din("rope", [NTB, 64, 2, TB])
        onehot_in = self.din("onehot", [32, RL])
        ident_in = self.din("ident", [128, 128])

        out_d = self.dscr("out", [NTB, 128, 16, TB], F32, out=True)
        X0F = self.dscr("X0F", [NTB, 128, 16, TB], F32)
        X1F = self.dscr("X1F", [NTB, 128, 16, TB], F32)
        X2F = self.dscr("X2F", [NTB, 128, 16, TB], F32)
        X0B = self.dscr("X0B", [NTB, 128, 16, TB], BF16)
        X1B = self.dscr("X1B", [NTB, 128, 16, TB], BF16)
        X2B = self.dscr("X2B", [NTB, 128, 16, TB], BF16)
        Z = self.dscr("Z", [NTB, 128, 16, TB], F32)
        OB = self.dscr("OB", [NTB, 128, 16, TB], BF16)
        H = self.dscr("H", [NTB, 128, 44, TB], BF16)
        CQN = self.dscr("CQN", [NTB, 128, 4, TB], BF16)
        CKVN = self.dscr("CKVN", [NTB, 128, 4, TB], BF16)
        KR = self.dscr("KR", [NTB, 64, TB], BF16)
        QN = self.dscr("QN", [NTB, 128, 16, TB], BF16)
        QR = self.dscr("QR", [NTB, 64, 16, TB], BF16)
        KN = self.dscr("KN", [NTB, 128, 16, TB], BF16)
        V = self.dscr("V", [32, 128, 2048], BF16)
        RD = self.dscr("RD", [8, RL], F32)
        ZT = self.dscr("ZT", [8, 128, RL], F32)

        Wt = [self.sb("W%d" % i, [128, 18432], BF16) for i in range(2)]
        At = [self.sb("A%d" % i, [128, 8192], BF16) for i in range(2)]
        Ft = [self.sb("F%d" % i, [128, 4096], F32) for i in range(2)]
        SBt = [self.sb("SB%d" % i, [128, 2048], BF16) for i in range(4)]
        SFt = [self.sb("SF%d" % i, [128, 2048], F32) for i in range(3)]
        Tt = [self.sb("T%d" % i, [128, 512], F32) for i in range(6)]
        PTt = [self.sb("PT%d" % i, [128, 512], BF16) for i in range(4)]
        STt = [self.sb("ST%d" % i, [128, 128], F32) for i in range(16)]
        RTt = [self.sb("RT%d" % i, [64, 1024], F32) for i in range(1)]
        ones_t = self.sb("ones", [128, 128], BF16)
        ones32_t = self.sb("ones32", [128, 128], F32)
        ident_t = self.sb("ident_sb", [128, 128], BF16)
        vecs_t = self.sb("vecs_sb", [128, 284], F32)
        small_t = self.sb("small", [128, 64], F32)
        relb_t = self.sb("relb_sb", [32, 8], F32)
        PSt = [st.enter_context(nc.psum_tensor("ps%d" % i, [128, 512], F32)) for i in range(8)]

        sch = Sched(nc, st)
        self.sch = sch
        Wb = [Buf("W0"), Buf("W1")]
        Ab = [Buf("A0"), Buf("A1")]
        Fb = [Buf("F0"), Buf("F1")]
        SBb = [Buf("SB%d" % i) for i in range(4)]
        SFb = [Buf("SF%d" % i) for i in range(3)]
        Tb_ = [Buf("T%d" % i) for i in range(6)]
        PTb = [Buf("PT%d" % i) for i in range(4)]
        STb = [Buf("ST%d" % i) for i in range(16)]
        RTb = [Buf("RT%d" % i) for i in range(1)]
        PSb = [Buf("PS%d" % i) for i in range(8)]
        cb = Buf("consts")
        smallb = Buf("small")
        W = [t[:, :] for t in Wt]
        A = [t[:, :] for t in At]
        Fa = [t[:, :] for t in Ft]
        SBa = [t[:, :] for t in SBt]
        SFa = [t[:, :] for t in SFt]
        Ta = [t[:, :] for t in Tt]
        PTa = [t[:, :] for t in PTt]
        STa = [t[:, :] for t in STt]
        RTa = [t[:, :] for t in RTt]
        PS = [t[:, :] for t in PSt]
        ones = ones_t[:, :]
        ones32 = ones32_t[:, :]
        vecs = vecs_t[:, :]
        small = small_t[:, :]

        rr = {"fqa": 0, "fqb": 0, "sb": 0, "sf": 0, "t": 0, "pt": 0, "st": 0, "w": 0, "a": 0, "f": 0, "rt": 0}

        def nxt(kind, n):
            i = rr[kind]
            rr[kind] = (i + 1) % n
            return i

        def v3(ap, off, c, n):
            return ap[:, off:off + c * n].rearrange("p (c n) -> p c n", n=n)

        def ln_g(l, s, c):
            return vecs[:, (l * 2 + s) * 16 + c:(l * 2 + s) * 16 + c + 1]

        def ln_b(l, s, c):
            return vecs[:, 128 + (l * 2 + s) * 16 + c:128 + (l * 2 + s) * 16 + c + 1]

        def qn_g(j, c):
            return vecs[:, 256 + j * 4 + c:256 + j * 4 + c + 1]

        def kvn_g(j, c):
            return vecs[:, 264 + j * 4 + c:264 + j * 4 + c + 1]

        def sub_g(j, c):
            return vecs[:, 272 + j * 2 + c:272 + j * 2 + c + 1]

        def lam_c(j, c):
            return vecs[:, 276 + j * 4 + c:276 + j * 4 + c + 1]

        sch.op("dve", lambda e: e.memset(ones, 1.0), writes=[cb])
        sch.op("dve", lambda e: e.memset(ones32, 1.0), writes=[cb])
        sch.dma("sp", vecs, vecs_in[:, :], writes=[cb])
        sch.dma("pool", ident_t[:, :], ident_in[:, :], writes=[cb])
        ident = ident_t[:, :]
        sch.dma("sp", relb_t[:, :], relb_in[:, :], writes=[cb])
        oh_ap = Ft[0][0:32, 0:RL]
        rrow_ap = Ft[1][0:8, 0:RL]
        sch.dma("sp", oh_ap, onehot_in[:, :], writes=[Fb[0]])
        for tb in range(NTB):
            sch.dma("pool", X0B[tb], x_in[tb], writes=[self.db("X0B", tb)])

        has_diff = L >= 2
        if has_diff:
            for c0 in range(0, RL, 512):
                w_ = min(512, RL - c0)
                sch.op("pe", lambda e, c0=c0, w_=w_: e.matmul(PS[0][0:8, 0:w_], relb_t[:, :], oh_ap[:, c0:c0 + w_],
                                                              start=True, stop=True),
                       reads=[cb, Fb[0]], writes=[PSb[0]])
                sch.op("dve", lambda e, c0=c0, w_=w_: e.tensor_copy(out=rrow_ap[:, c0:c0 + w_], in_=PS[0][0:8, 0:w_]),
                       reads=[PSb[0]], writes=[Fb[1]])
            sch.dma("sp", RD[:, :], rrow_ap, reads=[Fb[1]], writes=[self.db("RD")])
            for h in range(8):
                src = bass.AP(RD.tensor, h * RL, [[0, 128], [1, RL]])
                sch.dma("sp", ZT[h], src, reads=[self.db("RD")], writes=[self.db("ZT", h)])
            for h in range(8):
                for s_, row in ((0, 15), (1, 31)):
                    src = bass.AP(relb_in.tensor, row * 8 + h, [[0, 128], [1, 1]])
                    sch.dma("sp", small[:, 16 + 2 * h + s_:17 + 2 * h + s_], src, writes=[smallb])

        db = self.db
        PPd = self.dscr("PP", [NTB, 128, 16, TB], F32)
        QDd = self.dscr("QD", [NTB, 128, 16, TB], BF16)
        KDd = self.dscr("KD", [NTB, 128, 16, TB], BF16)
        VDd = self.dscr("VD", [32, 128, 2048], BF16)
        self.itc = 0
        self.rt_cache = {}

        def load_w(src_ap, kc, ncols):
            wi = nxt("w", 2)
            dst = v3(W[wi], 0, kc, ncols)
            step = max(1, 4096 // ncols)
            for c0 in range(0, kc, step):
                c1 = min(kc, c0 + step)
                sch.dma("pool", dst[:, c0:c1, :], src_ap[:, c0:c1, :], writes=[Wb[wi]])
            return wi

        def load_a(src_ap, kc, srcb, q="sp"):
            ai = nxt("a", 2)
            sch.dma(q, v3(A[ai], 0, kc, TB), src_ap, reads=[srcb], writes=[Ab[ai]])
            return ai

        class Pend:
            def __init__(self):
                self.p = None

            def push(self, fn):
                if self.p is not None:
                    self.p()
                self.p = fn

            def flush(self):
                self.push(None)

        def gemm(wslabs, kc_total, ncols, groups, a_src, a_name, kgroups, epi, aq="sp", pre=None, ksplit=False):
            pend = Pend()
            for sl, wsrc in enumerate(wslabs):
                if sl == 0 and pre is not None:
                    spare = 1 - rr["w"]
                if ksplit:
                    half = kc_total // 2
                    wvs = []
                    for hi_ in range(2):
                        dstv = v3(W[hi_], 0, half, ncols)
                        step = max(1, 4096 // ncols)
                        for c0 in range(0, half, step):
                            c1 = min(half, c0 + step)
                            sch.dma("pool", dstv[:, c0:c1, :], wsrc[:, hi_ * half + c0:hi_ * half + c1, :], writes=[Wb[hi_]])
                        wvs.append(dstv)
                else:
                    wi = load_w(wsrc, kc_total, ncols)
                    wv = v3(W[wi], 0, kc_total, ncols)
                for tb in range(NTB):
                    if sl == 0 and pre is not None and tb == 0:
                        pend.flush()
                        pre(0, spare)
                    single = len(kgroups) == 1
                    if single:
                        ai0 = load_a(a_src(tb), kc_total, db(a_name, tb), q=aq)
                    for gi, tiles in enumerate(groups):
                        base = 4 * (self.itc % 2)
                        self.itc += 1
                        banks = [base + i for i in range(len(tiles))]
                        for kgi, kcs in enumerate(kgroups):
                            if single:
                                ai = ai0
                                av = v3(A[ai], 0, kc_total, TB)
                            else:
                                ai = load_a(a_src(tb)[:, kcs[0]:kcs[-1] + 1, :], len(kcs), db(a_name, tb), q=aq)
                                av = v3(A[ai], 0, len(kcs), TB)
                            for t_i, (co, wd) in enumerate(tiles):
                                for i, kc in enumerate(kcs):
                                    ia = kc if single else i
                                    if ksplit:
                                        wi = kc // half
                                        wv = wvs[wi]
                                        kcw = kc % half
                                    else:
                                        kcw = kc
                                    sch.op("pe", lambda e, b=banks[t_i], co=co, wd=wd, kc=kcw, ia=ia, wv=wv, av=av,
                                           st_=(kgi == 0 and i == 0),
                                           sp_=(kgi == len(kgroups) - 1 and i == len(kcs) - 1):
                                           e.matmul(PS[b][0:wd, :], wv[:, kc, co:co + wd], av[:, ia, :], start=st_, stop=sp_),
                                           reads=[Wb[wi], Ab[ai]], writes=[PSb[banks[t_i]]])
                        pend.push(lambda sl=sl, tb=tb, gi=gi, banks=banks: epi(sl, tb, gi, banks))
                    if sl == 0 and pre is not None and tb + 1 < NTB:
                        pend.flush()
                        pre(tb + 1, spare)
            pend.flush()

        def copy_bank(eng, out_ap, outb, bank, rows=128):
            if eng == "act":
                sch.op("act", lambda e: e.activation(out=out_ap, in_=PS[bank][0:rows, :], func=AF.Copy),
                       reads=[PSb[bank]], writes=[outb])
            else:
                sch.op("dve", lambda e: e.tensor_copy(out=out_ap, in_=PS[bank][0:rows, :]),
                       reads=[PSb[bank]], writes=[outb])

        def epi_store_bf16(dst_fn):
            def epi(sl, tb, gi, banks):
                si = nxt("sb", 4)
                sv = v3(SBa[si], 0, len(banks), TB)
                for t_i, b in enumerate(banks):
                    copy_bank("act" if t_i % 2 == 0 else "dve", sv[:, t_i, :], SBb[si], b)
                dap, dname = dst_fn(sl, tb, gi)
                sch.dma("sp", dap, sv, reads=[SBb[si]], writes=[db(dname, tb)])
            return epi

        def rms_epi(cf_ap, cfb, nt, n, gfn, extra, stat_bank, out_ap, outb):
            for j in range(nt):
                pi = nxt("pt", 4)
                sqv = PTa[pi][:, 0:n]
                sch.op("pool", lambda e, j=j, sqv=sqv: e.tensor_tensor(out=sqv, in0=cf_ap[:, j, :], in1=cf_ap[:, j, :],
                                                                      op=ALU.mult), reads=[cfb], writes=[PTb[pi]])
                sch.op("pe", lambda e, j=j, sqv=sqv: e.matmul(PS[stat_bank][:, 0:n], ones, sqv, start=(j == 0),
                                                             stop=(j == nt - 1)),
                       reads=[PTb[pi], cb], writes=[PSb[stat_bank]])
            ti = nxt("t", 6)
            rs = Ta[ti][:, 0:n]
            sch.op("dve", lambda e: e.tensor_scalar(out=rs, in0=PS[stat_bank][:, 0:n], scalar1=1.0 / (nt * 128),
                                                    scalar2=RMS_EPS, op0=ALU.mult, op1=ALU.add),
                   reads=[PSb[stat_bank]], writes=[Tb_[ti]])
            sch.op("act", lambda e: e.activation(out=rs, in_=rs, func=AF.Sqrt), reads=[Tb_[ti]], writes=[Tb_[ti]])
            sch.op("dve", lambda e: e.reciprocal(out=rs, in_=rs), reads=[Tb_[ti]], writes=[Tb_[ti]])
            if extra != 1.0:
                sch.op("dve", lambda e: e.tensor_scalar(out=rs, in0=rs, scalar1=float(extra), scalar2=None, op0=ALU.mult),
                       reads=[Tb_[ti]], writes=[Tb_[ti]])
            for j in range(nt):
                sch.op("dve", lambda e, j=j: e.scalar_tensor_tensor(out=out_ap[:, j, :], in0=cf_ap[:, j, :], scalar=gfn(j),
                                                                     in1=rs, op0=ALU.mult, op1=ALU.mult),
                       reads=[cfb, Tb_[ti], cb], writes=[outb])

        def rope_epi(bank_x, bank_sw, tb, out_ap, outb):
            if self.rt_cache.get("tb") != tb:
                ri = nxt("rt", 1)
                sch.dma("sp", RTa[ri].rearrange("p (c n) -> p c n", n=TB), rope_in[tb], writes=[RTb[ri]])
                self.rt_cache = {"tb": tb, "ri": ri}
            ri = self.rt_cache["ri"]
            rt = RTa[ri].rearrange("p (c n) -> p c n", n=TB)
            t1 = nxt("t", 6)
            t2 = nxt("t", 6)
            sch.op("dve", lambda e: e.tensor_tensor(out=Ta[t1][0:64, :], in0=PS[bank_x][0:64, :], in1=rt[:, 0, :], op=ALU.mult),
                   reads=[PSb[bank_x], RTb[ri]], writes=[Tb_[t1]])
            sch.op("dve", lambda e: e.tensor_tensor(out=Ta[t2][0:64, :], in0=PS[bank_sw][0:64, :], in1=rt[:, 1, :], op=ALU.mult),
                   reads=[PSb[bank_sw], RTb[ri]], writes=[Tb_[t2]])
            sch.op("pool", lambda e: e.tensor_tensor(out=out_ap, in0=Ta[t1][0:64, :], in1=Ta[t2][0:64, :], op=ALU.add),
                   reads=[Tb_[t1], Tb_[t2]], writes=[outb])

        lnb = {}

        fq = [Buf("fq%d" % i, Fb[i // 2]) for i in range(4)]
        HBQ = 128
        NQ = TB // HBQ
        lnstat = {}

        def fq_ap(qi):
            return Fa[qi // 2][:, (qi % 2) * 2048:(qi % 2) * 2048 + 2048]

        NG = NTB * NQ

        def ln_pre(l, s, XFd, XFn, XBd, XBn):
            def zview(g):
                return fq_ap(g % 4).rearrange("p (c n) -> p c n", n=HBQ)

            def L(g):
                tb, hf = g // NQ, g % NQ
                sch.dma("sp", zview(g), Z[tb][:, :, hf * HBQ:(hf + 1) * HBQ], reads=[db("Z", tb)], writes=[fq[g % 4]])

            def A1(g, spare):
                wq = g % 4
                key = (spare, wq)
                if key not in lnb:
                    lnb[key] = (Buf("lnzb%d%d" % key, Wb[spare]), Buf("lnzq%d%d" % key, Wb[spare]))
                zbb, zqb = lnb[key]
                off = wq * 4096
                fa = fq_ap(wq)
                zb = W[spare][:, off:off + 2048]
                zq = W[spare][:, off + 2048:off + 4096]
                zbv = v3(W[spare], off, 16, HBQ)
                zqv = v3(W[spare], off + 2048, 16, HBQ)
                sch.op("dve", lambda e: e.tensor_copy(out=zb, in_=fa), reads=[fq[wq]], writes=[zbb])
                sch.op("act", lambda e: e.activation(out=zq, in_=fa, func=AF.Square), reads=[fq[wq]], writes=[zqb])
                b0 = 2 * wq
                for c in range(16):
                    sch.op("pe", lambda e, c=c: e.matmul(PS[b0][:, 0:HBQ], ones, zbv[:, c, :], start=(c == 0), stop=(c == 15)),
                           reads=[zbb, cb], writes=[PSb[b0]])
                for c in range(16):
                    sch.op("pe", lambda e, c=c: e.matmul(PS[b0 + 1][:, 0:HBQ], ones, zqv[:, c, :], start=(c == 0), stop=(c == 15)),
                           reads=[zqb, cb], writes=[PSb[b0 + 1]])

            def A2(g):
                b0 = 2 * (g % 4)
                si = nxt("st", 16)
                sj = nxt("st", 16)
                lnstat[g] = (si, sj)
                mean = STa[si][:, 0:HBQ]
                rstd = STa[sj][:, 0:HBQ]
                sch.op("dve", lambda e: e.tensor_scalar(out=mean, in0=PS[b0][:, 0:HBQ], scalar1=1.0 / D, scalar2=None,
                                                        op0=ALU.mult), reads=[PSb[b0]], writes=[STb[si]])
                sch.op("dve", lambda e: e.tensor_tensor(out=rstd, in0=mean, in1=mean, op=ALU.mult),
                       reads=[STb[si]], writes=[STb[sj]])
                sch.op("dve", lambda e: e.scalar_tensor_tensor(out=rstd, in0=PS[b0 + 1][:, 0:HBQ], scalar=1.0 / D, in1=rstd,
                                                               op0=ALU.mult, op1=ALU.subtract),
                       reads=[PSb[b0 + 1], STb[sj]], writes=[STb[sj]])
                sch.op("dve", lambda e: e.tensor_scalar(out=rstd, in0=rstd, scalar1=LN_EPS, scalar2=None, op0=ALU.add),
                       reads=[STb[sj]], writes=[STb[sj]])

            def A2b(g):
                si, sj = lnstat[g]
                rstd = STa[sj][:, 0:HBQ]
                sch.op("act", lambda e: e.activation(out=rstd, in_=rstd, func=AF.Sqrt), reads=[STb[sj]], writes=[STb[sj]])
                sch.op("dve", lambda e: e.reciprocal(out=rstd, in_=rstd), reads=[STb[sj]], writes=[STb[sj]])

            def B1(g):
                qi = g % 4
                fbuf = fq[qi]
                z = zview(g)
                si, sj = lnstat[g]
                mean_b = STa[si][:, 0:HBQ].unsqueeze(1).to_broadcast([128, 16, HBQ])
                rstd_b = STa[sj][:, 0:HBQ].unsqueeze(1).to_broadcast([128, 16, HBQ])
                sch.op("dve", lambda e: e.tensor_tensor(out=z, in0=z, in1=mean_b, op=ALU.subtract),
                       reads=[fbuf, STb[si]], writes=[fbuf])
                sch.op("dve", lambda e: e.tensor_tensor(out=z, in0=z, in1=rstd_b, op=ALU.mult),
                       reads=[fbuf, STb[sj]], writes=[fbuf])

            def B(g):
                tb, hf = g // NQ, g % NQ
                qi = g % 4
                fbuf = fq[qi]
                fa = fq_ap(qi)
                z = zview(g)
                for c in range(16):
                    sch.op("act", lambda e, c=c: e.activation(out=z[:, c, :], in_=z[:, c, :], func=AF.Identity,
                                                              bias=ln_b(l, s, c), scale=ln_g(l, s, c)),
                           reads=[fbuf, cb], writes=[fbuf])
                oi = nxt("sb", 4)
                sch.op("act", lambda e: e.activation(out=SBa[oi], in_=fa, func=AF.Copy), reads=[fbuf], writes=[SBb[oi]])
                sch.dma("sp", XFd[tb][:, :, hf * HBQ:(hf + 1) * HBQ], z, reads=[fbuf], writes=[db(XFn, tb)])
                sch.dma("sp", XBd[tb][:, :, hf * HBQ:(hf + 1) * HBQ], v3(SBa[oi], 0, 16, HBQ), reads=[SBb[oi]],
                        writes=[db(XBn, tb)])

            a1_done = set()

            def steps(g_lo, g_hi, spare):
                for g in range(g_lo, g_hi):
                    if 0 <= g + 3 < NG:
                        L(g + 3)
                    if 0 <= g < NG:
                        B1(g)
                    if 0 <= g + 1 < NG and (g + 1) not in a1_done:
                        A1(g + 1, spare)
                        a1_done.add(g + 1)
                    if 0 <= g + 2 < NG and g != g_hi - 1:
                        A1(g + 2, spare)
                        a1_done.add(g + 2)
                    if 0 <= g + 1 < NG:
                        A2(g + 1)
                    if 0 <= g < NG:
                        B(g)
                    if 0 <= g + 1 < NG:
                        A2b(g + 1)

            def hook(tb, spare):
                if tb == 0:
                    steps(-3, 2 * NQ, spare)
                elif tb + 1 < NTB:
                    steps((tb + 1) * NQ, (tb + 2) * NQ, spare)
            return hook

        def epi_resid(XFd, XFn, tiles_per_slab, group_tiles):
            def epi(sl, tb, gi, banks):
                c0 = sl * tiles_per_slab + gi * group_tiles
                nt = len(banks)
                xi = nxt("sf", 3)
                xf = v3(SFa[xi], 0, nt, TB)
                sch.dma("sp", xf, XFd[tb][:, c0:c0 + nt, :], reads=[db(XFn, tb)], writes=[SFb[xi]])
                for t_i in range(nt):
                    sch.op("dve", lambda e, t_i=t_i, xf=xf, b=banks[t_i]: e.scalar_tensor_tensor(
                        out=xf[:, t_i, :], in0=xf[:, t_i, :], scalar=float(ALPHA), in1=PS[b][:, :],
                        op0=ALU.mult, op1=ALU.add), reads=[SFb[xi], PSb[banks[t_i]]], writes=[SFb[xi]])
                sch.dma("sp", Z[tb][:, c0:c0 + nt, :], xf, reads=[SFb[xi]], writes=[db("Z", tb)])
            return epi

        def g4(n):
            return [[(c * 128, 128) for c in range(g * 4, g * 4 + 4)] for g in range(n // 512)]

        K16 = [list(range(16))]
        K4 = [list(range(4))]

        def gemm_v(wslabs, kc_total, ncols, a_src, a_name, Vd, vname, vcol0_fn):
            pend = Pend()
            for sl, wsrc in enumerate(wslabs):
                wi = load_w(wsrc, kc_total, ncols)
                wv = v3(W[wi], 0, kc_total, ncols)
                for tb in range(NTB):
                    ai = load_a(a_src(tb), kc_total, db(a_name, tb))
                    av = v3(A[ai], 0, kc_total, TB)
                    ncg = ncols // 512
                    for tt in range(4):
                        bl = []
                        for cg in range(ncg):
                            b = self.itc % 8
                            self.itc += 1
                            bl.append(b)
                            for kc in range(kc_total):
                                sch.op("pe", lambda e, b=b, kc=kc, tt=tt, cg=cg, wv=wv, av=av: e.matmul(
                                    PS[b][:, :], av[:, kc, tt * 128:(tt + 1) * 128], wv[:, kc, cg * 512:(cg + 1) * 512],
                                    start=(kc == 0), stop=(kc == kc_total - 1)),
                                    reads=[Wb[wi], Ab[ai]], writes=[PSb[b]])

                        def epi(sl=sl, tb=tb, tt=tt, bl=bl, ncg=ncg):
                            si = nxt("sb", 4)
                            for cg, b in enumerate(bl):
                                copy_bank("act" if cg % 2 == 0 else "dve", SBa[si][:, cg * 512:(cg + 1) * 512], SBb[si], b)
                            col = vcol0_fn(sl)
                            sch.dma("sp", Vd[tb * 4 + tt][:, col:col + ncg * 512], SBa[si][:, 0:ncg * 512], reads=[SBb[si]],
                                    writes=[db(vname, tb)])
                        pend.push(epi)
            pend.flush()

        def attn_mla():
            scale = float(192 ** -0.5)
            KNv = KN.rearrange("t p c n -> p t c n")
            QNv = QN.rearrange("t p c n -> p t c n")
            QRv = QR.rearrange("t p c n -> p t c n")
            Vv = V.rearrange("t p f -> p t f")
            krb = Buf("krb", Ab[0])
            krv = A[0][0:64, 0:4096]
            sch.dma("sp", krv.rearrange("p (t n) -> p t n", n=TB), KR.rearrange("t p n -> p t n"),
                    reads=[db("KR", t) for t in range(NTB)], writes=[krb])
            knb = [Buf("knb%d" % i, Wb[0]) for i in range(2)]
            vb = [Buf("vb%d" % i, Wb[0]) for i in range(2)]
            qnb = [Buf("qnb%d" % i, Wb[1]) for i in range(2)]
            qrb = [Buf("qrb%d" % i, Wb[1]) for i in range(2)]
            allk = [db("KN", t) for t in range(NTB)]
            allv = [db("V", t) for t in range(NTB)]
            allq = [db("QN", t) for t in range(NTB)]
            allqr = [db("QR", t) for t in range(NTB)]
            cnt = 0
            sidx = 0
            accb = [[Buf("accD%d" % i, Fb[i]), Buf("accP%d" % i, Fb[i])] for i in range(2)]
            acca = [[Fa[i][:, 0:512], Fa[i][:, 512:1024]] for i in range(2)]

            def views(h):
                i = h % 2
                return (W[0][:, i * 4096:(i + 1) * 4096], v3(W[0], 8192 + i * 4096, 32, 128),
                        W[1][:, i * 4096:(i + 1) * 4096], W[1][0:64, 8192 + i * 4096:8192 + (i + 1) * 4096])

            def loads(h):
                i = h % 2
                knv, vv, qnv, qrv = views(h)
                sch.dma("sp", knv.rearrange("p (t n) -> p t n", n=TB), KNv[:, :, h, :], reads=allk, writes=[knb[i]])
                sch.dma("sp", qnv.rearrange("p (t n) -> p t n", n=TB), QNv[:, :, h, :], reads=allq, writes=[qnb[i]])
                sch.dma("sp", qrv.rearrange("p (t n) -> p t n", n=TB), QRv[:, :, h, :], reads=allqr, writes=[qrb[i]])
                sch.dma("sp", vv, Vv[:, :, h * 128:(h + 1) * 128], reads=allv, writes=[vb[i]])
            loads(0)
            for h in range(16):
                i = h % 2
                knv, vv, qnv, qrv = views(h)
                if h + 1 < 16:
                    loads(h + 1)
                for qb in range(NTB):
                    ob = 3 + (cnt % 2)
                    smb = 5 + (cnt % 2)
                    cnt += 1
                    qs = slice(qb * TB, (qb + 1) * TB)
                    sbank = {}

                    def emit_s(kt):
                        nonlocal sidx
                        b = sidx % 3
                        sidx += 1
                        sbank[kt] = b
                        ks = slice(kt * 128, (kt + 1) * 128)
                        sch.op("pe", lambda e, b=b, ks=ks, knv=knv, qnv=qnv, qs=qs: e.matmul(PS[b][:, :], knv[:, ks], qnv[:, qs], start=True, stop=False),
                               reads=[knb[i], qnb[i]], writes=[PSb[b]])
                        sch.op("pe", lambda e, b=b, ks=ks, qrv=qrv, qs=qs: e.matmul(PS[b][:, :], krv[:, ks], qrv[:, qs], start=False, stop=True),
                               reads=[krb, qrb[i]], writes=[PSb[b]])
                    emit_s(0)
                    emit_s(1)
                    for kt in range(32):
                        if kt + 2 < 32:
                            emit_s(kt + 2)
                        b = sbank[kt]
                        pi = nxt("pt", 4)
                        sch.op("act", lambda e, b=b, pi=pi: e.activation(out=PTa[pi], in_=PS[b][:, :], func=AF.Exp, scale=scale),
                               reads=[PSb[b]], writes=[PTb[pi]])
                        sch.op("pe", lambda e, pi=pi, kt=kt, ob=ob, vv=vv: e.matmul(PS[ob][:, :], vv[:, kt, :], PTa[pi],
                                                                            start=(kt == 0), stop=(kt == 31)),
                               reads=[vb[i], PTb[pi]], writes=[PSb[ob]])
                        par = cnt % 2
                        which = kt % 2
                        eng = "dve"
                        acc_ap = acca[par][which]
                        if kt < 2:
                            sch.op(eng, lambda e, pi=pi, acc_ap=acc_ap: e.tensor_copy(out=acc_ap, in_=PTa[pi]),
                                   reads=[PTb[pi]], writes=[accb[par][which]])
                        else:
                            sch.op(eng, lambda e, pi=pi, acc_ap=acc_ap: e.tensor_tensor(out=acc_ap, in0=acc_ap, in1=PTa[pi],
                                                                                       op=ALU.add),
                                   reads=[PTb[pi], accb[par][which]], writes=[accb[par][which]])
                    for which in range(2):
                        sch.op("pe", lambda e, which=which, smb=smb, a_=acca[cnt % 2][which]: e.matmul(
                            PS[smb][:, :], ones32, a_, start=(which == 0), stop=(which == 1)),
                            reads=[cb, accb[cnt % 2][which]], writes=[PSb[smb]])
                    ti = nxt("t", 6)
                    si = nxt("sb", 4)
                    sch.op("dve", lambda e, ti=ti, smb=smb: e.reciprocal(out=Ta[ti], in_=PS[smb][:, :]),
                           reads=[PSb[smb]], writes=[Tb_[ti]])
                    sch.op("dve", lambda e, ti=ti, si=si, ob=ob: e.tensor_tensor(out=SBa[si][:, 0:512], in0=PS[ob][:, :],
                                                                                in1=Ta[ti], op=ALU.mult),
                           reads=[PSb[ob], Tb_[ti]], writes=[SBb[si]])
                    sch.dma("sp", OB[qb][:, h, :], SBa[si][:, 0:512], reads=[SBb[si]], writes=[db("OB", qb)])

        def attn_diff(j, layer_idx):
            scale = float(128 ** -0.5)
            lambda_init = 0.8 - 0.6 * math.exp(-0.3 * layer_idx)
            sch.op("dve", lambda e: e.tensor_tensor(out=small[:, 0:1], in0=lam_c(j, 0), in1=lam_c(j, 1), op=ALU.mult),
                   reads=[cb], writes=[smallb])
            sch.op("dve", lambda e: e.tensor_tensor(out=small[:, 1:2], in0=lam_c(j, 2), in1=lam_c(j, 3), op=ALU.mult),
                   reads=[cb, smallb], writes=[smallb])
            sch.op("pe", lambda e: e.matmul(PS[7][:, 0:2], ones32, small[:, 0:2], start=True, stop=True),
                   reads=[cb, smallb], writes=[PSb[7]])
            sch.op("act", lambda e: e.activation(out=small[:, 2:4], in_=PS[7][:, 0:2], func=AF.Exp),
                   reads=[PSb[7], smallb], writes=[smallb])
            sch.op("dve", lambda e: e.tensor_tensor(out=small[:, 4:5], in0=small[:, 3:4], in1=small[:, 2:3], op=ALU.subtract),
                   reads=[smallb], writes=[smallb])
            sch.op("dve", lambda e: e.tensor_scalar(out=small[:, 4:5], in0=small[:, 4:5], scalar1=-float(lambda_init),
                                                    scalar2=None, op0=ALU.add), reads=[smallb], writes=[smallb])
            QDv = QDd.rearrange("t p c n -> p t c n")
            KDv = KDd.rearrange("t p c n -> p t c n")
            VDv = VDd.rearrange("t p f -> p t f")
            allq = [db("QD", t) for t in range(NTB)]
            allk = [db("KD", t) for t in range(NTB)]
            allv = [db("VD", t) for t in range(NTB)]
            kqb = [Buf("kq%d" % i, Wb[0]) for i in range(4)]
            vbb = [Buf("vd%d" % i, Wb[1]) for i in range(2)]
            sqb = [Buf("sq%d" % i, Ab[i]) for i in range(2)]
            daccb = [[Buf("dacc%d%d" % (i, k), Fb[i]) for k in range(2)] for i in range(2)]
            dacca = [[Fa[i][:, 3072 + 512 * k:3584 + 512 * k] for k in range(2)] for i in range(2)]
            deferred = []
            sidx = 0
            def vloads(h):
                hi = h % 2
                vv = v3(W[1], hi * 8192, 32, 256)
                sch.dma("sp", vv, VDv[:, :, h * 256:(h + 1) * 256], reads=allv, writes=[vbb[hi]])
                bt = v3(Fa[hi], 0, 6, TB)
                for idx in range(6):
                    delta = idx * 128 - 128
                    c0 = 639 - delta
                    src = bass.AP(ZT.tensor, h * 128 * RL + c0, [[RL - 1, 128], [1, TB]])
                    sch.dma("sp", bt[:, idx, :], src, reads=[db("ZT", h)], writes=[Fb[hi]])
                btf = Fa[hi][:, 0:3072]
                bhi = A[hi][:, 0:3072]
                blo = A[hi][:, 3072:6144]
                sch.op("dve", lambda e, btf=btf: e.tensor_scalar(out=btf, in0=btf, scalar1=float(1.0 / scale), scalar2=None,
                                                                 op0=ALU.mult), reads=[Fb[hi]], writes=[Fb[hi]])
                sch.op("dve", lambda e, btf=btf, bhi=bhi: e.tensor_copy(out=bhi, in_=btf), reads=[Fb[hi]], writes=[Ab[hi]])
                sch.op("dve", lambda e, btf=btf, bhi=bhi: e.tensor_tensor(out=btf, in0=btf, in1=bhi, op=ALU.subtract),
                       reads=[Fb[hi], Ab[hi]], writes=[Fb[hi]])
                sch.op("dve", lambda e, btf=btf, blo=blo: e.tensor_copy(out=blo, in_=btf), reads=[Fb[hi]], writes=[Ab[hi]])
            vloads(0)
            for h in range(8):
                hi = h % 2
                fi = hi
                vv = v3(W[1], hi * 8192, 32, 256)
                bt = v3(Fa[hi], 0, 6, TB)
                kv_ = []
                qv_ = []
                for gi in range(2):
                    g = 2 * h + gi
                    kvw = W[0][:, gi * 4096:(gi + 1) * 4096]
                    qvw = W[0][:, 8192 + gi * 4096:8192 + (gi + 1) * 4096]
                    sch.dma("sp", kvw.rearrange("p (t n) -> p t n", n=TB), KDv[:, :, g, :], reads=allk, writes=[kqb[gi]])
                    sch.dma("sp", qvw.rearrange("p (t n) -> p t n", n=TB), QDv[:, :, g, :], reads=allq, writes=[kqb[2 + gi]])
                    kv_.append(kvw)
                    qv_.append(qvw)
                if h + 1 < 8:
                    vloads(h + 1)
                for qb in range(NTB):
                    qs = slice(qb * TB, (qb + 1) * TB)
                    for gi in range(2):
                        obs = (2 + 2 * gi, 3 + 2 * gi)
                        smb = 6 + gi
                        sbank = {}

                        def emit_s(kt, gi=gi):
                            nonlocal sidx
                            b = sidx % 2
                            sidx += 1
                            sbank[kt] = b
                            ks = slice(kt * 128, (kt + 1) * 128)
                            delta_ = kt * 128 - qb * TB
                            near_ = -128 <= delta_ <= 512
                            sch.op("pe", lambda e, b=b, ks=ks, kk=kv_[gi], qq=qv_[gi], qs=qs, near_=near_: e.matmul(
                                PS[b][:, :], kk[:, ks], qq[:, qs], start=True, stop=(not near_)),
                                reads=[kqb[gi], kqb[2 + gi]], writes=[PSb[b]])
                            if near_:
                                ix = (delta_ + 128) // 128
                                for part in range(2):
                                    bsrc = A[hi][:, part * 3072 + ix * TB:part * 3072 + (ix + 1) * TB]
                                    sch.op("pe", lambda e, b=b, bsrc=bsrc, part=part: e.matmul(
                                        PS[b][:, :], ident, bsrc, start=False, stop=(part == 1)),
                                        reads=[cb, Ab[hi]], writes=[PSb[b]])
                        emit_s(0)
                        emit_s(1)
                        for kt in range(32):
                            b = sbank[kt]
                            pi = nxt("pt", 4)
                            delta = kt * 128 - qb * TB
                            if -128 <= delta <= 512:
                                sch.op("act", lambda e, b=b, pi=pi: e.activation(out=PTa[pi], in_=PS[b][:, :], func=AF.Exp,
                                                                                 scale=scale),
                                       reads=[PSb[b]], writes=[PTb[pi]])
                            else:
                                col = 16 + 2 * h + (0 if delta < 0 else 1)
                                sch.op("act", lambda e, b=b, pi=pi, col=col: e.activation(
                                    out=PTa[pi], in_=PS[b][:, :], func=AF.Exp, scale=scale, bias=small[:, col:col + 1]),
                                    reads=[PSb[b], smallb], writes=[PTb[pi]])
                            for dv in range(2):
                                sch.op("pe", lambda e, pi=pi, kt=kt, dv=dv, ob=obs[dv], vv=vv: e.matmul(
                                    PS[ob][:, :], vv[:, kt, dv * 128:(dv + 1) * 128], PTa[pi], start=(kt == 0), stop=(kt == 31)),
                                    reads=[vbb[hi], PTb[pi]], writes=[PSb[obs[dv]]])
                                if dv == 0 and kt + 2 < 32:
                                    emit_s(kt + 2)
                            wa = kt % 2
                            if kt < 2:
                                sch.op("dve", lambda e, pi=pi, a_=dacca[gi][wa]: e.tensor_copy(out=a_, in_=PTa[pi]),
                                       reads=[PTb[pi]], writes=[daccb[gi][wa]])
                            else:
                                sch.op("dve", lambda e, pi=pi, a_=dacca[gi][wa]: e.tensor_tensor(out=a_, in0=a_, in1=PTa[pi],
                                                                                             op=ALU.add),
                                       reads=[PTb[pi], daccb[gi][wa]], writes=[daccb[gi][wa]])
                            if gi == 0 and kt == 6 and deferred:
                                deferred.pop(0)()
                        for wa in range(2):
                            sch.op("pe", lambda e, wa=wa, smb=smb, a_=dacca[gi][wa]: e.matmul(
                                PS[smb][:, :], ones32, a_, start=(wa == 0), stop=(wa == 1)),
                                reads=[cb, daccb[gi][wa]], writes=[PSb[smb]])
                    r0 = nxt("t", 6)
                    r1 = nxt("t", 6)
                    sch.op("dve", lambda e, r0=r0: e.reciprocal(out=Ta[r0], in_=PS[6][:, :]), reads=[PSb[6]], writes=[Tb_[r0]])
                    sch.op("dve", lambda e, r1=r1: e.reciprocal(out=Ta[r1], in_=PS[7][:, :]), reads=[PSb[7]], writes=[Tb_[r1]])
                    sch.op("dve", lambda e, r1=r1: e.tensor_scalar(out=Ta[r1], in0=Ta[r1], scalar1=small[:, 4:5], scalar2=None,
                                                                   op0=ALU.mult), reads=[Tb_[r1], smallb], writes=[Tb_[r1]])
                    ci = nxt("sf", 3)
                    cf = v3(SFa[ci], 0, 2, TB)
                    for dv in range(2):
                        t0 = nxt("t", 6)
                        sch.op("dve", lambda e, t0=t0, dv=dv, r0=r0: e.tensor_tensor(out=Ta[t0], in0=PS[2 + dv][:, :], in1=Ta[r0],
                                                                                     op=ALU.mult),
                               reads=[PSb[2 + dv], Tb_[r0]], writes=[Tb_[t0]])
                        sch.op("dve", lambda e, dv=dv, r1=r1, cf=cf: e.tensor_tensor(out=cf[:, dv, :], in0=PS[4 + dv][:, :],
                                                                                     in1=Ta[r1], op=ALU.mult),
                               reads=[PSb[4 + dv], Tb_[r1]], writes=[SFb[ci]])
                        sch.op("pool", lambda e, t0=t0, dv=dv, cf=cf: e.tensor_tensor(out=cf[:, dv, :], in0=cf[:, dv, :],
                                                                                      in1=Ta[t0], op=ALU.add),
                               reads=[SFb[ci], Tb_[t0]], writes=[SFb[ci]])
                    sqv = v3(A[hi], 6144, 2, TB)
                    for dv in range(2):
                        sch.op("pool", lambda e, dv=dv, cf=cf, sqv=sqv: e.tensor_tensor(out=sqv[:, dv, :], in0=cf[:, dv, :],
                                                                                        in1=cf[:, dv, :], op=ALU.mult),
                               reads=[SFb[ci]], writes=[sqb[hi]])

                    def part2(cf=cf, ci=ci, sqv=sqv, hi=hi, qb=qb, h=h):
                        for dv in range(2):
                            sch.op("pe", lambda e, dv=dv, sqv=sqv: e.matmul(PS[7][:, :], ones, sqv[:, dv, :], start=(dv == 0),
                                                                           stop=(dv == 1)),
                                   reads=[sqb[hi], cb], writes=[PSb[7]])
                        ti = nxt("t", 6)
                        rs = Ta[ti]
                        sch.op("dve", lambda e, rs=rs: e.tensor_scalar(out=rs, in0=PS[7][:, :], scalar1=1.0 / 256, scalar2=RMS_EPS,
                                                                      op0=ALU.mult, op1=ALU.add),
                               reads=[PSb[7]], writes=[Tb_[ti]])
                        sch.op("act", lambda e, rs=rs: e.activation(out=rs, in_=rs, func=AF.Sqrt), reads=[Tb_[ti]], writes=[Tb_[ti]])
                        sch.op("dve", lambda e, rs=rs: e.reciprocal(out=rs, in_=rs), reads=[Tb_[ti]], writes=[Tb_[ti]])
                        sch.op("dve", lambda e, rs=rs: e.tensor_scalar(out=rs, in0=rs, scalar1=float(1.0 - lambda_init), scalar2=None,
                                                                      op0=ALU.mult), reads=[Tb_[ti]], writes=[Tb_[ti]])
                        si = nxt("sb", 4)
                        ov = v3(SBa[si], 0, 2, TB)
                        for dv in range(2):
                            sch.op("dve", lambda e, dv=dv, cf=cf, ov=ov, rs=rs: e.scalar_tensor_tensor(
                                out=ov[:, dv, :], in0=cf[:, dv, :], scalar=sub_g(j, dv), in1=rs, op0=ALU.mult, op1=ALU.mult),
                                reads=[SFb[ci], Tb_[ti], cb], writes=[SBb[si]])
                        sch.dma("sp", OB[qb][:, 2 * h:2 * h + 2, :], ov, reads=[SBb[si]], writes=[db("OB", qb)])
                    deferred.append(part2)
            while deferred:
                deferred.pop(0)()

        XF_cur, XF_name = x_in, "x"
        for l in range(L):
            j = l // 2
            if l % 2 == 0:
                def epi_m1(sl, tb, gi, banks, j=j):
                    if gi < 2:
                        fi = nxt("sf", 3)
                        cf = v3(SFa[fi], 0, 4, TB)
                        for t_i, b in enumerate(banks):
                            copy_bank("act", cf[:, t_i, :], SFb[fi], b)
                        si = nxt("sb", 4)
                        ov = v3(SBa[si], 0, 4, TB)
                        gfn = (lambda c: qn_g(j, c)) if gi == 0 else (lambda c: kvn_g(j, c))
                        rms_epi(cf, SFb[fi], 4, TB, gfn, 1.0, banks[0], ov, SBb[si])
                        dst, nm = (CQN, "CQN") if gi == 0 else (CKVN, "CKVN")
                        sch.dma("sp", dst[tb], ov, reads=[SBb[si]], writes=[db(nm, tb)])
                    else:
                        si = nxt("sb", 4)
                        ov = SBa[si][0:64, 0:512]
                        rope_epi(banks[0], banks[1], tb, ov, SBb[si])
                        sch.dma("sp", KR[tb], ov, reads=[SBb[si]], writes=[db("KR", tb)])
                groups = [[(c * 128, 128) for c in range(0, 4)], [(c * 128, 128) for c in range(4, 8)],
                          [(1024, 64), (1088, 64)]]
                gemm([mla_win[j]], 16, 1152, groups, lambda tb: X0B[tb], "X0B", K16, epi_m1)

                m2s = {}

                def epi_m2(sl, tb, gi, banks):
                    h = gi
                    if h % 4 == 0:
                        m2s["n"] = nxt("sb", 4)
                        m2s["r"] = nxt("sb", 4)
                    si, s2 = m2s["n"], m2s["r"]
                    k = h % 4
                    copy_bank("act", SBa[si][:, k * 512:(k + 1) * 512], SBb[si], banks[0])
                    rope_epi(banks[1], banks[2], tb, SBa[s2][0:64, k * 512:(k + 1) * 512], SBb[s2])
                    if k == 3:
                        sch.dma("sp", QN[tb][:, h - 3:h + 1, :], v3(SBa[si], 0, 4, TB), reads=[SBb[si]], writes=[db("QN", tb)])
                        sch.dma("sp", QR[tb][:, h - 3:h + 1, :], v3(SBa[s2], 0, 4, TB)[0:64], reads=[SBb[s2]],
                                writes=[db("QR", tb)])
                groups = [[(h * 256, 128), (h * 256 + 128, 64), (h * 256 + 192, 64)] for h in range(16)]
                gemm([mla_wuq[j]], 4, 4096, groups, lambda tb: CQN[tb], "CQN", K4, epi_m2)
                gemm([mla_wuk[j]], 4, 2048, g4(2048), lambda tb: CKVN[tb], "CKVN", K4,
                     epi_store_bf16(lambda sl, tb, gi: (KN[tb][:, gi * 4:gi * 4 + 4, :], "KN")))
                gemm_v([mla_wuv[j]], 4, 2048, lambda tb: CKVN[tb], "CKVN", V, "V", lambda sl: 0)
                attn_mla()
                wo = mla_wo[j]
            else:
                def dst_qk(sl, tb, gi):
                    t0 = sl * 8 + gi * 4
                    if t0 < 16:
                        return QDd[tb][:, t0:t0 + 4, :], "QD"
                    return KDd[tb][:, t0 - 16:t0 - 12, :], "KD"
                gemm([diff_wqk[j][s_] for s_ in range(4)], 16, 1024, g4(1024), lambda tb: X0B[tb], "X0B", K16,
                     epi_store_bf16(dst_qk))
                gemm_v([diff_wv[j][0], diff_wv[j][1]], 16, 1024, lambda tb: X0B[tb], "X0B", VDd, "VD", lambda sl: sl * 1024)
                attn_diff(j, l)
                wo = diff_wo[j]
            gemm([wo[0], wo[1]], 16, 1024, g4(1024), lambda tb: OB[tb], "OB", K16, epi_resid(XF_cur, XF_name, 8, 4))

            def epi_f1(sl, tb, gi, banks):
                si = nxt("sb", 4)
                hv = v3(SBa[si], 0, 2, TB)
                for t in range(2):
                    ti = nxt("t", 6)
                    sch.op("act", lambda e, ti=ti, b=banks[t]: e.activation(out=Ta[ti], in_=PS[b][:, :], func=AF.Silu),
                           reads=[PSb[banks[t]]], writes=[Tb_[ti]])
                    sch.op("dve", lambda e, ti=ti, t=t, hv=hv, b=banks[2 + t]: e.tensor_tensor(
                        out=hv[:, t, :], in0=Ta[ti], in1=PS[b][:, :], op=ALU.mult),
                        reads=[Tb_[ti], PSb[banks[2 + t]]], writes=[SBb[si]])
                c0 = sl * 4 + gi * 2
                sch.dma("sp", H[tb][:, c0:c0 + 2, :], hv, reads=[SBb[si]], writes=[db("H", tb)])
            groups = [[(gi * 512 + k * 128, 128) for k in range(4)] for gi in range(2)]
            gemm([ffn_win[l][s_] for s_ in range(11)], 16, 1024, groups, lambda tb: X1B[tb], "X1B", K16, epi_f1,
                 pre=ln_pre(l, 0, X1F, "X1F", X1B, "X1B"))
            kg44 = [list(range(g * 11, g * 11 + 11)) for g in range(4)]
            gemm([ffn_wout[l][s_] for s_ in range(4)], 44, 512, g4(512), lambda tb: H[tb], "H", kg44,
                 epi_resid(X1F, "X1F", 4, 4), ksplit=True)
            last = (l == L - 1)
            XNd, XNn = (out_d, "out") if last else (X0F, "X0F")
            pre = ln_pre(l, 1, X2F, "X2F", X2B, "X2B")
            pend = Pend()
            for sl in range(2):
                if sl == 0:
                    spare = 1 - rr["w"]
                wi = nxt("w", 2)
                gv = v3(W[wi], 0, 16, 1024)
                pv = v3(W[wi], 16384, 2, 1024)
                for c0_ in range(0, 16, 4):
                    sch.dma("pool", gv[:, c0_:c0_ + 4, :], ple_wg[l][sl][:, c0_:c0_ + 4, :], writes=[Wb[wi]])
                sch.dma("pool", pv, ple_wp[l][sl], writes=[Wb[wi]])
                for tb in range(NTB):
                    if sl == 0 and tb == 0:
                        pend.flush()
                        pre(0, spare)
                    ai = load_a(X2B[tb], 16, db("X2B", tb))
                    av = v3(A[ai], 0, 16, TB)
                    pi_ = nxt("sb", 4)
                    ptv = v3(SBa[pi_], 0, 2, TB)
                    sch.dma("pool", ptv, p_in[l][tb], writes=[SBb[pi_]])
                    for gi in range(4):
                        base = 4 * (self.itc % 2)
                        self.itc += 1
                        banks = [base, base + 1, base + 2, base + 3]
                        for t in range(2):
                            co = gi * 256 + t * 128
                            for kc in range(16):
                                sch.op("pe", lambda e, b=banks[t], co=co, kc=kc, gv=gv, av=av: e.matmul(
                                    PS[b][:, :], gv[:, kc, co:co + 128], av[:, kc, :], start=(kc == 0), stop=(kc == 15)),
                                    reads=[Wb[wi], Ab[ai]], writes=[PSb[banks[t]]])
                            for kc in range(2):
                                sch.op("pe", lambda e, b=banks[2 + t], co=co, kc=kc, pv=pv, ptv=ptv: e.matmul(
                                    PS[b][:, :], pv[:, kc, co:co + 128], ptv[:, kc, :], start=(kc == 0), stop=(kc == 1)),
                                    reads=[Wb[wi], SBb[pi_]], writes=[PSb[banks[2 + t]]])

                        def epi(sl=sl, tb=tb, gi=gi, banks=banks):
                            c0 = sl * 8 + gi * 2
                            xi = nxt("sf", 3)
                            xv = v3(SFa[xi], 0, 2, TB)
                            sch.dma("sp", xv, X2F[tb][:, c0:c0 + 2, :], reads=[db("X2F", tb)], writes=[SFb[xi]])
                            si = nxt("sb", 4)
                            bv = v3(SBa[si], 0, 2, TB)
                            tis = []
                            for t in range(2):
                                ti = nxt("t", 6)
                                tis.append(ti)
                                sch.op("act", lambda e, ti=ti, b=banks[t]: e.activation(out=Ta[ti], in_=PS[b][:, :],
                                                                                        func=AF.Sigmoid),
                                       reads=[PSb[banks[t]]], writes=[Tb_[ti]])
                                sch.op("dve", lambda e, ti=ti, b=banks[2 + t]: e.tensor_tensor(out=Ta[ti], in0=Ta[ti],
                                                                                               in1=PS[b][:, :], op=ALU.mult),
                                       reads=[Tb_[ti], PSb[banks[2 + t]]], writes=[Tb_[ti]])
                            for t in range(2):
                                sch.op("dve", lambda e, t=t, ti=tis[t], xv=xv: e.tensor_tensor(out=xv[:, t, :], in0=xv[:, t, :],
                                                                                               in1=Ta[ti], op=ALU.add),
                                       reads=[Tb_[tis[t]], SFb[xi]], writes=[SFb[xi]])
                            sch.dma("sp", XNd[tb][:, c0:c0 + 2, :], xv, reads=[SFb[xi]], writes=[db(XNn, tb)])
                            if not last:
                                sch.op("act", lambda e, xv=xv, bv=bv: e.activation(out=bv, in_=xv, func=AF.Copy),
                                       reads=[SFb[xi]], writes=[SBb[si]])
                                sch.dma("sp", X0B[tb][:, c0:c0 + 2, :], bv, reads=[SBb[si]], writes=[db("X0B", tb)])
                        pend.push(epi)
                    if sl == 0 and tb + 1 < NTB:
                        pend.flush()
                        pre(tb + 1, spare)
            pend.flush()
            XF_cur, XF_name = X0F, "X0F"

        sch.finalize()
        with nc.Block() as block:
            @block.tensor
            def _(e):
                sch.emit(e, "pe")

            @block.scalar
            def _(e):
                sch.emit(e, "act")

            @block.vector
            def _(e):
                sch.emit(e, "dve")

            @block.gpsimd
            def _(e):
                sch.emit(e, "pool")

            @block.sync
            def _(e):
                sch.emit(e, "sp")
        self.stack.close()
        return self


def _pack(w, ncols):
    k, n = w.shape
    return np.ascontiguousarray(w.reshape(k // 128, 128, n // ncols, ncols).transpose(2, 1, 0, 3))


def _fm(a, c):
    return np.ascontiguousarray(a.reshape(NTB, TB, c, 128).transpose(0, 3, 2, 1))


def _prep_shared(inp):
    f = lambda a: np.asarray(a, dtype=np.float32)
    sw = np.concatenate([np.arange(32, 64), np.arange(0, 32)])
    o = {}
    w = f(inp["mla_w_in"])
    o["mla_win"] = np.stack([_pack(np.concatenate([w[j], w[j][:, 1024:1088][:, sw]], 1), 1152)[0] for j in range(2)])
    w = f(inp["mla_w_uq"]).reshape(2, 512, 16, 192)
    wq = np.concatenate([w[..., :128], w[..., 128:192], w[..., 128:192][..., sw]], -1).reshape(2, 512, 4096)
    o["mla_wuq"] = np.stack([_pack(wq[j], 4096)[0] for j in range(2)])
    w = f(inp["mla_w_ukv"]).reshape(2, 512, 16, 256)
    o["mla_wuk"] = np.stack([_pack(np.ascontiguousarray(w[j][..., :128]).reshape(512, 2048), 2048)[0] for j in range(2)])
    o["mla_wuv"] = np.stack([_pack(np.ascontiguousarray(w[j][..., 128:]).reshape(512, 2048), 2048)[0] for j in range(2)])
    o["mla_wo"] = np.stack([_pack(f(inp["mla_w_o"])[j], 1024) for j in range(2)])
    w = f(inp["diff_w_in"])
    o["diff_wqk"] = np.stack([_pack(np.ascontiguousarray(w[j][:, :4096]), 1024) for j in range(2)])
    o["diff_wv"] = np.stack([_pack(np.ascontiguousarray(w[j][:, 4096:]), 1024) for j in range(2)])
    o["diff_wo"] = np.stack([_pack(f(inp["diff_w_o"])[j], 1024) for j in range(2)])
    w = f(inp["ffn_w_in"])
    lst = []
    for l in range(DEPTH):
        g = w[l][:, :DFF].reshape(D, 11, 2, 1, 256)
        u = w[l][:, DFF:].reshape(D, 11, 2, 1, 256)
        lst.append(_pack(np.concatenate([g, u], 3).reshape(D, 11 * 1024), 1024))
    o["ffn_win"] = np.stack(lst)
    o["ffn_wout"] = np.stack([_pack(f(inp["ffn_w_out"])[l], 512) for l in range(DEPTH)])
    o["ple_wg"] = np.stack([_pack(f(inp["ple_w_gate"])[l], 1024) for l in range(DEPTH)])
    o["ple_wp"] = np.stack([_pack(f(inp["ple_w_proj"])[l], 1024) for l in range(DEPTH)])
    vecs = np.zeros((128, 284), np.float32)
    vecs[:, 0:128] = f(inp["ln_g"]).reshape(4, 2, 16, 128).transpose(3, 0, 1, 2).reshape(128, 128)
    vecs[:, 128:256] = f(inp["ln_b"]).reshape(4, 2, 16, 128).transpose(3, 0, 1, 2).reshape(128, 128)
    vecs[:, 256:264] = f(inp["mla_q_norm"]).reshape(2, 4, 128).transpose(2, 0, 1).reshape(128, 8)
    vecs[:, 264:272] = f(inp["mla_kv_norm"]).reshape(2, 4, 128).transpose(2, 0, 1).reshape(128, 8)
    vecs[:, 272:276] = f(inp["diff_sub_norm"]).reshape(2, 2, 128).transpose(2, 0, 1).reshape(128, 4)
    vecs[:, 276:284] = f(inp["diff_lambda"]).transpose(2, 0, 1).reshape(128, 8)
    o["vecs"] = vecs
    o["relb"] = np.ascontiguousarray(f(inp["rel_bias"]))
    rope, onehot = _const_tables()
    o["rope"] = rope
    o["onehot"] = onehot
    o["ident"] = np.eye(128, dtype=np.float32)
    return o


_NC_CACHE = {}


def _get_nc(debug=None, nlayers=DEPTH, stop_after=None):
    key = (tuple(debug or ()), nlayers, stop_after)
    if key not in _NC_CACHE:
        kb = KB(debug=debug, nlayers=nlayers)
        kb.stop_after = stop_after
        kb.build()
        _NC_CACHE[key] = kb.nc
    return _NC_CACHE[key]


def kernel(**inputs):
    shared = _prep_shared(inputs)
    x = np.asarray(inputs["x"], dtype=np.float32)
    p = np.asarray(inputs["p"], dtype=np.float32)
    in_maps = []
    for b in range(NCORES):
        m = dict(shared)
        m["x"] = _fm(x[b], 16)
        m["p"] = np.stack([_fm(p[l, b], 2) for l in range(DEPTH)])
        in_maps.append(m)
    nc = _get_nc()
    res = run_bass_kernel_spmd(nc, in_maps, core_ids=list(range(NCORES)))
    out = np.empty((NCORES, S, D), np.float32)
    for b in range(NCORES):
        o = np.asarray(res.results[b]["out"]).reshape(NTB, 128, 16, TB)
        out[b] = o.transpose(0, 3, 2, 1).reshape(S, D)
    return out
```
